# Optimizing a Trainium2 kernel written in Bass

```python
import math
import jax
import jax.numpy as jnp
from jax import lax
import numpy as np

D_MODEL = 1024
BATCH = 8
SEQ = 2048
DEPTH = 4
DEC_BATCH = 128
DEC_SEQ = 1
PAST_LEN = 2048
PAGE_SIZE = 128

N_MIXERS = 2
N_NSA = (DEPTH + 1) // 2
N_SSD = DEPTH // 2
D_FF = 4 * D_MODEL
EPS = 1e-6
NSA_HEADS = 16
NSA_HEAD_DIM = D_MODEL // NSA_HEADS
NSA_KV_HEADS = 4
NSA_REP = NSA_HEADS // NSA_KV_HEADS
NSA_KV_W = 2 * NSA_KV_HEADS * NSA_HEAD_DIM
NSA_IN = NSA_HEADS * NSA_HEAD_DIM + 3 * NSA_KV_W + 3 * NSA_HEADS
CMP_BLOCK = 32
CMP_STRIDE = 16
SEL_BLOCK = 64
SEL_TOPK = 8
WINDOW = 512
Q_BLOCK = 128
SSD_D_INNER = 2 * D_MODEL
SSD_HEAD_DIM = 64
SSD_HEADS = SSD_D_INNER // SSD_HEAD_DIM
SSD_GROUPS = 4
SSD_REP = SSD_HEADS // SSD_GROUPS
SSD_STATE = 128
SSD_CONV_W = 4
SSD_CHUNK = 256
SSD_CONV_DIM = SSD_D_INNER + 2 * SSD_GROUPS * SSD_STATE
SSD_IN = SSD_D_INNER + SSD_CONV_DIM + SSD_HEADS
BIG = 1e30
NEG = -1e30

kernel_name = 'nsa_ssd_hybrid_step'


def rmsnorm(x, w):
    xf = x.astype(jnp.float32)
    r = lax.rsqrt(jnp.mean(xf * xf, axis=-1, keepdims=True) + EPS)
    return (xf * r).astype(x.dtype) * w


def adaln(c, w, b):
    m = jax.nn.silu(c) @ w + b
    return jnp.split(m[:, None, :], 6, axis=-1)


def modulate(x, w, shift, scale):
    return rmsnorm(x, w) * (1.0 + scale) + shift


def sq_relu_mlp(h, w1, w2):
    return jnp.square(jax.nn.relu(h @ w1)) @ w2


def split_cols(a, sizes):
    return jnp.split(a, [int(s) for s in np.cumsum(sizes)[:-1]], axis=-1)


def masked_probs(s, mask):
    s = jnp.where(mask, s.astype(jnp.float32), NEG)
    return jax.nn.softmax(s, axis=-1) * mask


def nsa_project(h, w_in):
    B, T, _ = h.shape
    q, kv_c, kv_s, kv_w, gates = split_cols(
        h @ w_in, [NSA_HEADS * NSA_HEAD_DIM, NSA_KV_W, NSA_KV_W, NSA_KV_W, 3 * NSA_HEADS])
    q = q.reshape(B, T, NSA_KV_HEADS, NSA_REP, NSA_HEAD_DIM)
    kv = lambda a: a.reshape(B, T, 2, NSA_KV_HEADS, NSA_HEAD_DIM)
    gates = jax.nn.sigmoid(gates.astype(jnp.float32)).reshape(B, T, NSA_KV_HEADS, NSA_REP, 3)
    return q, kv(kv_c), kv(kv_s), kv(kv_w), gates


def nsa_merge(o_cmp, o_sel, o_win, gates):
    g = gates.astype(o_cmp.dtype)
    return g[..., 0:1] * o_cmp + g[..., 1:2] * o_sel + g[..., 2:3] * o_win


def compress_rows(rows, pe, w1, w2):
    B, T, G, HD = rows.shape
    nc = (T - CMP_BLOCK) // CMP_STRIDE + 1
    idx = np.arange(nc)[:, None] * CMP_STRIDE + np.arange(CMP_BLOCK)[None, :]
    blk = rows[:, idx] + pe[:, None, :]
    blk = jnp.swapaxes(blk, 2, 3).reshape(B, nc, G, CMP_BLOCK * HD)
    return jax.nn.silu(blk @ w1) @ w2


def nsa_core(q, t, kc, vc, gather_sel, n_sel, kw, vw, pos_w):
    B, Q, G, R, HD = q.shape
    scale = HD ** -0.5
    nc = kc.shape[1]
    c_start = np.arange(nc) * CMP_STRIDE
    cmp_end = jnp.asarray(c_start + CMP_BLOCK - 1)
    m_cmp = (cmp_end[None, :] <= t[:, None])[None, :, None, None, :]
    p_cmp = masked_probs(jnp.einsum('bqgrd,bcgd->bqgrc', q, kc) * scale, m_cmp)
    o_cmp = jnp.einsum('bqgrc,bcgd->bqgrd', p_cmp.astype(vc.dtype), vc)
    nj = max(n_sel, SEL_TOPK)
    s_start = np.arange(nj) * SEL_BLOCK
    overlap = jnp.asarray((c_start[:, None] < s_start[None, :] + SEL_BLOCK)
                          & (c_start[:, None] + CMP_BLOCK > s_start[None, :]), jnp.float32)
    imp = jnp.einsum('bqgrc,cj->bqgj', p_cmp, overlap)
    j = jnp.arange(nj)
    cur = t // SEL_BLOCK
    valid = (j[None, :] < n_sel) & (j[None, :] * SEL_BLOCK <= t[:, None])
    forced = (j[None, :] == 0) | (j[None, :] == cur[:, None]) | (j[None, :] == cur[:, None] - 1)
    score = jnp.where(forced[None, :, None, :], BIG, imp)
    score = jnp.where(valid[None, :, None, :], score, NEG)
    top_s, top_idx = lax.top_k(score, SEL_TOPK)
    k_sel, v_sel = gather_sel(top_idx)
    pos_sel = top_idx[..., None] * SEL_BLOCK + jnp.arange(SEL_BLOCK)
    m_sel = (top_s > NEG / 2)[..., None] & (pos_sel <= t[None, :, None, None, None])
    m_sel = m_sel.reshape(B, Q, G, 1, SEL_TOPK * SEL_BLOCK)
    s_sel = jnp.einsum('bqgrd,bqgksd->bqgrks', q, k_sel).reshape(B, Q, G, R, SEL_TOPK * SEL_BLOCK) * scale
    p_sel = masked_probs(s_sel, m_sel).reshape(B, Q, G, R, SEL_TOPK, SEL_BLOCK)
    o_sel = jnp.einsum('bqgrks,bqgksd->bqgrd', p_sel.astype(v_sel.dtype), v_sel)
    d = t[:, None] - pos_w[None, :]
    m_win = ((pos_w[None, :] >= 0) & (d >= 0) & (d <= WINDOW))[None, :, None, None, :]
    p_win = masked_probs(jnp.einsum('bqgrd,bsgd->bqgrs', q, kw) * scale, m_win)
    o_win = jnp.einsum('bqgrs,bsgd->bqgrd', p_win.astype(vw.dtype), vw)
    return o_cmp, o_sel, o_win


def nsa_prompt(h, w_in, w_out, pe, w1, w2):
    B, T, _ = h.shape
    q, kv_c, kv_s, kv_w, gates = nsa_project(h, w_in)
    kc = compress_rows(kv_c[:, :, 0], pe[0], w1[0], w2[0])
    vc = compress_rows(kv_c[:, :, 1], pe[1], w1[1], w2[1])
    n_sel = T // SEL_BLOCK
    sel_blocks = kv_s.reshape(B, n_sel, SEL_BLOCK, 2, NSA_KV_HEADS, NSA_HEAD_DIM)
    b_ix = jnp.arange(B)[:, None, None, None]
    g_ix = jnp.arange(NSA_KV_HEADS)[None, None, :, None]

    def gather_sel(idx):
        blk = sel_blocks[b_ix, jnp.minimum(idx, n_sel - 1), :, :, g_ix, :]
        return blk[..., 0, :], blk[..., 1, :]

    win_pad = jnp.pad(kv_w, ((0, 0), (WINDOW, 0), (0, 0), (0, 0), (0, 0)))
    n_qb = T // Q_BLOCK

    def query_block(args):
        q_blk, g_blk, qb = args
        start = qb * Q_BLOCK
        t = start + jnp.arange(Q_BLOCK)
        win = lax.dynamic_slice_in_dim(win_pad, start, WINDOW + Q_BLOCK, axis=1)
        pos_w = start - WINDOW + jnp.arange(WINDOW + Q_BLOCK)
        o = nsa_core(q_blk, t, kc, vc, gather_sel, n_sel, win[:, :, 0], win[:, :, 1], pos_w)
        return nsa_merge(*o, g_blk)

    to_blocks = lambda a: jnp.swapaxes(a.reshape((B, n_qb, Q_BLOCK) + a.shape[2:]), 0, 1)
    o = lax.map(query_block, (to_blocks(q), to_blocks(gates), jnp.arange(n_qb)))
    o = jnp.swapaxes(o, 0, 1).reshape(B, T, NSA_HEADS * NSA_HEAD_DIM)
    return o @ w_out, kv_c, kv_s, kv_w[:, -min(WINDOW, T):]


def nsa_sample(h, cache_c, cache_s, cache_w, page_table, w_in, w_out, pe, w1, w2):
    B, T, _ = h.shape
    q, kv_c, kv_s, kv_w, gates = nsa_project(h, w_in)
    past = page_table.shape[1] * PAGE_SIZE
    t = past + jnp.arange(T)
    past_c = cache_c[page_table].reshape((B, past) + cache_c.shape[2:])
    rows_c = jnp.concatenate([past_c, kv_c], axis=1)
    kc = compress_rows(rows_c[:, :, 0], pe[0], w1[0], w2[0])
    vc = compress_rows(rows_c[:, :, 1], pe[1], w1[1], w2[1])
    n_sel = -(-(past + T) // SEL_BLOCK)
    n_past_blk = past // SEL_BLOCK
    n_new_blk = n_sel - n_past_blk
    bpp = PAGE_SIZE // SEL_BLOCK
    pool_blocks = cache_s.reshape((-1, SEL_BLOCK) + cache_s.shape[2:])
    new_blocks = jnp.pad(kv_s, ((0, 0), (0, n_new_blk * SEL_BLOCK - T), (0, 0), (0, 0), (0, 0)))
    new_blocks = new_blocks.reshape(B, n_new_blk, SEL_BLOCK, 2, NSA_KV_HEADS, NSA_HEAD_DIM)
    b_ix = jnp.arange(B)[:, None, None, None]
    g_ix = jnp.arange(NSA_KV_HEADS)[None, None, :, None]

    def gather_sel(idx):
        logical = jnp.minimum(idx, n_past_blk - 1)
        phys = page_table[b_ix, logical // bpp] * bpp + logical % bpp
        old = pool_blocks[phys, :, :, g_ix, :]
        new = new_blocks[b_ix, jnp.clip(idx - n_past_blk, 0, n_new_blk - 1), :, :, g_ix, :]
        blk = jnp.where((idx >= n_past_blk)[..., None, None, None], new, old)
        return blk[..., 0, :], blk[..., 1, :]

    keep = cache_w.shape[1]
    win = jnp.concatenate([cache_w, kv_w], axis=1)
    pos_w = past - keep + jnp.arange(keep + T)
    o = nsa_core(q, t, kc, vc, gather_sel, n_sel, win[:, :, 0], win[:, :, 1], pos_w)
    o = nsa_merge(*o, gates).reshape(B, T, NSA_HEADS * NSA_HEAD_DIM)
    return o @ w_out, kv_c, kv_s, win[:, -keep:]


def ssd_scan(x, dt, a, b_in, c_in, init_state):
    B, T, G, R, P = x.shape
    L = math.gcd(SSD_CHUNK, T)
    nc = T // L
    f32 = jnp.float32
    chunk = lambda v: v.reshape((B, nc, L) + v.shape[2:])
    xdt = chunk(x.astype(f32) * dt[..., None])
    bc = chunk(b_in.astype(f32))
    cc = chunk(c_in.astype(f32))
    a_cum = jnp.cumsum(chunk(dt * a), axis=2)
    causal = np.tril(np.ones((L, L), dtype=bool))[:, :, None, None]
    seg = a_cum[:, :, :, None] - a_cum[:, :, None, :]
    decay = jnp.exp(jnp.where(causal, seg, NEG))
    cb = jnp.einsum('bclgn,bcsgn->bclsg', cc, bc)
    y_diag = jnp.einsum('bclsg,bclsgr,bcsgrp->bclgrp', cb, decay, xdt)
    decay_end = jnp.exp(a_cum[:, :, -1:] - a_cum)
    chunk_states = jnp.einsum('bcsgn,bcsgr,bcsgrp->bcgrpn', bc, decay_end, xdt)
    chunk_decay = jnp.exp(a_cum[:, :, -1])

    def step(state, inp):
        cs, cd = inp
        return state * cd[..., None, None] + cs, state

    final, prev = lax.scan(step, init_state.astype(f32),
                           (jnp.swapaxes(chunk_states, 0, 1), jnp.swapaxes(chunk_decay, 0, 1)))
    prev = jnp.swapaxes(prev, 0, 1)
    y_off = jnp.einsum('bclgn,bcgrpn,bclgr->bclgrp', cc, prev, jnp.exp(a_cum))
    return (y_diag + y_off).reshape(B, T, G, R, P), final


def ssd_mixer(h, conv_buf, ssm_state, w_in, conv_w, conv_b, dt_bias, a_log, d_skip, norm_w, w_out):
    B, T, _ = h.shape
    z, xbc, dt = split_cols(h @ w_in, [SSD_D_INNER, SSD_CONV_DIM, SSD_HEADS])
    xp = jnp.concatenate([conv_buf.astype(xbc.dtype), xbc], axis=1)
    conv = conv_b + xp[:, 0:T] * conv_w[0]
    for k in range(1, SSD_CONV_W):
        conv = conv + xp[:, k:k + T] * conv_w[k]
    new_conv = xp[:, T:]
    xbc = jax.nn.silu(conv)
    x, b_in, c_in = split_cols(xbc, [SSD_D_INNER, SSD_GROUPS * SSD_STATE, SSD_GROUPS * SSD_STATE])
    x = x.reshape(B, T, SSD_GROUPS, SSD_REP, SSD_HEAD_DIM)
    b_in = b_in.reshape(B, T, SSD_GROUPS, SSD_STATE)
    c_in = c_in.reshape(B, T, SSD_GROUPS, SSD_STATE)
    dt = jax.nn.softplus(dt.astype(jnp.float32) + dt_bias).reshape(B, T, SSD_GROUPS, SSD_REP)
    a = -jnp.exp(a_log.astype(jnp.float32)).reshape(SSD_GROUPS, SSD_REP)
    init = ssm_state.reshape(B, SSD_GROUPS, SSD_REP, SSD_HEAD_DIM, SSD_STATE)
    y, final = ssd_scan(x, dt, a, b_in, c_in, init)
    y = y + d_skip.reshape(SSD_GROUPS, SSD_REP)[:, :, None] * x
    y = y.astype(h.dtype).reshape(B, T, SSD_D_INNER)
    y = rmsnorm(y * jax.nn.silu(z), norm_w)
    new_state = final.reshape(B, SSD_HEADS, SSD_HEAD_DIM, SSD_STATE).astype(ssm_state.dtype)
    return y @ w_out, new_conv, new_state


def setup_inputs(seed: int = 0) -> dict:
    key = jax.random.key(seed)
    ks = jax.random.split(key, 32)

    def nrm(i, shape, scale):
        return jax.random.normal(ks[i], shape, jnp.float32) * scale

    n_pages = PAST_LEN // PAGE_SIZE
    used = DEC_BATCH * n_pages
    n_pool = (5 * used + 3) // 4
    page_table = jax.random.permutation(ks[0], n_pool)[:used].reshape(DEC_BATCH, n_pages).astype(jnp.int32)
    win_keep = min(WINDOW, PAST_LEN)
    kv_row = (2, NSA_KV_HEADS, NSA_HEAD_DIM)
    dt0 = jnp.exp(jax.random.uniform(ks[1], (N_SSD, SSD_HEADS), jnp.float32, math.log(1e-3), math.log(1e-1)))
    return {
        'x_prompt': nrm(2, (BATCH, SEQ, D_MODEL), 1.0),
        'x_sample': nrm(3, (DEC_BATCH, DEC_SEQ, D_MODEL), 1.0),
        'cache_kv_cmp': nrm(4, (N_NSA, n_pool, PAGE_SIZE) + kv_row, 1.0),
        'cache_kv_sel': nrm(5, (N_NSA, n_pool, PAGE_SIZE) + kv_row, 1.0),
        'cache_kv_win': nrm(6, (N_NSA, DEC_BATCH, win_keep) + kv_row, 1.0),
        'state_ssm': nrm(7, (N_SSD, DEC_BATCH, SSD_HEADS, SSD_HEAD_DIM, SSD_STATE), 0.3),
        'state_conv': nrm(8, (N_SSD, DEC_BATCH, SSD_CONV_W - 1, SSD_CONV_DIM), 1.0),
        'page_table': page_table,
        'c_prompt': nrm(9, (BATCH, D_MODEL), 1.0),
        'c_sample': nrm(10, (DEC_BATCH, D_MODEL), 1.0),
        'ada_w': nrm(11, (DEPTH, D_MODEL, 6 * D_MODEL), 0.5 * D_MODEL ** -0.5),
        'ada_b': nrm(12, (DEPTH, 6 * D_MODEL), 0.02),
        'norm_w': 1.0 + nrm(13, (DEPTH, 2, D_MODEL), 0.05),
        'mlp_w1': nrm(14, (DEPTH, D_MODEL, D_FF), D_MODEL ** -0.5),
        'mlp_w2': nrm(15, (DEPTH, D_FF, D_MODEL), D_FF ** -0.5),
        'nsa_w_in': nrm(16, (N_NSA, D_MODEL, NSA_IN), D_MODEL ** -0.5),
        'nsa_w_out': nrm(17, (N_NSA, NSA_HEADS * NSA_HEAD_DIM, D_MODEL), (NSA_HEADS * NSA_HEAD_DIM) ** -0.5),
        'nsa_cmp_pe': nrm(18, (N_NSA, 2, CMP_BLOCK, NSA_HEAD_DIM), 0.1),
        'nsa_cmp_w1': nrm(19, (N_NSA, 2, CMP_BLOCK * NSA_HEAD_DIM, NSA_HEAD_DIM), (CMP_BLOCK * NSA_HEAD_DIM) ** -0.5),
        'nsa_cmp_w2': nrm(20, (N_NSA, 2, NSA_HEAD_DIM, NSA_HEAD_DIM), NSA_HEAD_DIM ** -0.5),
        'ssd_w_in': nrm(21, (N_SSD, D_MODEL, SSD_IN), D_MODEL ** -0.5),
        'ssd_conv_w': nrm(22, (N_SSD, SSD_CONV_W, SSD_CONV_DIM), SSD_CONV_W ** -0.5),
        'ssd_conv_b': nrm(23, (N_SSD, SSD_CONV_DIM), 0.02),
        'ssd_dt_bias': dt0 + jnp.log(-jnp.expm1(-dt0)),
        'ssd_a_log': jnp.log(jax.random.uniform(ks[24], (N_SSD, SSD_HEADS), jnp.float32, 1.0, 16.0)),
        'ssd_d': 1.0 + nrm(25, (N_SSD, SSD_HEADS), 0.1),
        'ssd_norm_w': 1.0 + nrm(26, (N_SSD, SSD_D_INNER), 0.05),
        'ssd_w_out': nrm(27, (N_SSD, SSD_D_INNER, D_MODEL), SSD_D_INNER ** -0.5),
        'final_norm_w': 1.0 + nrm(28, (D_MODEL,), 0.05),
    }


def reference(x_prompt, x_sample, cache_kv_cmp, cache_kv_sel, cache_kv_win, state_ssm, state_conv,
              page_table, c_prompt, c_sample, ada_w, ada_b, norm_w, mlp_w1, mlp_w2,
              nsa_w_in, nsa_w_out, nsa_cmp_pe, nsa_cmp_w1, nsa_cmp_w2,
              ssd_w_in, ssd_conv_w, ssd_conv_b, ssd_dt_bias, ssd_a_log, ssd_d, ssd_norm_w, ssd_w_out,
              final_norm_w):
    xp, xs = x_prompt, x_sample
    bp = x_prompt.shape[0]
    cmp_p, cmp_s, sel_p, sel_s, win_p, win_s = [], [], [], [], [], []
    ssm_p, ssm_s, conv_p, conv_s = [], [], [], []
    for i in range(DEPTH):
        j = i // N_MIXERS
        mp = adaln(c_prompt, ada_w[i], ada_b[i])
        ms = adaln(c_sample, ada_w[i], ada_b[i])
        hp = modulate(xp, norm_w[i, 0], mp[0], mp[1])
        hs = modulate(xs, norm_w[i, 0], ms[0], ms[1])
        if i % N_MIXERS == 0:
            op, kc_p, ks_p, kw_p = nsa_prompt(hp, nsa_w_in[j], nsa_w_out[j], nsa_cmp_pe[j],
                                              nsa_cmp_w1[j], nsa_cmp_w2[j])
            os_, kc_s, ks_s, kw_s = nsa_sample(hs, cache_kv_cmp[j], cache_kv_sel[j], cache_kv_win[j],
                                               page_table, nsa_w_in[j], nsa_w_out[j], nsa_cmp_pe[j],
                                               nsa_cmp_w1[j], nsa_cmp_w2[j])
            cmp_p.append(kc_p); cmp_s.append(kc_s)
            sel_p.append(ks_p); sel_s.append(ks_s)
            win_p.append(kw_p); win_s.append(kw_s)
        else:
            ssd_w = (ssd_w_in[j], ssd_conv_w[j], ssd_conv_b[j], ssd_dt_bias[j], ssd_a_log[j],
                     ssd_d[j], ssd_norm_w[j], ssd_w_out[j])
            zero_conv = jnp.zeros((bp, SSD_CONV_W - 1, SSD_CONV_DIM), hp.dtype)
            zero_ssm = jnp.zeros((bp, SSD_HEADS, SSD_HEAD_DIM, SSD_STATE), jnp.float32)
            op, cv_p, st_p = ssd_mixer(hp, zero_conv, zero_ssm, *ssd_w)
            os_, cv_s, st_s = ssd_mixer(hs, state_conv[j], state_ssm[j], *ssd_w)
            conv_p.append(cv_p); conv_s.append(cv_s)
            ssm_p.append(st_p); ssm_s.append(st_s)
        xp = xp + mp[2] * op
        xs = xs + ms[2] * os_
        hp = modulate(xp, norm_w[i, 1], mp[3], mp[4])
        hs = modulate(xs, norm_w[i, 1], ms[3], ms[4])
        xp = xp + mp[5] * sq_relu_mlp(hp, mlp_w1[i], mlp_w2[i])
        xs = xs + ms[5] * sq_relu_mlp(hs, mlp_w1[i], mlp_w2[i])
    y_prompt = rmsnorm(xp, final_norm_w)
    y_sample = rmsnorm(xs, final_norm_w)
    return (y_prompt, y_sample, jnp.stack(cmp_p), jnp.stack(cmp_s), jnp.stack(sel_p), jnp.stack(sel_s),
            jnp.stack(win_p), jnp.stack(win_s), jnp.stack(ssm_p), jnp.stack(ssm_s),
            jnp.stack(conv_p), jnp.stack(conv_s))
```

```python
import contextlib
import math
import numpy as np
import concourse.bass as bass
import concourse.mybir as mybir
from concourse.bass_utils import run_bass_kernel_spmd

F32 = mybir.dt.float32
BF16 = mybir.dt.bfloat16
I32 = mybir.dt.int32
U32 = mybir.dt.uint32
AF = mybir.ActivationFunctionType
ALU = mybir.AluOpType
AX = mybir.AxisListType

EPOCH = 40000
NDMASEM = 8
NEGB = -1.0e5
SCALE = 0.125
EPS = 1e-6
BIGV = 1e30


class Buf:
    __slots__ = ("t", "name", "w", "rc", "rd", "const", "nowaw", "ws", "psum")

    def __init__(self, t, name):
        self.t = t
        self.name = name
        self.w = None
        self.rc = {}
        self.rd = []
        self.const = False
        self.psum = False
        self.nowaw = False
        self.ws = []

    def __getitem__(self, idx):
        return self.t[idx]


class Q:
    def __init__(self, name):
        self.name = name
        self.ops = []
        self.sems = []
        self.count = 0
        self.known = {}
        self.pending = []
        self.dsems = []
        self.dcount = 0
        self.dlast = {}

    def cur_dep(self):
        if not self.sems or self.count == 0:
            return None
        return (self.name, self.sems[-1], self.count, False)


class Sched:
    def __init__(self, nc, root):
        self.nc = nc
        self.root = root
        self.stack = root
        self.q = {n: Q(n) for n in ("pe", "dve", "act", "pool", "sp")}
        self.bufs = []
        self.n_ops = 0
        self.psums = []
        self.pi = 0
        self.mute = False
        self.pa = 0
        self.uid = 0

    def sem(self, name):
        return self.root.enter_context(self.nc.semaphore(name))

    def sbuf(self, name, shape, dt=F32):
        self.uid += 1
        t = self.stack.enter_context(self.nc.sbuf_tensor(f"{name}_{self.uid}", list(shape), dt))
        b = Buf(t, name)
        self.bufs.append(b)
        return b

    def view(self, ap, name):
        b = Buf(ap, name)
        self.bufs.append(b)
        return b

    def dram(self, name, shape, dt=F32, kind="Internal"):
        t = self.nc.dram_tensor(name, list(shape), dt, kind=kind)
        b = Buf(t.ap(), name)
        b.nowaw = True
        self.bufs.append(b)
        return b

    def init_psum(self):
        for i in range(8):
            t = self.root.enter_context(self.nc.psum_tensor(f"psum{i}", [128, 512], F32))
            b = Buf(t, f"psum{i}")
            b.psum = True
            self.psums.append(b)
            self.bufs.append(b)

    def ps(self):
        b = self.psums[self.pi % 6]
        self.pi += 1
        return b

    def ps_acc(self):
        b = self.psums[6 + self.pa % 2]
        self.pa += 1
        return b

    @contextlib.contextmanager
    def scope(self):
        st = contextlib.ExitStack()
        prev = self.stack
        self.stack = st
        nb = len(self.bufs)
        try:
            yield self
        finally:
            self.barrier()
            del self.bufs[nb:]
            st.close()
            self.stack = prev

    def _need(self, q, deps):
        waits = []
        for d in deps:
            if d is None:
                continue
            qn, sem, val, is_dma = d
            if qn == q.name and not is_dma and q.name == "pe":
                continue
            k = id(sem)
            if q.known.get(k, 0) >= val:
                continue
            q.known[k] = val
            waits.append((sem, val))
        return waits

    def _gather(self, reads, writes, qn=None):
        deps = []
        for b in reads:
            if b.const:
                continue
            if b.nowaw:
                deps.extend(b.ws)
            else:
                deps.append(b.w)
            if b.psum:
                deps.extend(d for k, d in b.rc.items() if k != qn)
        for b in writes:
            if not b.nowaw:
                deps.append(b.w)
            deps.extend(b.rc.values())
            deps.extend(b.rd)
        return deps

    def _commit(self, dep, reads, writes, is_dma):
        for b in reads:
            if b.const:
                continue
            if is_dma:
                b.rd.append(dep)
            else:
                b.rc[dep[0]] = dep
        for b in writes:
            if b.nowaw:
                b.ws.append(dep)
            else:
                b.w = dep
            b.rc = {}
            b.rd = []

    def op(self, qn, fn, reads=(), writes=()):
        if self.mute:
            return None
        q = self.q[qn]
        deps = self._gather(reads, writes, qn)
        waits = q.pending + self._need(q, deps)
        q.pending = []
        if not q.sems or q.count >= EPOCH:
            q.sems.append(self.sem(f"{qn}_e{len(q.sems)}"))
            q.count = 0
        q.count += 1
        sem = q.sems[-1]
        dep = (qn, sem, q.count, False)
        q.ops.append((waits, fn, (sem, 1)))
        self._commit(dep, reads, writes, False)
        self.n_ops += 1
        return dep

    def dma(self, qn, out, in_, reads=(), writes=(), fn=None):
        if self.mute:
            return None
        q = self.q[qn]
        deps = self._gather(reads, writes)
        if not q.dsems:
            q.dsems = [self.sem(f"{qn}_d{i}") for i in range(NDMASEM)]
        slot = q.dcount % NDMASEM
        sem = q.dsems[slot]
        val = 16 * (q.dcount // NDMASEM + 1)
        prev = q.dlast.get(slot)
        if prev is not None:
            deps.append(prev)
        waits = q.pending + self._need(q, deps)
        q.pending = []
        q.dcount += 1
        dep = (qn, sem, val, True)
        q.dlast[slot] = dep
        if fn is None:
            def fn(e, out=out, in_=in_):
                return e.dma_start(out=out, in_=in_)
        q.ops.append((waits, fn, (sem, 16)))
        self._commit(dep, reads, writes, True)
        self.n_ops += 1
        return dep

    def all_deps(self):
        deps = []
        for q in self.q.values():
            deps.append(q.cur_dep())
            deps.extend(q.dlast.values())
        return deps

    def barrier(self):
        deps = self.all_deps()
        for q in self.q.values():
            q.pending = q.pending + self._need(q, deps)
        for b in self.bufs:
            b.w = None
            b.ws = []
            b.rc = {}
            b.rd = []

    def finish(self, qn="sp"):
        q = self.q[qn]
        waits = q.pending + self._need(q, self.all_deps())
        q.pending = []
        q.ops.append((waits, None, None))

    def emit(self):
        nc = self.nc
        with nc.Block() as block:
            def run(q):
                def body(e):
                    for waits, fn, inc in q.ops:
                        for sem, val in waits:
                            e.wait_ge(sem, val)
                        if fn is not None:
                            ins = fn(e)
                            ins.then_inc(inc[0], inc[1])
                    for sem, val in q.pending:
                        e.wait_ge(sem, val)
                return body
            block.sync(run(self.q["sp"]))
            block.tensor(run(self.q["pe"]))
            block.vector(run(self.q["dve"]))
            block.scalar(run(self.q["act"]))
            block.gpsimd(run(self.q["pool"]))


class Rot:
    def __init__(self, items):
        self.items = items
        self.i = 0

    def next(self):
        b = self.items[self.i % len(self.items)]
        self.i += 1
        return b


class Cfg:
    def __init__(self, T=2048, NL=4, NPOOL=2560, dbg=False):
        self.T = T
        self.NL = NL
        self.NN = (NL + 1) // 2
        self.NS = NL // 2
        self.NPOOL = NPOOL
        self.TT = T + 16
        self.NQT = T // 512
        self.NKC = T // 128
        self.NCB = (T - 32) // 16 + 1
        self.NSEL = T // 64
        self.NCH = T // 256
        self.dbg = dbg


def make_consts(cfg):
    T = cfg.T
    c = {}
    c["c_ident"] = np.eye(128, dtype=np.float32)
    c["c_ones"] = np.ones((128, 128), np.float32)
    i = np.arange(128)[:, None]
    t = np.arange(T)[None, :]
    cmpb = np.where((16 * i + 31 <= t) & (i < cfg.NCB), 0.0, NEGB).astype(np.float32)
    c["c_cmpb"] = cmpb
    p = np.arange(128)[:, None]
    f = np.arange(1024)[None, :]
    c["c_cb"] = np.where(f - p >= 512, 0.0, NEGB).astype(np.float32)
    f = np.arange(896)[None, :]
    c["c_wb"] = np.where(f - p <= 384, 0.0, NEGB).astype(np.float32)
    e = np.zeros((32, cfg.NKC, 128), np.float32)
    for kc in range(cfg.NKC):
        for pp in range(128):
            j = 2 * kc + pp // 64
            if j < 32:
                e[j, kc, pp] = 1.0
    c["c_eall"] = e
    ovl = np.zeros((128, 32), np.float32)
    cs = np.arange(cfg.NCB) * 16
    ss = np.arange(cfg.NSEL) * 64
    o = (cs[:, None] < ss[None, :] + 64) & (cs[:, None] + 32 > ss[None, :])
    ovl[:cfg.NCB, :cfg.NSEL] = o
    c["c_ovl"] = ovl
    ovs = np.zeros((128, 32), np.float32)
    cs = np.arange(127) * 16
    ss = np.arange(32) * 64
    ovs[:127, :] = (cs[:, None] < ss[None, :] + 64) & (cs[:, None] + 32 > ss[None, :])
    c["c_ovs"] = ovs
    nq = T // 128
    tt = (np.arange(nq)[None, :, None] * 128 + np.arange(128)[:, None, None])
    j = np.arange(32)[None, None, :]
    valid = (j < cfg.NSEL) & (j * 64 <= tt)
    cur = tt // 64
    forced = (j == 0) | (j == cur) | (j == cur - 1)
    c["c_m1"] = (valid & ~forced).astype(np.float32)
    c["c_m2"] = np.where(~valid, -BIGV, np.where(forced, BIGV, 0.0)).astype(np.float32)
    c["c_valid"] = valid.astype(np.float32)
    jj = np.arange(128)[:, None]
    ll = np.arange(128)[None, :]
    tri = (jj <= ll).astype(np.float32)
    c["c_tri"] = tri
    tf = np.zeros((128, 2, 256), np.float32)
    tf[:, 0, :128] = tri
    tf[:, 0, 128:] = 1.0
    tf[:, 1, 128:] = tri
    c["c_trifull"] = tf
    ca = np.zeros((128, 2, 256), np.float32)
    l = np.arange(256)[None, :]
    for st in range(2):
        s = st * 128 + np.arange(128)[:, None]
        ca[:, st, :] = (l >= s)
    c["c_causal01"] = ca
    e2 = np.zeros((8, 4, 128), np.float32)
    for m in range(4):
        for pp in range(128):
            e2[2 * m + pp // 64, m, pp] = 1.0
    c["c_e2"] = e2
    c["c_p64"] = (np.arange(128) % 64).astype(np.float32).reshape(128, 1)
    c["c_p128"] = np.arange(128).astype(np.float32).reshape(128, 1)
    c["c_iota32"] = np.tile(np.arange(32, dtype=np.float32)[None, :], (64, 1))
    negh = np.zeros((128, 4), np.float32)
    negh[64:, :] = NEGB
    c["c_negh"] = negh
    selb = np.zeros((16, 16, 128), np.float32)
    for b in range(16):
        selb[b, b, :] = 1.0
    c["c_selb"] = selb
    ehp = np.zeros((32, 16, 128), np.float32)
    for hp in range(16):
        for pp in range(128):
            ehp[2 * hp + pp // 64, hp, pp] = 1.0
    c["c_ehp"] = ehp
    m1s = np.ones((64, 32), np.float32)
    m1s[:, 0] = 0
    m1s[:, 31] = 0
    m2s = np.zeros((64, 32), np.float32)
    m2s[:, 0] = BIGV
    m2s[:, 31] = BIGV
    c["c_m1s"] = m1s
    c["c_m2s"] = m2s
    return c


def build(cfg):
    T, TT, NQT, NKC, NCB, NL, NN, NS = cfg.T, cfg.TT, cfg.NQT, cfg.NKC, cfg.NCB, cfg.NL, cfg.NN, cfg.NS
    NQ128 = T // 128
    NROWS = cfg.NPOOL * 128
    nc = bass.Bass("TRN2", target_bir_lowering=False)
    root = contextlib.ExitStack()
    S = Sched(nc, root)
    S.init_psum()

    I = {}
    O = {}

    def inp(name, shape, dt=F32):
        I[name] = nc.dram_tensor(name, list(shape), dt, kind="ExternalInput").ap()
        return I[name]

    def outp(name, shape, dt=F32):
        b = S.dram(name, shape, dt, kind="ExternalOutput")
        O[name] = b
        return b

    inp("xT", [1024, T]); inp("xsT", [1024, 16]); inp("cT", [1024, 17]); inp("pt", [16, 16], I32)
    inp("ada_w", [NL, 1024, 6144]); inp("ada_bT", [NL, 128, 48]); inp("norm_wT", [NL, 128, 16])
    inp("mlp_w1", [NL, 1024, 4096]); inp("mlp_w2", [NL, 4096, 1024]); inp("final_wT", [128, 8])
    if NN:
        inp("cwin", [NN, 16, 512, 512]); inp("pool_c", [NN, NROWS, 512]); inp("pool_s", [NN, NROWS, 512])
        inp("nsa_w_in", [NN, 1024, 2608]); inp("nsa_w_out", [NN, 1024, 1024])
        inp("cmp_pe2", [NN, 128, 2, 16]); inp("cmp_w1std", [NN, 128, 2, 16, 64]); inp("cmp_w1d", [NN, 128, 2, 32, 64])
        inp("cmp_w2", [NN, 64, 2, 64])
    if NS:
        inp("sssm", [NS, 16, 2048, 128]); inp("sconvT", [NS, 3072, 16, 3])
        inp("ssd_w_in", [NS, 1024, 5152]); inp("ssd_conv_wT", [NS, 128, 24, 4]); inp("ssd_conv_bT", [NS, 128, 24])
        inp("ssd_dtb", [NS, 1, 32]); inp("ssd_alog", [NS, 1, 32]); inp("ssd_dT", [NS, 128, 16])
        inp("ssd_dtbT", [NS, 32, 1]); inp("ssd_alogT", [NS, 32, 1])
        inp("ssd_norm_wT", [NS, 128, 16]); inp("ssd_w_out", [NS, 2048, 1024])
    consts = make_consts(cfg)
    for k, v in consts.items():
        inp(k, list(v.shape))

    outp("yT", [1024, TT])
    for j in range(NN):
        for cn in ("c", "s", "w"):
            outp(f"kvp_{cn}{j}", [512, T])
            outp(f"kvs_{cn}{j}", [512, 16])
        outp(f"kvwin_p{j}", [512, min(512, T)])
        outp(f"kvwin_old{j}", [16, 511, 512])
    for j in range(NS):
        outp(f"ssmp{j}", [16, 128, 128])
        outp(f"ssms{j}", [16, 2048, 128])
        outp(f"convp{j}", [3072, 3])
        outp(f"convs{j}", [3072, 16, 3])
    qT_scr = S.dram("qT_scr", [1024, T])
    g_scr = S.dram("g_scr", [48, T])
    vtok_scr = [S.dram(f"vtok_scr{a}", [T, 256], BF16) for a in range(2)]
    oT_scr = S.dram("oT_scr", [1024, T], BF16)
    z_scr = S.dram("z_scr", [2048, TT])
    y_scr = S.dram("y_scr", [2048, TT])
    xbc_scr = S.dram("xbc_scr", [3072, T])
    xbcA_scr = S.dram("xbcA_scr", [3072, T])

    def ntok(tt):
        return 512 if tt < NQT else 16

    def tsl(tt):
        return slice(tt * 512, tt * 512 + ntok(tt))

    xT = S.sbuf("xT", [128, 8, TT])
    xt = [S.view(xT.t[:, :, tsl(tt)], f"xt{tt}") for tt in range(NQT + 1)]
    ident = S.sbuf("ident", [128, 128]); ones = S.sbuf("ones", [128, 128])
    identb = S.sbuf("identb", [128, 128], BF16)
    scT = S.sbuf("scT", [128, 8, 17]); scTb = S.sbuf("scTb", [128, 8, 17], BF16)
    mod = S.sbuf("mod", [128, 48, 17])
    A1 = S.sbuf("A1", [128, 8, 17]); A2 = S.sbuf("A2", [128, 8, 17])
    nw = S.sbuf("nw", [128, 16]); adab = S.sbuf("adab", [128, 48])
    fw = S.sbuf("fw", [128, 8])
    eps_t = S.sbuf("eps_t", [128, 1])

    S.dma("sp", xT[:, :, 0:T], I["xT"].rearrange("(c p) t -> p c t", p=128), writes=[xT])
    S.dma("sp", xT[:, :, T:TT], I["xsT"].rearrange("(c p) t -> p c t", p=128), writes=[xT])
    S.dma("sp", scT[:], I["cT"].rearrange("(c p) t -> p c t", p=128), writes=[scT])
    S.dma("sp", ident[:], I["c_ident"], writes=[ident])
    S.dma("sp", ones[:], I["c_ones"], writes=[ones])
    S.dma("sp", fw[:], I["final_wT"], writes=[fw])
    S.op("dve", lambda e: e.memset(eps_t[:], EPS), writes=[eps_t])
    S.op("act", lambda e: e.activation(out=scT[:], in_=scT[:], func=AF.Silu), reads=[scT], writes=[scT])
    S.op("dve", lambda e: e.tensor_copy(out=scTb[:], in_=scT[:]), reads=[scT], writes=[scTb])
    S.op("dve", lambda e: e.tensor_copy(out=identb[:], in_=ident[:]), reads=[ident], writes=[identb])
    S.barrier()
    ident.const = True
    identb.const = True
    ones.const = True
    scTb.const = True

    def mm(out_b, out_ap, l_b, l_ap, r_b, r_ap, start=True, stop=True):
        S.op("pe", lambda e: e.matmul(out_ap, lhsT=l_ap, rhs=r_ap, start=start, stop=stop),
             reads=[l_b, r_b], writes=[out_b])

    def tr(out_b, out_ap, in_b, in_ap, np_in):
        S.op("pe", lambda e: e.transpose(out=out_ap, in_=in_ap, identity=ident[0:np_in, 0:np_in]),
             reads=[in_b], writes=[out_b])

    def act(out_b, out_ap, in_b, in_ap, func, bias=None, scale=1.0, extra_reads=()):
        if bias is None:
            S.op("act", lambda e: e.activation(out=out_ap, in_=in_ap, func=func, scale=scale),
                 reads=[in_b, *extra_reads], writes=[out_b])
        else:
            S.op("act", lambda e: e.activation(out=out_ap, in_=in_ap, func=func, bias=bias, scale=scale),
                 reads=[in_b, *extra_reads], writes=[out_b])

    def tt_op(eng, out_b, out_ap, a_b, a_ap, b_b, b_ap, op):
        S.op(eng, lambda e: e.tensor_tensor(out=out_ap, in0=a_ap, in1=b_ap, op=op), reads=[a_b, b_b], writes=[out_b])

    def ts_op(eng, out_b, out_ap, a_b, a_ap, s1, s2, op0, op1=None, extra_reads=()):
        if op1 is None:
            S.op(eng, lambda e: e.tensor_scalar(out=out_ap, in0=a_ap, scalar1=s1, scalar2=None, op0=op0),
                 reads=[a_b, *extra_reads], writes=[out_b])
        else:
            S.op(eng, lambda e: e.tensor_scalar(out=out_ap, in0=a_ap, scalar1=s1, scalar2=s2, op0=op0, op1=op1),
                 reads=[a_b, *extra_reads], writes=[out_b])

    def stt(out_b, out_ap, a_b, a_ap, scalar, b_b, b_ap, op0, op1, extra_reads=(), accum=None):
        if accum is None:
            S.op("dve", lambda e: e.scalar_tensor_tensor(out=out_ap, in0=a_ap, scalar=scalar, in1=b_ap, op0=op0, op1=op1),
                 reads=[a_b, b_b, *extra_reads], writes=[out_b])
        else:
            ab, aap = accum
            S.op("dve", lambda e: e.scalar_tensor_tensor(out=out_ap, in0=a_ap, scalar=scalar, in1=b_ap, op0=op0, op1=op1,
                                                         accum_out=aap),
                 reads=[a_b, b_b, *extra_reads], writes=[out_b, ab])

    def cp(eng, out_b, out_ap, in_b, in_ap):
        if eng == "act":
            S.op("act", lambda e: e.copy(out=out_ap, in_=in_ap), reads=[in_b], writes=[out_b])
        else:
            S.op(eng, lambda e: e.tensor_copy(out=out_ap, in_=in_ap), reads=[in_b], writes=[out_b])

    def recip(out_b, out_ap, in_b, in_ap):
        S.op("dve", lambda e: e.reciprocal(out=out_ap, in_=in_ap), reads=[in_b], writes=[out_b])

    def load_w(wt, W_ap, r0, nrow, c0, ncol):
        S.dma("pool", wt[:, 0:nrow // 128, 0:ncol], W_ap[r0:r0 + nrow, c0:c0 + ncol].rearrange("(c p) n -> p c n", p=128),
              writes=[wt])

    def rms_rinv(src_b, src_ap_fn, KC, N, sq_rot, r_b, denom):
        ps = S.ps()
        for k in range(KC):
            sq = sq_rot.next()
            S.op("act", lambda e, k=k, sq=sq: e.activation(out=sq[:, 0:N], in_=src_ap_fn(k), func=AF.Square),
                 reads=[src_b], writes=[sq])
            mm(ps, ps[:, 0:N], ones, ones[:, :], sq, sq[:, 0:N], start=(k == 0), stop=(k == KC - 1))
        act(r_b, r_b[:, 0:N], ps, ps[:, 0:N], AF.Sqrt, bias=eps_t[:, 0:1], scale=1.0 / denom, extra_reads=[eps_t])
        recip(r_b, r_b[:, 0:N], r_b, r_b[:, 0:N])

    def modulate(tt, A_b, Bsl, h, sq_rot, r_b):
        N = ntok(tt)
        x = xt[tt]
        rms_rinv(x, lambda k: x[:, k, :], 8, N, sq_rot, r_b, 1024.0)
        if tt < NQT:
            for k in range(8):
                tm = S._mtmp.next()
                stt(tm, tm[:, 0:N], x, x[:, k, :], A_b[:, k, 0:1], r_b, r_b[:, 0:N], ALU.mult, ALU.mult, extra_reads=[A_b])
                act(h, h[:, k, 0:N], tm, tm[:, 0:N], AF.Identity, bias=mod[:, Bsl + k, 0:1], extra_reads=[mod])
        else:
            tm = S._mtmp16
            tt_op("dve", tm, tm[:, :, 0:N], x, x[:, :, :], r_b, r_b[:, 0:N].unsqueeze(1).to_broadcast([128, 8, N]), ALU.mult)
            tt_op("dve", tm, tm[:, :, 0:N], tm, tm[:, :, 0:N], A_b, A_b[:, :, 1:17], ALU.mult)
            tt_op("dve", h, h[:, :, 0:N], tm, tm[:, :, 0:N], mod, mod[:, Bsl:Bsl + 8, 1:17], ALU.add)

    def resid_update(tt, kchunk, ps, N, Gsl):
        x = xt[tt]
        if tt < NQT:
            stt(x, x[:, kchunk, :], ps, ps[:, 0:N], mod[:, Gsl + kchunk, 0:1], x, x[:, kchunk, :], ALU.mult, ALU.add,
                extra_reads=[mod])
        else:
            tmp = S._tmp16.next()
            tt_op("dve", tmp, tmp[:, 0:N], ps, ps[:, 0:N], mod, mod[:, Gsl + kchunk, 1:17], ALU.mult)
            tt_op("dve", x, x[:, kchunk, :], x, x[:, kchunk, :], tmp, tmp[:, 0:N], ALU.add)

    S._tmp16 = Rot([S.sbuf(f"tmp16_{a}", [128, 16]) for a in range(4)])
    S._mtmp = Rot([S.sbuf(f"mtmp{a}", [128, 512]) for a in range(3)])
    S._mtmp16 = S.sbuf("mtmp16", [128, 8, 16])

    def out_proj(W_ap, KC, src_fn, tt, wts, Gsl):
        N = ntok(tt)
        for blk in range(2):
            pss = [S.ps() for _ in range(4)]
            nkg = (KC + 7) // 8
            for kg in range(nkg):
                kn = min(8, KC - kg * 8)
                wt = wts.next()
                load_w(wt, W_ap, kg * 1024, kn * 128, blk * 512, 512)
                for jc in range(4):
                    for k in range(kn):
                        sb, sap = src_fn(kg * 8 + k)
                        mm(pss[jc], pss[jc][:, 0:N], wt, wt[:, k, jc * 128:(jc + 1) * 128], sb, sap,
                           start=(kg == 0 and k == 0), stop=(kg == nkg - 1 and k == kn - 1))
            for jc in range(4):
                resid_update(tt, blk * 4 + jc, pss[jc], N, Gsl)

    def adaln(i):
        with S.scope():
            wts = Rot([S.sbuf(f"wt{a}", [128, 8, 512], BF16) for a in range(4)])
            S.dma("sp", adab[:], I["ada_bT"][i], writes=[adab])
            S.dma("sp", nw[:], I["norm_wT"][i], writes=[nw])
            for blk in range(12):
                wt = wts.next()
                load_w(wt, I["ada_w"][i], 0, 1024, blk * 512, 512)
                ps = S.ps()
                for jc in range(4):
                    for k in range(8):
                        mm(ps, ps[:, jc * 17:(jc + 1) * 17], wt, wt[:, k, jc * 128:(jc + 1) * 128], scTb, scTb[:, k, :],
                           start=(k == 0), stop=(k == 7))
                tt_op("dve", mod, mod[:, blk * 4:(blk + 1) * 4, :], ps, ps[:, 0:68].rearrange("p (a b) -> p a b", a=4),
                      adab, adab[:, blk * 4:(blk + 1) * 4].unsqueeze(2).to_broadcast([128, 4, 17]), ALU.add)
            for (Ab, sc0, w0) in ((A1, 8, 0), (A2, 32, 8)):
                ts_op("dve", Ab, Ab[:], mod, mod[:, sc0:sc0 + 8, :], 1.0, None, ALU.add)
                tt_op("dve", Ab, Ab[:], Ab, Ab[:], nw, nw[:, w0:w0 + 8].unsqueeze(2).to_broadcast([128, 8, 17]), ALU.mult)

    def mlp(i):
        with S.scope():
            wts = Rot([S.sbuf(f"wt{a}", [128, 8, 512], BF16) for a in range(4)])
            hb = Rot([S.sbuf(f"h{a}", [128, 8, 512], BF16) for a in range(2)])
            hid = S.sbuf("hid", [128, 32, 512], BF16)
            rls = Rot([S.sbuf(f"rl{a}", [128, 512]) for a in range(3)])
            sq_rot = Rot([S.sbuf(f"sq{a}", [128, 512]) for a in range(2)])
            r_b = S.sbuf("r", [128, 512])
            for tt in range(NQT + 1):
                N = ntok(tt)
                h = hb.next()
                modulate(tt, A2, 24, h, sq_rot, r_b)
                for blk in range(8):
                    wt = wts.next()
                    load_w(wt, I["mlp_w1"][i], 0, 1024, blk * 512, 512)
                    for jc in range(4):
                        ps = S.ps()
                        for k in range(8):
                            mm(ps, ps[:, 0:N], wt, wt[:, k, jc * 128:(jc + 1) * 128], h, h[:, k, 0:N], start=(k == 0), stop=(k == 7))
                        c = blk * 4 + jc
                        rl = rls.next()
                        act(rl, rl[:, 0:N], ps, ps[:, 0:N], AF.Relu)
                        tt_op("dve", hid, hid[:, c, 0:N], rl, rl[:, 0:N], rl, rl[:, 0:N], ALU.mult)
                out_proj(I["mlp_w2"][i], 32, lambda k, N=N: (hid, hid[:, k, 0:N]), tt, wts, 40)

    def nsa_layer(i, j):
        kvp = [O[f"kvp_{cn}{j}"] for cn in ("c", "s", "w")]
        kvs_o = [O[f"kvs_{cn}{j}"] for cn in ("c", "s", "w")]
        W_in = I["nsa_w_in"][j]
        with S.scope():
            qsT = S.sbuf("qsT", [128, 8, 16])
            kvs = S.sbuf("kvs", [128, 12, 16])
            sigGs = S.sbuf("sigGs", [48, 16])
            c0 = S.sbuf("c0", [64, 2])
            w2 = S.sbuf("w2", [64, 2, 64]); w2dup = S.sbuf("w2dup", [64, 128])
            oTs = S.sbuf("oTs", [128, 8, 16], BF16)
            S.dma("sp", w2[:], I["cmp_w2"][j], writes=[w2])
            S.dma("sp", w2dup[:, 0:64], I["cmp_w2"][j][:, 0, :], writes=[w2dup])
            S.dma("sp", w2dup[:, 64:128], I["cmp_w2"][j][:, 0, :], writes=[w2dup])
            with S.scope():
                wts = Rot([S.sbuf(f"wt{a}", [128, 8, 512], BF16) for a in range(4)])
                hb = Rot([S.sbuf(f"h{a}", [128, 8, 512], BF16) for a in range(2)])
                sq_rot = Rot([S.sbuf(f"sq{a}", [128, 512]) for a in range(2)])
                r_b = S.sbuf("r", [128, 512])
                evs = Rot([S.sbuf(f"ev{a}", [128, 512]) for a in range(6)])
                vts = Rot([S.sbuf(f"vt{a}", [128, 256], BF16) for a in range(3)])
                w1s = S.sbuf("w1s", [128, 2, 16, 64]); pe2 = S.sbuf("pe2", [128, 2, 16])
                S.dma("sp", w1s[:], I["cmp_w1std"][j], writes=[w1s])
                S.dma("sp", pe2[:], I["cmp_pe2"][j], writes=[pe2])
                for kv in range(2):
                    ps = S.ps()
                    for cch in range(16):
                        mm(ps, ps[0:64, 0:1], w1s, w1s[:, kv, cch, :], pe2, pe2[:, kv, cch:cch + 1], start=(cch == 0), stop=(cch == 15))
                    cp("dve", c0, c0[:, kv:kv + 1], ps, ps[0:64, 0:1])
                for tt in range(NQT + 1):
                    N = ntok(tt)
                    prompt = tt < NQT
                    h = hb.next()
                    modulate(tt, A1, 0, h, sq_rot, r_b)
                    for blk in range(6):
                        ncol = min(512, 2608 - blk * 512)
                        wt = wts.next()
                        load_w(wt, W_in, 0, 1024, blk * 512, ncol)
                        for jc in range((ncol + 127) // 128):
                            cw = min(128, ncol - jc * 128)
                            ps = S.ps()
                            for k in range(8):
                                mm(ps, ps[0:cw, 0:N], wt, wt[:, k, jc * 128:jc * 128 + cw], h, h[:, k, 0:N], start=(k == 0), stop=(k == 7))
                            gc = blk * 4 + jc
                            if gc < 8:
                                if prompt:
                                    ev = evs.next()
                                    cp("act", ev, ev[:, 0:N], ps, ps[:, 0:N])
                                    S.dma("sp", qT_scr[gc * 128:(gc + 1) * 128, tsl(tt)], ev[:, 0:N], reads=[ev], writes=[qT_scr])
                                else:
                                    cp("act", qsT, qsT[:, gc, :], ps, ps[:, 0:N])
                            elif gc < 20:
                                cache = (gc - 8) // 4
                                rr = (gc - 8) % 4
                                if prompt:
                                    ev = evs.next()
                                    cp("act", ev, ev[:, 0:N], ps, ps[:, 0:N])
                                    S.dma("sp", kvp[cache][rr * 128:(rr + 1) * 128, tsl(tt)], ev[:, 0:N], reads=[ev], writes=[kvp[cache]])
                                    if cache == 2 and tt * 512 >= T - 512:
                                        wo = O[f"kvwin_p{j}"]
                                        c_off = tt * 512 - max(0, T - 512)
                                        S.dma("sp", wo[rr * 128:(rr + 1) * 128, c_off:c_off + N], ev[:, 0:N], reads=[ev], writes=[wo])
                                else:
                                    cp("act", kvs, kvs[:, gc - 8, :], ps, ps[:, 0:N])
                            else:
                                if prompt:
                                    ev = evs.next()
                                    act(ev, ev[0:48, 0:N], ps, ps[0:48, 0:N], AF.Sigmoid)
                                    S.dma("sp", g_scr[:, tsl(tt)], ev[0:48, 0:N], reads=[ev], writes=[g_scr])
                                else:
                                    act(sigGs, sigGs[:, :], ps, ps[0:48, 0:N], AF.Sigmoid)
                        if prompt and blk in (3, 4):
                            for sub in range(4):
                                ps = S.ps()
                                for k in range(8):
                                    mm(ps, ps[:, 0:256], h, h[:, k, sub * 128:(sub + 1) * 128], wt, wt[:, k, 256:512], start=(k == 0), stop=(k == 7))
                                vt = vts.next()
                                cp("dve", vt, vt[:, :], ps, ps[:, 0:256])
                                r0 = tt * 512 + sub * 128
                                S.dma("sp", vtok_scr[blk - 3][r0:r0 + 128, :], vt[:, :], reads=[vt], writes=[vtok_scr[blk - 3]])
                for cache in range(3):
                    S.dma("sp", kvs_o[cache][:, :].rearrange("(c p) b -> p c b", p=128), kvs[:, cache * 4:(cache + 1) * 4, :],
                          reads=[kvs], writes=[kvs_o[cache]])
                wold = O[f"kvwin_old{j}"]
                for b in range(16):
                    S.dma("sp", wold[b], I["cwin"][j, b, 1:512, :], writes=[wold])
            for g in range(4):
                nsa_group(j, g, kvp, c0, w2, w2dup)
            nsa_sample(j, qsT, kvs, sigGs, c0, w2, oTs)
            with S.scope():
                wts = Rot([S.sbuf(f"wt{a}", [128, 8, 512], BF16) for a in range(4)])
                ots = Rot([S.sbuf(f"ot{a}", [128, 8, 512], BF16) for a in range(2)])
                for tt in range(NQT + 1):
                    N = ntok(tt)
                    if tt < NQT:
                        ot = ots.next()
                        S.dma("sp", ot[:, :, 0:N], oT_scr[:, tsl(tt)].rearrange("(c p) t -> p c t", p=128), reads=[oT_scr], writes=[ot])
                        out_proj(I["nsa_w_out"][j], 8, lambda k, ot=ot, N=N: (ot, ot[:, k, 0:N]), tt, wts, 16)
                    else:
                        out_proj(I["nsa_w_out"][j], 8, lambda k: (oTs, oTs[:, k, :]), tt, wts, 16)

    def nsa_group(j, g, kvp, c0, w2, w2dup):
        with S.scope():
            acc = S.sbuf("acc", [128, 2, T])
            sigG = S.sbuf("sigG", [48, T])
            kcT = S.sbuf("kcT", [128, 128]); vc = S.sbuf("vc", [128, 64])
            selbT = S.sbuf("selbT", [32, T], BF16)
            pnsum = S.sbuf("pnsum", [128, T])
            S.dma("sp", sigG[:], g_scr[:, :], reads=[g_scr], writes=[sigG])
            with S.scope():
                w1 = S.sbuf("w1", [64, 2, 32, 64])
                KV = S.sbuf("kvc", [64, 2, T])
                hid = S.sbuf("hid", [64, 2, 128])
                S.dma("sp", w1[:], I["cmp_w1d"][j][0:64], writes=[w1])
                S.dma("sp", KV[:, 0, :], kvp[0][g * 64:(g + 1) * 64, :], reads=[kvp[0]], writes=[KV])
                S.dma("sp", KV[:, 1, :], kvp[0][256 + g * 64:256 + (g + 1) * 64, :], reads=[kvp[0]], writes=[KV])
                for kv in range(2):
                    ps = S.ps()
                    for l in range(32):
                        mm(ps, ps[0:64, 0:NCB], w1, w1[:, kv, l, :], KV, KV[:, kv, l:l + 16 * (NCB - 1) + 1:16], start=(l == 0), stop=(l == 31))
                    act(hid, hid[:, kv, 0:NCB], ps, ps[0:64, 0:NCB], AF.Silu, bias=c0[:, kv:kv + 1], extra_reads=[c0])
                ps = S.ps()
                mm(ps, ps[:, 0:NCB], w2dup, w2dup[:, :], hid, hid[:, 0, 0:NCB])
                cp("dve", kcT, kcT[:, 0:NCB], ps, ps[:, 0:NCB])
                ps = S.ps()
                mm(ps, ps[0:NCB, 0:64], hid, hid[:, 1, 0:NCB], w2, w2[:, 1, :])
                cp("dve", vc, vc[0:NCB, :], ps, ps[0:NCB, 0:64])
            with S.scope():
                qb = [S.sbuf(f"q{a}", [128, T]) for a in range(2)]
                cmpb = S.sbuf("cmpb", [128, T])
                Pts = Rot([S.sbuf(f"P{a}", [128, 512]) for a in range(2)])
                rDs = Rot([S.sbuf(f"rD{a}", [128, 512]) for a in range(2)])
                pns = Rot([S.sbuf(f"pn{a}", [128, 512]) for a in range(2)])
                ogs = Rot([S.sbuf(f"og{a}", [64, 512]) for a in range(2)])
                for c2 in range(2):
                    S.dma("sp", qb[c2][:], qT_scr[(2 * g + c2) * 128:(2 * g + c2 + 1) * 128, :], reads=[qT_scr], writes=[qb[c2]])
                S.dma("sp", cmpb[:], I["c_cmpb"], writes=[cmpb])
                for qt in range(NQT):
                    qs = slice(qt * 512, (qt + 1) * 512)
                    for hh in range(4):
                        c2 = hh // 2
                        hb_ = 64 * (hh % 2)
                        h = 4 * g + hh
                        ps = S.ps()
                        mm(ps, ps[0:NCB, :], kcT, kcT[hb_:hb_ + 64, 0:NCB], qb[c2], qb[c2][hb_:hb_ + 64, qs], start=True, stop=False)
                        mm(ps, ps[0:NCB, :], ident, ident[0:NCB, 0:NCB], cmpb, cmpb[0:NCB, qs], start=False, stop=True)
                        Pt = Pts.next()
                        act(Pt, Pt[0:NCB, :], ps, ps[0:NCB, :], AF.Exp, scale=SCALE)
                        psd = S.ps()
                        mm(psd, psd[:, :], ones, ones[0:NCB, :], Pt, Pt[0:NCB, :])
                        rD = rDs.next()
                        ts_op("dve", rD, rD[:, :], psd, psd[:, :], 1e-30, None, ALU.max)
                        recip(rD, rD[:, :], rD, rD[:, :])
                        pn = pns.next()
                        tt_op("dve", pn, pn[0:NCB, :], Pt, Pt[0:NCB, :], rD, rD[0:NCB, :], ALU.mult)
                        if hh == 0:
                            cp("pool", pnsum, pnsum[0:NCB, qs], pn, pn[0:NCB, :])
                        else:
                            tt_op("pool", pnsum, pnsum[0:NCB, qs], pnsum, pnsum[0:NCB, qs], pn, pn[0:NCB, :], ALU.add)
                        pso = S.ps()
                        mm(pso, pso[0:64, :], vc, vc[0:NCB, 0:64], pn, pn[0:NCB, :])
                        psg = S.ps()
                        gi = h * 3 + 0
                        mm(psg, psg[0:64, :], ident, ident[0:48, gi:gi + 1].to_broadcast([48, 64]), sigG, sigG[0:48, qs])
                        og = ogs.next()
                        cp("act", og, og[:, :], psg, psg[0:64, :])
                        tt_op("dve", acc, acc[hb_:hb_ + 64, c2, qs], pso, pso[0:64, :], og, og[:, :], ALU.mult)
            with S.scope():
                m1 = S.sbuf("m1", [128, NQ128 * 32]); m2 = S.sbuf("m2", [128, NQ128 * 32]); vl = S.sbuf("vl", [128, NQ128 * 32])
                ovl = S.sbuf("ovl", [128, 32])
                score = S.sbuf("score", [128, NQ128 * 32]); sel = S.sbuf("sel", [128, NQ128 * 32])
                top8 = S.sbuf("top8", [128, NQ128, 8])
                S.dma("sp", m1[:], I["c_m1"].rearrange("p a b -> p (a b)"), writes=[m1])
                S.dma("sp", m2[:], I["c_m2"].rearrange("p a b -> p (a b)"), writes=[m2])
                S.dma("sp", vl[:], I["c_valid"].rearrange("p a b -> p (a b)"), writes=[vl])
                S.dma("sp", ovl[:], I["c_ovl"], writes=[ovl])
                ps = S.ps()
                for sub in range(NQ128):
                    mm(ps, ps[:, sub * 32:(sub + 1) * 32], pnsum, pnsum[0:NCB, sub * 128:(sub + 1) * 128], ovl, ovl[0:NCB, :])
                W_ = NQ128 * 32
                tt_op("dve", score, score[:, :], ps, ps[:, 0:W_], m1, m1[:, :], ALU.mult)
                tt_op("dve", score, score[:, :], score, score[:, :], m2, m2[:, :], ALU.add)
                for sub in range(NQ128):
                    S.op("dve", lambda e, sub=sub: e.max(out=top8[:, sub, :], in_=score[:, sub * 32:(sub + 1) * 32]), reads=[score], writes=[top8])
                tt_op("dve", sel, sel[:, :].rearrange("p (a b) -> p a b", b=32), score, score[:, :].rearrange("p (a b) -> p a b", b=32),
                      top8, top8[:, :, 7:8].to_broadcast([128, NQ128, 32]), ALU.is_ge)
                tt_op("dve", sel, sel[:, :], sel, sel[:, :], vl, vl[:, :], ALU.mult)
                ts_op("dve", sel, sel[:, :], sel, sel[:, :], -NEGB, NEGB, ALU.mult, ALU.add)
                for qt in range(NQT):
                    ps = S.ps()
                    for s4 in range(4):
                        sub = qt * 4 + s4
                        tr(ps, ps[0:32, s4 * 128:(s4 + 1) * 128], sel, sel[:, sub * 32:(sub + 1) * 32], 128)
                    cp("act", selbT, selbT[:, qt * 512:(qt + 1) * 512], ps, ps[0:32, :])
            with S.scope():
                qb = [S.sbuf(f"q{a}", [128, T], BF16) for a in range(2)]
                Ks = S.sbuf("Ks", [128, T], BF16); Kw = S.sbuf("Kw", [128, T], BF16)
                VA = [S.sbuf(f"VA{a}", [128, NKC, 128], BF16) for a in range(2)]
                eall = S.sbuf("eall", [32, NKC, 128], BF16); cb = S.sbuf("cb", [128, 1024], BF16); wb = S.sbuf("wb", [128, 896], BF16)
                Pts = Rot([S.sbuf(f"P{a}", [128, 512], BF16) for a in range(4)])
                rDs = Rot([S.sbuf(f"rD{a}", [64, 512]) for a in range(2)])
                ogs = Rot([S.sbuf(f"og{a}", [64, 512]) for a in range(2)])
                tmps = Rot([S.sbuf(f"tmp{a}", [128, 512]) for a in range(2)])
                for c2 in range(2):
                    S.dma("pool", qb[c2][:], qT_scr[(2 * g + c2) * 128:(2 * g + c2 + 1) * 128, :], reads=[qT_scr], writes=[qb[c2]])
                for half in range(2):
                    S.dma("pool", Ks[half * 64:(half + 1) * 64, :], kvp[1][g * 64:(g + 1) * 64, :], reads=[kvp[1]], writes=[Ks])
                    S.dma("pool", Kw[half * 64:(half + 1) * 64, :], kvp[2][g * 64:(g + 1) * 64, :], reads=[kvp[2]], writes=[Kw])
                for a in range(2):
                    S.op("pool", lambda e, a=a: e.memset(VA[a][:, :, 64:128], 1.0), writes=[VA[a]])
                    S.dma("sp", VA[a][:, :, 0:64], vtok_scr[a][:, g * 64:(g + 1) * 64].rearrange("(c p) d -> p c d", p=128),
                          reads=[vtok_scr[a]], writes=[VA[a]])
                S.dma("pool", eall[:], I["c_eall"], writes=[eall])
                S.dma("pool", cb[:], I["c_cb"], writes=[cb])
                S.dma("pool", wb[:], I["c_wb"], writes=[wb])

                def finalize(psO, br, h, hb_, c2, qs):
                    rD = rDs.next()
                    ts_op("dve", rD, rD[:, :], psO, psO[64:128, :], 1e-30, None, ALU.max)
                    recip(rD, rD[:, :], rD, rD[:, :])
                    psg = S.ps()
                    gi = h * 3 + br
                    mm(psg, psg[0:64, :], ident, ident[0:48, gi:gi + 1].to_broadcast([48, 64]), sigG, sigG[0:48, qs])
                    og = ogs.next()
                    cp("act", og, og[:, :], psg, psg[0:64, :])
                    tt_op("pool", og, og[:, :], og, og[:, :], rD, rD[:, :], ALU.mult)
                    tmp = tmps.next()
                    tt_op("dve", tmp, tmp[hb_:hb_ + 64, :], psO, psO[0:64, :], og, og[:, :], ALU.mult)
                    tt_op("pool", acc, acc[hb_:hb_ + 64, c2, qs], acc, acc[hb_:hb_ + 64, c2, qs], tmp, tmp[hb_:hb_ + 64, :], ALU.add)

                for qt in range(NQT):
                    qs = slice(qt * 512, (qt + 1) * 512)
                    for hh in range(4):
                        c2 = hh // 2
                        hb_ = 64 * (hh % 2)
                        h = 4 * g + hh
                        qap = qb[c2][hb_:hb_ + 64, qs]
                        psO = S.ps_acc()
                        nk = 4 * qt + 4
                        for kc in range(nk):
                            ps = S.ps()
                            ks = slice(kc * 128, (kc + 1) * 128)
                            diag = kc >= 4 * qt
                            mm(ps, ps[:, :], Ks, Ks[hb_:hb_ + 64, ks], qb[c2], qap, start=True, stop=False)
                            mm(ps, ps[:, :], eall, eall[:, kc, :], selbT, selbT[:, qs], start=False, stop=not diag)
                            if diag:
                                d = 128 * kc - 512 * qt
                                mm(ps, ps[:, :], identb, identb[:, :], cb, cb[:, 512 - d:1024 - d], start=False, stop=True)
                            Pt = Pts.next()
                            act(Pt, Pt[:, :], ps, ps[:, :], AF.Exp, scale=SCALE)
                            mm(psO, psO[:, :], VA[0], VA[0][:, kc, :], Pt, Pt[:, :], start=(kc == 0), stop=(kc == nk - 1))
                        finalize(psO, 1, h, hb_, c2, qs)
                        psO = S.ps_acc()
                        kcs = list(range(max(0, 4 * qt - 4), 4 * qt + 4))
                        for ki, kc in enumerate(kcs):
                            ps = S.ps()
                            ks = slice(kc * 128, (kc + 1) * 128)
                            mm(ps, ps[:, :], Kw, Kw[hb_:hb_ + 64, ks], qb[c2], qap, start=True, stop=False)
                            if kc >= 4 * qt:
                                d = 128 * kc - 512 * qt
                                mm(ps, ps[:, :], identb, identb[:, :], cb, cb[:, 512 - d:1024 - d], start=False, stop=True)
                            else:
                                m = kc - 4 * qt + 4
                                mm(ps, ps[:, :], identb, identb[:, :], wb, wb[:, 384 - 128 * m:896 - 128 * m], start=False, stop=True)
                            Pt = Pts.next()
                            act(Pt, Pt[:, :], ps, ps[:, :], AF.Exp, scale=SCALE)
                            mm(psO, psO[:, :], VA[1], VA[1][:, kc, :], Pt, Pt[:, :], start=(ki == 0), stop=(ki == len(kcs) - 1))
                        finalize(psO, 2, h, hb_, c2, qs)
            for c2 in range(2):
                S.dma("pool", oT_scr[(2 * g + c2) * 128:(2 * g + c2 + 1) * 128, :], acc[:, c2, :], reads=[acc], writes=[oT_scr])

    def nsa_sample(j, qsT, kvs, sigGs, c0, w2, oTs):
        pool_c = I["pool_c"].rearrange("n r c -> (n r) c")
        pool_s = I["pool_s"].rearrange("n r c -> (n r) c")
        roff = float(j * NROWS)
        with S.scope():
            qsg = S.sbuf("qsg", [64, 16, 16]); knew = S.sbuf("knew", [64, 24, 16])
            Gb = S.sbuf("Gb", [64, 48, 16])
            Oc = S.sbuf("Oc", [64, 16, 16])
            Osd = [S.sbuf(f"Osd{a}", [64, 16, 16]) for a in range(4)]
            PN = S.sbuf("PN", [128, 64])
            osa = S.sbuf("osa", [64, 16, 16])
            ptb = S.sbuf("ptb", [128, 256], I32); ptf = S.sbuf("ptf", [128, 256]); idxc = S.sbuf("idxc", [128, 256], I32)
            p128 = S.sbuf("p128", [128, 1]); p64 = S.sbuf("p64", [128, 1])
            idxs = S.sbuf("idxs", [128, 256], I32)
            S.dma("sp", p128[:], I["c_p128"], writes=[p128]); S.dma("sp", p64[:], I["c_p64"], writes=[p64])
            cp("dve", qsg, qsg[:, 0:16:2, :], qsT, qsT[0:64, :, :])
            cp("dve", qsg, qsg[:, 1:16:2, :], qsT, qsT[64:128, :, :])
            cp("dve", knew, knew[:, 0:24:2, :], kvs, kvs[0:64, :, :])
            cp("dve", knew, knew[:, 1:24:2, :], kvs, kvs[64:128, :, :])
            for half, n in ((0, 32), (1, 16)):
                ps = S.ps()
                for a in range(n):
                    gi = half * 32 + a
                    mm(ps, ps[0:64, a * 16:(a + 1) * 16], ident, ident[0:48, gi:gi + 1].to_broadcast([48, 64]), sigGs, sigGs[:, :])
                cp("act", Gb, Gb[:, half * 32:half * 32 + n, :], ps, ps[0:64, 0:n * 16].rearrange("p (a b) -> p a b", b=16))
            S.dma("sp", ptb[:], I["pt"].rearrange("b n -> (b n)").partition_broadcast(128), writes=[ptb])
            cp("dve", ptf, ptf[:], ptb, ptb[:])
            ts_op("dve", ptf, ptf[:], ptf, ptf[:], 128.0, p128[:, 0:1], ALU.mult, ALU.add, extra_reads=[p128])
            if j > 0:
                ts_op("dve", ptf, ptf[:], ptf, ptf[:], roff, None, ALU.add)
            cp("dve", idxc, idxc[:], ptf, ptf[:])
            with S.scope():
                w1 = S.sbuf("w1", [128, 2, 32, 64], BF16)
                rowsT = S.sbuf("rowsT", [128, 4, 2048], BF16)
                gts = Rot([S.sbuf(f"gt{a}", [128, 512]) for a in range(4)])
                hid = S.sbuf("hid", [64, 2, 128])
                kcs_ = S.sbuf("kcs", [64, 128]); vcs = S.sbuf("vcs", [128, 64])
                Pt = S.sbuf("Pt", [128, 4]); rD = S.sbuf("rD", [128, 4]); pn = S.sbuf("pn", [128, 4])
                ovs = S.sbuf("ovs", [128, 32])
                S.dma("pool", w1[:], I["cmp_w1d"][j], writes=[w1])
                S.dma("sp", ovs[:], I["c_ovs"], writes=[ovs])
                for b in range(16):
                    for pg in range(16):
                        gt = gts.next()
                        col = b * 16 + pg
                        S.dma("pool", None, None, reads=[idxc], writes=[gt],
                              fn=lambda e, gt=gt, col=col: e.indirect_dma_start(
                                  out=gt[:, :], out_offset=None, in_=pool_c[:, :],
                                  in_offset=bass.IndirectOffsetOnAxis(ap=idxc[:, col:col + 1], axis=0)))
                        ps = S.ps()
                        for c4 in range(4):
                            tr(ps, ps[:, c4 * 128:(c4 + 1) * 128], gt, gt[:, c4 * 128:(c4 + 1) * 128], 128)
                        cp("act" if pg % 2 == 0 else "dve", rowsT, rowsT[:, :, pg * 128:(pg + 1) * 128],
                           ps, ps[:, :].rearrange("p (a b) -> p a b", a=4))
                    for g in range(4):
                        hb_ = 64 * (g % 2)
                        for kv in range(2):
                            c4 = kv * 2 + g // 2
                            ps = S.ps()
                            for l in range(32):
                                mm(ps, ps[0:64, 0:127], w1, w1[hb_:hb_ + 64, kv, l, :], rowsT, rowsT[hb_:hb_ + 64, c4, l:l + 16 * 126 + 1:16],
                                   start=(l == 0), stop=(l == 31))
                            act(hid, hid[:, kv, 0:127], ps, ps[0:64, 0:127], AF.Silu, bias=c0[:, kv:kv + 1], extra_reads=[c0])
                        ps = S.ps()
                        mm(ps, ps[0:64, 0:127], w2, w2[:, 0, :], hid, hid[:, 0, 0:127])
                        cp("dve", kcs_, kcs_[:, 0:127], ps, ps[0:64, 0:127])
                        ps = S.ps()
                        mm(ps, ps[0:127, 0:64], hid, hid[:, 1, 0:127], w2, w2[:, 1, :])
                        cp("dve", vcs, vcs[0:127, :], ps, ps[0:127, 0:64])
                        ps = S.ps()
                        mm(ps, ps[0:127, 0:4], kcs_, kcs_[:, 0:127], qsg, qsg[:, 4 * g:4 * g + 4, b])
                        act(Pt, Pt[0:127, :], ps, ps[0:127, 0:4], AF.Exp, scale=SCALE)
                        psd = S.ps()
                        mm(psd, psd[:, 0:4], ones, ones[0:127, :], Pt, Pt[0:127, :])
                        recip(rD, rD[:, :], psd, psd[:, 0:4])
                        tt_op("dve", pn, pn[0:127, :], Pt, Pt[0:127, :], rD, rD[0:127, :], ALU.mult)
                        m = b * 4 + g
                        S.op("dve", lambda e, m=m: e.tensor_reduce(out=PN[0:127, m:m + 1], in_=pn[0:127, :], axis=AX.X, op=ALU.add),
                             reads=[pn], writes=[PN])
                        pso = S.ps()
                        mm(pso, pso[0:64, 0:4], vcs, vcs[0:127, :], pn, pn[0:127, :])
                        cp("act", Oc, Oc[:, 4 * g:4 * g + 4, b], pso, pso[0:64, 0:4])
                m1s = S.sbuf("m1s", [64, 32]); m2s = S.sbuf("m2s", [64, 32]); iota32 = S.sbuf("iota32", [64, 32])
                score = S.sbuf("score", [64, 32]); top8 = S.sbuf("top8", [64, 8]); idxu = S.sbuf("idxu", [64, 8], U32)
                idxf = S.sbuf("idxf", [64, 8]); oh = S.sbuf("oh", [64, 32]); junk = S.sbuf("junk", [64, 32])
                ptg = S.sbuf("ptg", [64, 16], I32); ptgf = S.sbuf("ptgf", [64, 16]); PB = S.sbuf("PB", [64, 16, 2])
                phys = S.sbuf("phys", [64, 8]); physT = S.sbuf("physT", [8, 64]); e2 = S.sbuf("e2", [8, 4, 128])
                S.dma("sp", m1s[:], I["c_m1s"], writes=[m1s]); S.dma("sp", m2s[:], I["c_m2s"], writes=[m2s])
                S.dma("sp", iota32[:], I["c_iota32"], writes=[iota32]); S.dma("sp", e2[:], I["c_e2"], writes=[e2])
                for b in range(16):
                    S.dma("sp", ptg[4 * b:4 * b + 4, :], I["pt"][b, :].partition_broadcast(4), writes=[ptg])
                cp("dve", ptgf, ptgf[:], ptg, ptg[:])
                ts_op("dve", PB, PB[:, :, 0], ptgf, ptgf[:, :], 2.0, None, ALU.mult)
                ts_op("dve", PB, PB[:, :, 1], ptgf, ptgf[:, :], 2.0, 1.0, ALU.mult, ALU.add)
                ps = S.ps()
                mm(ps, ps[0:64, 0:32], PN, PN[0:127, 0:64], ovs, ovs[0:127, :])
                tt_op("dve", score, score[:], ps, ps[0:64, 0:32], m1s, m1s[:], ALU.mult)
                tt_op("dve", score, score[:], score, score[:], m2s, m2s[:], ALU.add)
                S.op("dve", lambda e: e.max(out=top8[:], in_=score[:]), reads=[score], writes=[top8])
                S.op("dve", lambda e: e.max_index(out=idxu[:], in_max=top8[:], in_values=score[:]), reads=[top8, score], writes=[idxu])
                cp("dve", idxf, idxf[:], idxu, idxu[:])
                PBf = PB[:, :, :].rearrange("p a b -> p (a b)")
                for k in range(8):
                    ts_op("dve", oh, oh[:], iota32, iota32[:], idxf[:, k:k + 1], None, ALU.is_equal, extra_reads=[idxf])
                    stt(junk, junk[:], oh, oh[:], 1.0, PB, PBf, ALU.mult, ALU.mult, accum=(phys, phys[:, k:k + 1]))
                ps = S.ps()
                tr(ps, ps[0:8, 0:64], phys, phys[0:64, 0:8], 64)
                cp("dve", physT, physT[:], ps, ps[0:8, 0:64])
                ps = S.ps()
                for m in range(4):
                    mm(ps, ps[:, m * 64:(m + 1) * 64], e2, e2[:, m, :], physT, physT[:, :])
                idsf = S.sbuf("idsf", [128, 256])
                ts_op("dve", idsf, idsf[:], ps, ps[:, 0:256], 64.0, p64[:, 0:1], ALU.mult, ALU.add, extra_reads=[p64])
                if j > 0:
                    ts_op("dve", idsf, idsf[:], idsf, idsf[:], roff, None, ALU.add)
                cp("dve", idxs, idxs[:], idsf, idsf[:])
            with S.scope():
                gss = Rot([S.sbuf(f"gs{a}", [128, 512]) for a in range(6)])
                wrs = [S.sbuf(f"wr{a}", [128, 512]) for a in range(4)]
                kTs = Rot([S.sbuf(f"kT{a}", [64, 128]) for a in range(3)])
                Pts = Rot([S.sbuf(f"P{a}", [128, 4]) for a in range(3)])
                negh = S.sbuf("negh", [128, 4])
                S.dma("sp", negh[:], I["c_negh"], writes=[negh])

                def branch(srcs, b, g, Ob, Db, last_bias):
                    psO = S.ps_acc()
                    psD = S.ps_acc()
                    for m in range(4):
                        src = srcs[m]
                        pst = S.ps()
                        tr(pst, pst[0:64, 0:128], src, src[:, g * 64:(g + 1) * 64], 128)
                        kT = kTs.next()
                        cp("act", kT, kT[:, :], pst, pst[0:64, 0:128])
                        pss = S.ps()
                        lb = last_bias and m == 3
                        mm(pss, pss[:, 0:4], kT, kT[:, :], qsg, qsg[:, 4 * g:4 * g + 4, b], start=True, stop=not lb)
                        if lb:
                            mm(pss, pss[:, 0:4], ident, ident[:, :], negh, negh[:, :], start=False, stop=True)
                        Pt = Pts.next()
                        act(Pt, Pt[:, :], pss, pss[:, 0:4], AF.Exp, scale=SCALE)
                        mm(psO, psO[0:64, 0:4], src, src[:, 256 + g * 64:256 + (g + 1) * 64], Pt, Pt[:, :], start=(m == 0), stop=(m == 3))
                        mm(psD, psD[0:64, 0:4], ones, ones[:, 0:64], Pt, Pt[:, :], start=(m == 0), stop=(m == 3))
                    cp("act", Ob, Ob[:, 4 * g:4 * g + 4, b], psO, psO[0:64, 0:4])
                    cp("dve", Db, Db[:, 4 * g:4 * g + 4, b], psD, psD[0:64, 0:4])

                for b in range(16):
                    for m in range(4):
                        S.dma("sp", wrs[m][:, :], I["cwin"][j, b, m * 128:(m + 1) * 128, :], writes=[wrs[m]])
                    for g in range(4):
                        srcs = []
                        for m in range(4):
                            gs = gss.next()
                            col = m * 64 + b * 4 + g
                            S.dma("pool", None, None, reads=[idxs], writes=[gs],
                                  fn=lambda e, gs=gs, col=col: e.indirect_dma_start(
                                      out=gs[:, :], out_offset=None, in_=pool_s[:, :],
                                      in_offset=bass.IndirectOffsetOnAxis(ap=idxs[:, col:col + 1], axis=0)))
                            srcs.append(gs)
                        branch(srcs, b, g, Osd[0], Osd[1], True)
                        branch(wrs, b, g, Osd[2], Osd[3], False)
            with S.scope():
                prod = S.sbuf("prod", [64, 16, 16]); pnew = S.sbuf("pnew", [64, 16, 16]); t1 = S.sbuf("t1", [64, 16, 16])
                rDn = S.sbuf("rDn", [64, 16, 16])
                tt_op("dve", osa, osa[:], Oc, Oc[:], Gb, Gb[:, 0:48:3, :], ALU.mult)
                for bi, cache in ((0, 1), (1, 2)):
                    Ob, Db = Osd[2 * bi], Osd[2 * bi + 1]
                    for g in range(4):
                        ki = cache * 8 + g
                        tt_op("dve", prod, prod[:, 4 * g:4 * g + 4, :], qsg, qsg[:, 4 * g:4 * g + 4, :],
                              knew, knew[:, ki:ki + 1, :].to_broadcast([64, 4, 16]), ALU.mult)
                    ps = S.ps()
                    mm(ps, ps[0:64, 0:256], ones, ones[0:64, 0:64], prod, prod[:, :, :].rearrange("p a b -> p (a b)"))
                    act(pnew, pnew[:, :, :].rearrange("p a b -> p (a b)"), ps, ps[0:64, 0:256], AF.Exp, scale=SCALE)
                    tt_op("dve", Db, Db[:], Db, Db[:], pnew, pnew[:], ALU.add)
                    for g in range(4):
                        vi = cache * 8 + 4 + g
                        tt_op("dve", t1, t1[:, 4 * g:4 * g + 4, :], pnew, pnew[:, 4 * g:4 * g + 4, :],
                              knew, knew[:, vi:vi + 1, :].to_broadcast([64, 4, 16]), ALU.mult)
                    tt_op("dve", Ob, Ob[:], Ob, Ob[:], t1, t1[:], ALU.add)
                    recip(rDn, rDn[:], Db, Db[:])
                    tt_op("dve", t1, t1[:], Ob, Ob[:], rDn, rDn[:], ALU.mult)
                    tt_op("dve", t1, t1[:], t1, t1[:], Gb, Gb[:, 1 + bi:48:3, :], ALU.mult)
                    tt_op("dve", osa, osa[:], osa, osa[:], t1, t1[:], ALU.add)
                cp("dve", oTs, oTs[0:64, :, :], osa, osa[:, 0:16:2, :])
                cp("dve", oTs, oTs[64:128, :, :], osa, osa[:, 1:16:2, :])

    def ssd_layer(i, j):
        W_in = I["ssd_w_in"][j]
        with S.scope():
            dt_tok = S.sbuf("dt_tok", [128, NQ128, 32]); dta = S.sbuf("dta", [128, NQ128, 32])
            dtb = S.sbuf("dtb", [128, 32]); aneg = S.sbuf("aneg", [128, 32])
            dtbT = S.sbuf("dtbT", [32, 1]); anegT = S.sbuf("anegT", [32, 1])
            zs = S.sbuf("zs", [128, 16, 16]); xbcs = S.sbuf("xbcs", [128, 24, 16]); dtsT = S.sbuf("dtsT", [32, 16])
            dsk = S.sbuf("dsk", [128, 16]); snw = S.sbuf("snw", [128, 16])
            one_t = S.sbuf("one_t", [128, 1])
            S.op("dve", lambda e: e.memset(one_t[:], 1.0), writes=[one_t])
            S.dma("sp", dtb[:], I["ssd_dtb"][j, 0, :].partition_broadcast(128), writes=[dtb])
            S.dma("sp", aneg[:], I["ssd_alog"][j, 0, :].partition_broadcast(128), writes=[aneg])
            S.dma("sp", dtbT[:], I["ssd_dtbT"][j], writes=[dtbT])
            S.dma("sp", anegT[:], I["ssd_alogT"][j], writes=[anegT])
            S.dma("sp", dsk[:], I["ssd_dT"][j], writes=[dsk])
            S.dma("sp", snw[:], I["ssd_norm_wT"][j], writes=[snw])
            act(aneg, aneg[:], aneg, aneg[:], AF.Exp)
            ts_op("dve", aneg, aneg[:], aneg, aneg[:], -1.0, None, ALU.mult)
            act(anegT, anegT[:], anegT, anegT[:], AF.Exp)
            ts_op("dve", anegT, anegT[:], anegT, anegT[:], -1.0, None, ALU.mult)

            def softplus(buf, ap, tmp_b, tmp_ap):
                act(tmp_b, tmp_ap, buf, ap, AF.Abs)
                act(tmp_b, tmp_ap, tmp_b, tmp_ap, AF.Exp, scale=-1.0)
                act(tmp_b, tmp_ap, tmp_b, tmp_ap, AF.Ln, bias=one_t[0:tmp_ap.shape[0], 0:1], extra_reads=[one_t])
                stt(buf, ap, buf, ap, 0.0, tmp_b, tmp_ap, ALU.max, ALU.add)

            with S.scope():
                wts = Rot([S.sbuf(f"wt{a}", [128, 8, 512], BF16) for a in range(4)])
                hb = Rot([S.sbuf(f"h{a}", [128, 8, 512], BF16) for a in range(2)])
                sq_rot = Rot([S.sbuf(f"sq{a}", [128, 512]) for a in range(2)])
                r_b = S.sbuf("r", [128, 512])
                evs = Rot([S.sbuf(f"ev{a}", [128, 512]) for a in range(6)])
                for tt in range(NQT + 1):
                    N = ntok(tt)
                    prompt = tt < NQT
                    h = hb.next()
                    modulate(tt, A1, 0, h, sq_rot, r_b)
                    for blk in range(11):
                        ncol = min(512, 5152 - blk * 512)
                        wt = wts.next()
                        load_w(wt, W_in, 0, 1024, blk * 512, ncol)
                        if blk == 10:
                            if prompt:
                                for sub in range(4):
                                    ps = S.ps()
                                    for k in range(8):
                                        mm(ps, ps[:, 0:32], h, h[:, k, sub * 128:(sub + 1) * 128], wt, wt[:, k, 0:32], start=(k == 0), stop=(k == 7))
                                    tt_op("dve", dt_tok, dt_tok[:, tt * 4 + sub, :], ps, ps[:, 0:32], dtb, dtb[:, :], ALU.add)
                            else:
                                ps = S.ps()
                                for k in range(8):
                                    mm(ps, ps[0:32, 0:16], wt, wt[:, k, 0:32], h, h[:, k, 0:16], start=(k == 0), stop=(k == 7))
                                ts_op("dve", dtsT, dtsT[:, :], ps, ps[0:32, 0:16], dtbT[:, 0:1], None, ALU.add, extra_reads=[dtbT])
                            continue
                        for jc in range(4):
                            ps = S.ps()
                            for k in range(8):
                                mm(ps, ps[:, 0:N], wt, wt[:, k, jc * 128:(jc + 1) * 128], h, h[:, k, 0:N], start=(k == 0), stop=(k == 7))
                            gc = blk * 4 + jc
                            if prompt:
                                ev = evs.next()
                                cp("act", ev, ev[:, 0:N], ps, ps[:, 0:N])
                                if gc < 16:
                                    S.dma("sp", z_scr[gc * 128:(gc + 1) * 128, tsl(tt)], ev[:, 0:N], reads=[ev], writes=[z_scr])
                                else:
                                    r0 = (gc - 16) * 128
                                    S.dma("sp", xbc_scr[r0:r0 + 128, tsl(tt)], ev[:, 0:N], reads=[ev], writes=[xbc_scr])
                                    if tt == NQT - 1:
                                        cvo = O[f"convp{j}"]
                                        S.dma("sp", cvo[r0:r0 + 128, :], ev[:, N - 3:N], reads=[ev], writes=[cvo])
                            else:
                                if gc < 16:
                                    cp("act", zs, zs[:, gc, :], ps, ps[:, 0:N])
                                else:
                                    cp("act", xbcs, xbcs[:, gc - 16, :], ps, ps[:, 0:N])
                tmpd = S.sbuf("tmpd", [128, NQ128, 32])
                softplus(dt_tok, dt_tok[:], tmpd, tmpd[:])
                tt_op("dve", dta, dta[:], dt_tok, dt_tok[:], aneg, aneg[:, :].unsqueeze(1).to_broadcast([128, NQ128, 32]), ALU.mult)
            import os
            STG = int(os.environ.get("SSD_STAGE", "9"))
            cw = S.sbuf("cw", [128, 24, 4]); cbias = S.sbuf("cbias", [128, 24])
            S.dma("sp", cw[:], I["ssd_conv_wT"][j], writes=[cw])
            S.dma("sp", cbias[:], I["ssd_conv_bT"][j], writes=[cbias])
            S.mute = STG < 2
            with S.scope():
                xins = [S.sbuf(f"xin{a}", [128, T + 3]) for a in range(2)]
                accs = [S.sbuf(f"cacc{a}", [128, T]) for a in range(2)]
                for a in range(2):
                    S.op("pool", lambda e, a=a: e.memset(xins[a][:, 0:3], 0.0), writes=[xins[a]])
                for c in range(24):
                    xin = xins[c % 2]
                    ac = accs[c % 2]
                    S.dma("sp", xin[:, 3:T + 3], xbc_scr[c * 128:(c + 1) * 128, :], reads=[xbc_scr], writes=[xin])
                    ts_op("dve", ac, ac[:, :], xin, xin[:, 3:T + 3], cw[:, c, 3:4], cbias[:, c:c + 1], ALU.mult, ALU.add, extra_reads=[cw, cbias])
                    for k in range(3):
                        stt(ac, ac[:, :], xin, xin[:, k:T + k], cw[:, c, k:k + 1], ac, ac[:, :], ALU.mult, ALU.add, extra_reads=[cw])
                    act(ac, ac[:, :], ac, ac[:, :], AF.Silu)
                    S.dma("pool", xbcA_scr[c * 128:(c + 1) * 128, :], ac[:, :], reads=[ac], writes=[xbcA_scr])
            S.mute = STG < 3
            with S.scope():
                ST = S.sbuf("ST", [128, 16, 128])
                xAs = Rot([S.sbuf(f"xA{a}", [128, 16, 256]) for a in range(1)])
                BCs = Rot([S.sbuf(f"BC{a}", [128, 8, 256]) for a in range(1)])
                xtok = S.sbuf("xtok", [128, 2, 2048]); Btok = S.sbuf("Btok", [128, 2, 512])
                xdt = S.sbuf("xdt", [128, 2, 2048])
                nac = S.sbuf("nac", [128, 2, 32])
                cbm = S.sbuf("cbm", [128, 4, 2, 256])
                trifull = S.sbuf("trifull", [128, 2, 256]); tri = S.sbuf("tri", [128, 128]); causal = S.sbuf("causal", [128, 2, 256])
                Erows = Rot([S.sbuf(f"Erow{a}", [128, 256]) for a in range(2)])
                decs = Rot([S.sbuf(f"dec{a}", [128, 256]) for a in range(3)])
                MTs = Rot([S.sbuf(f"MT{a}", [128, 256]) for a in range(3)])
                xdtes = Rot([S.sbuf(f"xdte{a}", [128, 2, 128]) for a in range(2)])
                yts = Rot([S.sbuf(f"yt{a}", [128, 256]) for a in range(2)])
                t1s = Rot([S.sbuf(f"t1{a}", [128, 256]) for a in range(2)])
                cds = Rot([S.sbuf(f"cd{a}", [128, 2]) for a in range(2)])
                S.dma("sp", trifull[:], I["c_trifull"], writes=[trifull])
                S.dma("sp", tri[:], I["c_tri"], writes=[tri])
                S.dma("sp", causal[:], I["c_causal01"], writes=[causal])
                SP = int(os.environ.get("SCAN_PART", "9"))
                for c in range(cfg.NCH):
                    csl = slice(c * 256, (c + 1) * 256)
                    xA = xAs.next()
                    BC = BCs.next()
                    S.dma("sp", xA[:], xbcA_scr[0:2048, csl].rearrange("(c p) t -> p c t", p=128), reads=[xbcA_scr], writes=[xA])
                    S.dma("sp", BC[:], xbcA_scr[2048:3072, csl].rearrange("(c p) t -> p c t", p=128), reads=[xbcA_scr], writes=[BC])
                    for st in range(2):
                        for q4 in range(4):
                            ps = S.ps()
                            for a in range(4):
                                ch = q4 * 4 + a
                                tr(ps, ps[:, a * 128:(a + 1) * 128], xA, xA[:, ch, st * 128:(st + 1) * 128], 128)
                            cp("act" if q4 % 2 == 0 else "dve", xtok, xtok[:, st, q4 * 512:(q4 + 1) * 512], ps, ps[:, :])
                        ps = S.ps()
                        for a in range(4):
                            tr(ps, ps[:, a * 128:(a + 1) * 128], BC, BC[:, a, st * 128:(st + 1) * 128], 128)
                        cp("act", Btok, Btok[:, st, :], ps, ps[:, :])
                    if SP < 2:
                        continue
                    ps = S.ps()
                    mm(ps, ps[:, 0:32], tri, tri[:, :], dta, dta[:, 2 * c, :], start=True, stop=True)
                    mm(ps, ps[:, 32:64], ones, ones[:, :], dta, dta[:, 2 * c, :], start=True, stop=False)
                    mm(ps, ps[:, 32:64], tri, tri[:, :], dta, dta[:, 2 * c + 1, :], start=False, stop=True)
                    ts_op("dve", nac, nac[:, :, :], ps, ps[:, 0:64].rearrange("p (a b) -> p a b", a=2), -1.0, None, ALU.mult)
                    for st in range(2):
                        tt_op("dve", xdt, xdt[:, st, :].rearrange("p (h d) -> p h d", d=64), xtok, xtok[:, st, :].rearrange("p (h d) -> p h d", d=64),
                              dt_tok, dt_tok[:, 2 * c + st, :].unsqueeze(2).to_broadcast([128, 32, 64]), ALU.mult)
                    for g in range(4):
                        for st in range(2):
                            ps = S.ps()
                            mm(ps, ps[:, 0:256], BC, BC[:, g, st * 128:(st + 1) * 128], BC, BC[:, 4 + g, :])
                            tt_op("dve", cbm, cbm[:, g, st, :], ps, ps[:, 0:256], causal, causal[:, st, :], ALU.mult)
                    SUB = int(os.environ.get("SCAN_SUB", "9"))
                    HPL = int(os.environ.get("SCAN_HP", "16"))
                    for hp in range(HPL if SP >= 3 else 0):
                        g = hp // 4
                        psR = []
                        for x in range(2):
                            hx = 2 * hp + x
                            ps = S.ps()
                            for jt in range(2):
                                mm(ps, ps[:, 0:256], dta, dta[:, 2 * c + jt, hx:hx + 1].to_broadcast([128, 128]), trifull, trifull[:, jt, :],
                                   start=(jt == 0), stop=(jt == 1))
                            psR.append(ps)
                        Erow = Erows.next()
                        act(Erow, Erow[0:64, :], psR[0], psR[0][0:64, 0:256], AF.Exp)
                        act(Erow, Erow[64:128, :], psR[1], psR[1][64:128, 0:256], AF.Exp)
                        cd = cds.next()
                        for x in range(2):
                            act(cd, cd[:, x:x + 1], psR[x], psR[x][:, 255:256], AF.Exp)
                        if SUB < 2:
                            continue
                        xdte = xdtes.next()
                        psY = [S.ps_acc(), S.ps_acc()]
                        for x in range(2):
                            hx = 2 * hp + x
                            for st in range(2):
                                dec = decs.next()
                                ts_op("dve", dec, dec[:, :], psR[x], psR[x][:, 0:256], nac[:, st, hx:hx + 1], 0.0, ALU.add, ALU.min, extra_reads=[nac])
                                act(dec, dec[:, :], dec, dec[:, :], AF.Exp)
                                MT = MTs.next()
                                tt_op("pool", MT, MT[:, :], dec, dec[:, :], cbm, cbm[:, g, st, :], ALU.mult)
                                mm(psY[x], psY[x][0:64, 0:256], xdt, xdt[:, st, hx * 64:(hx + 1) * 64], MT, MT[:, :], start=(st == 0), stop=(st == 1))
                                ts_op("dve", xdte, xdte[:, st, x * 64:(x + 1) * 64], xdt, xdt[:, st, hx * 64:(hx + 1) * 64], dec[:, 255:256], None, ALU.mult,
                                      extra_reads=[dec])
                        if SUB < 3:
                            continue
                        psS = S.ps()
                        for st in range(2):
                            mm(psS, psS[:, 0:128], Btok, Btok[:, st, g * 128:(g + 1) * 128], xdte, xdte[:, st, :], start=(st == 0), stop=(st == 1))
                        if SUB < 4:
                            continue
                        yt = yts.next()
                        cp("act", yt, yt[0:64, :], psY[0], psY[0][0:64, 0:256])
                        cp("act", yt, yt[64:128, :], psY[1], psY[1][0:64, 0:256])
                        if c > 0 and SUB >= 5:
                            psF = S.ps()
                            mm(psF, psF[:, 0:256], ST, ST[:, hp, :], BC, BC[:, 4 + g, :])
                            t1 = t1s.next()
                            tt_op("dve", t1, t1[:, :], psF, psF[:, 0:256], Erow, Erow[:, :], ALU.mult)
                            tt_op("pool", yt, yt[:, :], yt, yt[:, :], t1, t1[:, :], ALU.add)
                        stt(yt, yt[:, :], xA, xA[:, hp, :], dsk[:, hp:hp + 1], yt, yt[:, :], ALU.mult, ALU.add, extra_reads=[dsk])
                        S.dma("pool", y_scr[hp * 128:(hp + 1) * 128, csl], yt[:, :], reads=[yt], writes=[y_scr])
                        if SUB < 6:
                            continue
                        if c == 0:
                            cp("dve", ST, ST[:, hp, :], psS, psS[:, 0:128])
                        else:
                            for x in range(2):
                                stt(ST, ST[:, hp, x * 64:(x + 1) * 64], ST, ST[:, hp, x * 64:(x + 1) * 64], cd[:, x:x + 1],
                                    psS, psS[:, x * 64:(x + 1) * 64], ALU.mult, ALU.add, extra_reads=[cd])
                if SP >= 4:
                    S.dma("pool", O[f"ssmp{j}"][:, :, :].rearrange("h n p -> n h p"), ST[:, :, :], reads=[ST], writes=[O[f"ssmp{j}"]])
            S.mute = STG < 4
            with S.scope():
                cvb = S.sbuf("cvb", [128, 24, 16, 3]); cvo = S.sbuf("cvo", [128, 24, 16, 3])
                xa = S.sbuf("xa", [128, 24, 16]); tmpx = S.sbuf("tmpx", [128, 24, 16])
                tmps_ = S.sbuf("tmps", [32, 16]); dtas = S.sbuf("dtas", [32, 16])
                ehp = S.sbuf("ehp", [32, 16, 128]); selb = S.sbuf("selb", [16, 16, 128])
                dtx = S.sbuf("dtx", [128, 16, 16]); cdx = S.sbuf("cdx", [128, 16, 16]); xdts = S.sbuf("xdts", [128, 16, 16])
                BCtok = S.sbuf("BCtok", [16, 1024])
                Bbs = Rot([S.sbuf(f"Bb{a}", [128, 4, 128]) for a in range(2)])
                Cbs = Rot([S.sbuf(f"Cb{a}", [128, 4, 128]) for a in range(2)])
                sts = Rot([S.sbuf(f"st{a}", [128, 16, 128]) for a in range(2)])
                t1b = Rot([S.sbuf(f"t1b{a}", [128, 16, 128]) for a in range(2)])
                ys = S.sbuf("ys", [128, 16, 16])
                S.dma("sp", cvb[:], I["sconvT"][j].rearrange("(c p) b k -> p c b k", p=128), writes=[cvb])
                S.dma("sp", ehp[:], I["c_ehp"], writes=[ehp])
                S.dma("sp", selb[:], I["c_selb"], writes=[selb])
                tt_op("dve", xa, xa[:], xbcs, xbcs[:], cw, cw[:, :, 3:4].to_broadcast([128, 24, 16]), ALU.mult)
                tt_op("dve", xa, xa[:], xa, xa[:], cbias, cbias[:, :].unsqueeze(2).to_broadcast([128, 24, 16]), ALU.add)
                for k in range(3):
                    tt_op("dve", tmpx, tmpx[:], cvb, cvb[:, :, :, k], cw, cw[:, :, k:k + 1].to_broadcast([128, 24, 16]), ALU.mult)
                    tt_op("dve", xa, xa[:], xa, xa[:], tmpx, tmpx[:], ALU.add)
                act(xa, xa[:], xa, xa[:], AF.Silu)
                cp("pool", cvo, cvo[:, :, :, 0], cvb, cvb[:, :, :, 1])
                cp("pool", cvo, cvo[:, :, :, 1], cvb, cvb[:, :, :, 2])
                cp("pool", cvo, cvo[:, :, :, 2], xbcs, xbcs[:])
                S.dma("pool", O[f"convs{j}"][:, :, :].rearrange("(c p) b k -> p c b k", p=128), cvo[:], reads=[cvo], writes=[O[f"convs{j}"]])
                softplus(dtsT, dtsT[:, :], tmps_, tmps_[:, :])
                ts_op("dve", dtas, dtas[:, :], dtsT, dtsT[:, :], anegT[:, 0:1], None, ALU.mult, extra_reads=[anegT])
                ps = S.ps()
                for hp in range(16):
                    mm(ps, ps[:, hp * 16:(hp + 1) * 16], ehp, ehp[:, hp, :], dtsT, dtsT[:, :])
                cp("dve", dtx, dtx[:], ps, ps[:, 0:256].rearrange("p (a b) -> p a b", b=16))
                ps = S.ps()
                for hp in range(16):
                    mm(ps, ps[:, hp * 16:(hp + 1) * 16], ehp, ehp[:, hp, :], dtas, dtas[:, :])
                act(cdx, cdx[:], ps, ps[:, 0:256].rearrange("p (a b) -> p a b", b=16), AF.Exp)
                tt_op("dve", xdts, xdts[:], xa, xa[:, 0:16, :], dtx, dtx[:], ALU.mult)
                for half in range(2):
                    ps = S.ps()
                    for a in range(4):
                        tr(ps, ps[0:16, a * 128:(a + 1) * 128], xa, xa[:, 16 + half * 4 + a, :], 128)
                    cp("dve", BCtok, BCtok[:, half * 512:(half + 1) * 512], ps, ps[0:16, :])
                sso = O[f"ssms{j}"]
                for b in range(16):
                    ps0 = S.ps()
                    mm(ps0, ps0[:, :], selb, selb[:, b, :], BCtok, BCtok[:, 0:512])
                    ps1 = S.ps()
                    mm(ps1, ps1[:, :], selb, selb[:, b, :], BCtok, BCtok[:, 512:1024])
                    Bb = Bbs.next(); Cb = Cbs.next()
                    cp("act", Bb, Bb[:, :, :].rearrange("p a b -> p (a b)"), ps0, ps0[:, :])
                    cp("act", Cb, Cb[:, :, :].rearrange("p a b -> p (a b)"), ps1, ps1[:, :])
                    st_ = sts.next()
                    S.dma("sp", st_[:], I["sssm"][j, b].rearrange("(c p) n -> p c n", p=128), writes=[st_])
                    t1 = t1b.next()
                    for g in range(4):
                        tt_op("pool", t1, t1[:, 4 * g:4 * g + 4, :], Bb, Bb[:, g:g + 1, :].to_broadcast([128, 4, 128]),
                              xdts, xdts[:, 4 * g:4 * g + 4, b:b + 1].to_broadcast([128, 4, 128]), ALU.mult)
                    tt_op("dve", st_, st_[:], st_, st_[:], cdx, cdx[:, :, b:b + 1].to_broadcast([128, 16, 128]), ALU.mult)
                    tt_op("dve", st_, st_[:], st_, st_[:], t1, t1[:], ALU.add)
                    S.dma("pool", sso[b].rearrange("(c p) n -> p c n", p=128), st_[:], reads=[st_], writes=[sso])
                    for g in range(4):
                        tt_op("pool", t1, t1[:, 4 * g:4 * g + 4, :], st_, st_[:, 4 * g:4 * g + 4, :],
                              Cb, Cb[:, g:g + 1, :].to_broadcast([128, 4, 128]), ALU.mult)
                    S.op("dve", lambda e, t1=t1, b=b: e.tensor_reduce(out=ys[:, :, b], in_=t1[:, :, :], axis=AX.X, op=ALU.add), reads=[t1], writes=[ys])
                tt_op("dve", tmpx, tmpx[:, 0:16, :], xa, xa[:, 0:16, :], dsk, dsk[:, :].unsqueeze(2).to_broadcast([128, 16, 16]), ALU.mult)
                tt_op("dve", ys, ys[:], ys, ys[:], tmpx, tmpx[:, 0:16, :], ALU.add)
                S.dma("pool", y_scr[:, T:TT].rearrange("(c p) b -> p c b", p=128), ys[:], reads=[ys], writes=[y_scr])
                S.dma("pool", z_scr[:, T:TT].rearrange("(c p) b -> p c b", p=128), zs[:], reads=[zs], writes=[z_scr])
            S.mute = STG < 5
            with S.scope():
                wts = Rot([S.sbuf(f"wt{a}", [128, 8, 512], BF16) for a in range(4)])
                yb = S.sbuf("yb", [128, 16, 512]); zb = S.sbuf("zb", [128, 16, 512])
                ybh = S.sbuf("ybh", [128, 16, 512], BF16)
                sq_rot = Rot([S.sbuf(f"sq{a}", [128, 512]) for a in range(2)])
                r_b = S.sbuf("r", [128, 512])
                for tt in range(NQT + 1):
                    N = ntok(tt)
                    S.dma("sp", yb[:, :, 0:N], y_scr[:, tsl(tt)].rearrange("(c p) t -> p c t", p=128), reads=[y_scr], writes=[yb])
                    S.dma("sp", zb[:, :, 0:N], z_scr[:, tsl(tt)].rearrange("(c p) t -> p c t", p=128), reads=[z_scr], writes=[zb])
                    act(zb, zb[:, :, 0:N], zb, zb[:, :, 0:N], AF.Silu)
                    tt_op("dve", yb, yb[:, :, 0:N], yb, yb[:, :, 0:N], zb, zb[:, :, 0:N], ALU.mult)
                    rms_rinv(yb, lambda k, N=N: yb[:, k, 0:N], 16, N, sq_rot, r_b, 2048.0)
                    for k in range(16):
                        stt(ybh, ybh[:, k, 0:N], yb, yb[:, k, 0:N], snw[:, k:k + 1], r_b, r_b[:, 0:N], ALU.mult, ALU.mult, extra_reads=[snw])
                    out_proj(I["ssd_w_out"][j], 16, lambda k, N=N: (ybh, ybh[:, k, 0:N]), tt, wts, 16)
            S.mute = False

    import os
    DBG = os.environ.get("KDBG", "")
    for i in range(NL):
        adaln(i)
        if i % 2 == 0:
            if "nonsa" not in DBG:
                nsa_layer(i, i // 2)
        else:
            ssd_layer(i, i // 2)
        if "nomlp" not in DBG:
            mlp(i)
    with S.scope():
        sq_rot = Rot([S.sbuf(f"sq{a}", [128, 512]) for a in range(2)])
        r_b = S.sbuf("r", [128, 512])
        yo = Rot([S.sbuf(f"yo{a}", [128, 8, 512]) for a in range(2)])
        for tt in range(NQT + 1):
            N = ntok(tt)
            x = xt[tt]
            rms_rinv(x, lambda k, x=x: x[:, k, :], 8, N, sq_rot, r_b, 1024.0)
            y = yo.next()
            for k in range(8):
                stt(y, y[:, k, 0:N], x, x[:, k, :], fw[:, k:k + 1], r_b, r_b[:, 0:N], ALU.mult, ALU.mult, extra_reads=[fw])
            S.dma("pool", O["yT"][:, tsl(tt)].rearrange("(c p) t -> p c t", p=128), y[:, :, 0:N], reads=[y], writes=[O["yT"]])
    S.finish("sp")
    S.emit()
    root.close()
    return nc, consts, S.n_ops


def fm(v, nch):
    return np.ascontiguousarray(np.asarray(v, np.float32).reshape(nch, 128).T)


def prep_shared(inp, cfg, consts):
    NL, NN, NS = cfg.NL, cfg.NN, cfg.NS
    f32 = np.float32
    sh = {}
    sh["ada_w"] = np.ascontiguousarray(inp["ada_w"], f32)
    sh["ada_bT"] = np.stack([fm(inp["ada_b"][l], 48) for l in range(NL)])
    sh["norm_wT"] = np.stack([np.concatenate([fm(inp["norm_w"][l, 0], 8), fm(inp["norm_w"][l, 1], 8)], axis=1) for l in range(NL)])
    sh["mlp_w1"] = np.ascontiguousarray(inp["mlp_w1"], f32)
    sh["mlp_w2"] = np.ascontiguousarray(inp["mlp_w2"], f32)
    sh["final_wT"] = fm(inp["final_norm_w"], 8)
    if NN:
        npool = inp["cache_kv_cmp"].shape[1]
        sh["pool_c"] = np.ascontiguousarray(inp["cache_kv_cmp"], f32).reshape(NN, npool * 128, 512)
        sh["pool_s"] = np.ascontiguousarray(inp["cache_kv_sel"], f32).reshape(NN, npool * 128, 512)
        sh["nsa_w_in"] = np.ascontiguousarray(inp["nsa_w_in"], f32)
        sh["nsa_w_out"] = np.ascontiguousarray(inp["nsa_w_out"], f32)
        pe = np.asarray(inp["nsa_cmp_pe"], f32)
        sh["cmp_pe2"] = np.ascontiguousarray(pe.reshape(NN, 2, 16, 2, 64).transpose(0, 3, 4, 1, 2).reshape(NN, 128, 2, 16))
        w1 = np.asarray(inp["nsa_cmp_w1"], f32)
        sh["cmp_w1std"] = np.ascontiguousarray(w1.reshape(NN, 2, 16, 128, 64).transpose(0, 3, 1, 2, 4))
        w1d = w1.reshape(NN, 2, 32, 64, 64).transpose(0, 3, 1, 2, 4)
        sh["cmp_w1d"] = np.ascontiguousarray(np.concatenate([w1d, w1d], axis=1))
        sh["cmp_w2"] = np.ascontiguousarray(np.asarray(inp["nsa_cmp_w2"], f32).transpose(0, 2, 1, 3))
    if NS:
        sh["ssd_w_in"] = np.ascontiguousarray(inp["ssd_w_in"], f32)
        cw = np.asarray(inp["ssd_conv_w"], f32)
        sh["ssd_conv_wT"] = np.ascontiguousarray(cw.reshape(NS, 4, 24, 128).transpose(0, 3, 2, 1))
        sh["ssd_conv_bT"] = np.stack([fm(inp["ssd_conv_b"][l], 24) for l in range(NS)])
        sh["ssd_dtb"] = np.ascontiguousarray(np.asarray(inp["ssd_dt_bias"], f32).reshape(NS, 1, 32))
        sh["ssd_alog"] = np.ascontiguousarray(np.asarray(inp["ssd_a_log"], f32).reshape(NS, 1, 32))
        sh["ssd_dtbT"] = np.ascontiguousarray(np.asarray(inp["ssd_dt_bias"], f32).reshape(NS, 32, 1))
        sh["ssd_alogT"] = np.ascontiguousarray(np.asarray(inp["ssd_a_log"], f32).reshape(NS, 32, 1))
        sh["ssd_dT"] = np.stack([fm(np.repeat(np.asarray(inp["ssd_d"][l], f32), 64), 16) for l in range(NS)])
        sh["ssd_norm_wT"] = np.stack([fm(inp["ssd_norm_w"][l], 16) for l in range(NS)])
        sh["ssd_w_out"] = np.ascontiguousarray(inp["ssd_w_out"], f32)
    sh.update(consts)
    return sh


def prep_core(inp, cfg, c):
    f32 = np.float32
    NN, NS = cfg.NN, cfg.NS
    bs = slice(16 * c, 16 * c + 16)
    m = {}
    m["xT"] = np.ascontiguousarray(np.asarray(inp["x_prompt"][c], f32).T)
    m["xsT"] = np.ascontiguousarray(np.asarray(inp["x_sample"][bs, 0], f32).T)
    cc = np.concatenate([np.asarray(inp["c_prompt"][c:c + 1], f32), np.asarray(inp["c_sample"][bs], f32)], axis=0)
    m["cT"] = np.ascontiguousarray(cc.T)
    m["pt"] = np.ascontiguousarray(inp["page_table"][bs], np.int32)
    if NN:
        m["cwin"] = np.ascontiguousarray(np.asarray(inp["cache_kv_win"], f32)[:, bs]).reshape(NN, 16, 512, 512)
    if NS:
        m["sssm"] = np.ascontiguousarray(np.asarray(inp["state_ssm"], f32)[:, bs]).reshape(NS, 16, 2048, 128)
        sc = np.asarray(inp["state_conv"], f32)[:, bs]
        m["sconvT"] = np.ascontiguousarray(sc.transpose(0, 3, 1, 2))
    return m


def assemble(res, cfg, ncores):
    T, NN, NS = cfg.T, cfg.NN, cfg.NS
    f32 = np.float32
    B = ncores
    y_p = np.stack([res[c]["yT"][:, :T].T for c in range(B)]).astype(f32)
    y_s = np.concatenate([res[c]["yT"][:, T:].T for c in range(B)])[:, None, :].astype(f32)

    def kv_p(cn):
        return np.stack([np.stack([res[c][f"kvp_{cn}{j}"].T.reshape(T, 2, 4, 64) for c in range(B)]) for j in range(NN)])

    def kv_s(cn):
        return np.stack([np.concatenate([res[c][f"kvs_{cn}{j}"].T.reshape(16, 1, 2, 4, 64) for c in range(B)]) for j in range(NN)])

    wk = min(512, T)
    win_p = np.stack([np.stack([res[c][f"kvwin_p{j}"].T.reshape(wk, 2, 4, 64) for c in range(B)]) for j in range(NN)])
    win_s = np.stack([np.concatenate([np.concatenate([res[c][f"kvwin_old{j}"].reshape(16, 511, 2, 4, 64),
                                                      res[c][f"kvs_w{j}"].T.reshape(16, 1, 2, 4, 64)], axis=1)
                                      for c in range(B)]) for j in range(NN)])
    outs = [y_p, y_s, kv_p("c"), kv_s("c"), kv_p("s"), kv_s("s"), win_p, win_s]
    if NS:
        ssm_p = np.stack([np.stack([res[c][f"ssmp{j}"].reshape(16, 128, 2, 64).transpose(0, 2, 3, 1).reshape(32, 64, 128)
                                    for c in range(B)]) for j in range(NS)])
        ssm_s = np.stack([np.concatenate([res[c][f"ssms{j}"].reshape(16, 32, 64, 128) for c in range(B)]) for j in range(NS)])
        conv_p = np.stack([np.stack([res[c][f"convp{j}"].T for c in range(B)]) for j in range(NS)])
        conv_s = np.stack([np.concatenate([res[c][f"convs{j}"].transpose(1, 2, 0) for c in range(B)]) for j in range(NS)])
        outs += [ssm_p, ssm_s, conv_p, conv_s]
    return tuple(np.ascontiguousarray(o, dtype=f32) for o in outs)


def run(inp, cfg, ncores):
    nc, consts, _ = build(cfg)
    sh = prep_shared(inp, cfg, consts)
    in_maps = []
    for c in range(ncores):
        m = dict(sh)
        m.update(prep_core(inp, cfg, c))
        in_maps.append(m)
    res = run_bass_kernel_spmd(nc, in_maps, core_ids=list(range(ncores)))
    return assemble(res.results, cfg, ncores)


def kernel(**inputs):
    cfg = Cfg(T=2048, NL=4, NPOOL=int(inputs["cache_kv_cmp"].shape[1]))
    return run(inputs, cfg, 8)
```

```python
import contextlib
import math
import numpy as np
import concourse.bass as bass
import concourse.mybir as mybir
from concourse.bass_utils import run_bass_kernel_spmd

F32 = mybir.dt.float32
BF16 = mybir.dt.bfloat16
I32 = mybir.dt.int32
U32 = mybir.dt.uint32
AF = mybir.ActivationFunctionType
ALU = mybir.AluOpType
AX = mybir.AxisListType

EPOCH = 40000
NDMASEM = 8
NEGB = -1.0e5
SCALE = 0.125
EPS = 1e-6
BIGV = 1e30


class Buf:
    __slots__ = ("t", "name", "w", "rc", "rd", "const", "nowaw", "ws", "psum")

    def __init__(self, t, name):
        self.t = t
        self.name = name
        self.w = None
        self.rc = {}
        self.rd = []
        self.const = False
        self.psum = False
        self.nowaw = False
        self.ws = []

    def __getitem__(self, idx):
        return self.t[idx]


class Q:
    def __init__(self, name):
        self.name = name
        self.ops = []
        self.sems = []
        self.count = 0
        self.known = {}
        self.pending = []
        self.dsems = []
        self.dcount = 0
        self.dlast = {}

    def cur_dep(self):
        if not self.sems or self.count == 0:
            return None
        return (self.name, self.sems[-1], self.count, False)


class Sched:
    def __init__(self, nc, root):
        self.nc = nc
        self.root = root
        self.stack = root
        self.q = {n: Q(n) for n in ("pe", "dve", "act", "pool", "sp")}
        self.bufs = []
        self.n_ops = 0
        self.psums = []
        self.pi = 0
        self.mute = False
        self.pa = 0
        self.uid = 0

    def sem(self, name):
        return self.root.enter_context(self.nc.semaphore(name))

    def sbuf(self, name, shape, dt=F32):
        self.uid += 1
        t = self.stack.enter_context(self.nc.sbuf_tensor(f"{name}_{self.uid}", list(shape), dt))
        b = Buf(t, name)
        self.bufs.append(b)
        return b

    def view(self, ap, name):
        b = Buf(ap, name)
        self.bufs.append(b)
        return b

    def dram(self, name, shape, dt=F32, kind="Internal"):
        t = self.nc.dram_tensor(name, list(shape), dt, kind=kind)
        b = Buf(t.ap(), name)
        b.nowaw = True
        self.bufs.append(b)
        return b

    def init_psum(self):
        for i in range(8):
            t = self.root.enter_context(self.nc.psum_tensor(f"psum{i}", [128, 512], F32))
            b = Buf(t, f"psum{i}")
            b.psum = True
            self.psums.append(b)
            self.bufs.append(b)

    def ps(self):
        b = self.psums[self.pi % 6]
        self.pi += 1
        return b

    def ps_acc(self):
        b = self.psums[6 + self.pa % 2]
        self.pa += 1
        return b

    @contextlib.contextmanager
    def scope(self):
        st = contextlib.ExitStack()
        prev = self.stack
        self.stack = st
        nb = len(self.bufs)
        try:
            yield self
        finally:
            self.barrier()
            del self.bufs[nb:]
            st.close()
            self.stack = prev

    def _need(self, q, deps):
        waits = []
        for d in deps:
            if d is None:
                continue
            qn, sem, val, is_dma = d
            if qn == q.name and not is_dma and q.name == "pe":
                continue
            k = id(sem)
            if q.known.get(k, 0) >= val:
                continue
            q.known[k] = val
            waits.append((sem, val))
        return waits

    def _gather(self, reads, writes, qn=None):
        deps = []
        for b in reads:
            if b.const:
                continue
            if b.nowaw:
                deps.extend(b.ws)
            else:
                deps.append(b.w)
            if b.psum:
                deps.extend(d for k, d in b.rc.items() if k != qn)
        for b in writes:
            if not b.nowaw:
                deps.append(b.w)
            deps.extend(b.rc.values())
            deps.extend(b.rd)
        return deps

    def _commit(self, dep, reads, writes, is_dma):
        for b in reads:
            if b.const:
                continue
            if is_dma:
                b.rd.append(dep)
            else:
                b.rc[dep[0]] = dep
        for b in writes:
            if b.nowaw:
                b.ws.append(dep)
            else:
                b.w = dep
            b.rc = {}
            b.rd = []

    def op(self, qn, fn, reads=(), writes=()):
        if self.mute:
            return None
        q = self.q[qn]
        deps = self._gather(reads, writes, qn)
        waits = q.pending + self._need(q, deps)
        q.pending = []
        if not q.sems or q.count >= EPOCH:
            q.sems.append(self.sem(f"{qn}_e{len(q.sems)}"))
            q.count = 0
        q.count += 1
        sem = q.sems[-1]
        dep = (qn, sem, q.count, False)
        q.ops.append((waits, fn, (sem, 1)))
        self._commit(dep, reads, writes, False)
        self.n_ops += 1
        return dep

    def dma(self, qn, out, in_, reads=(), writes=(), fn=None):
        if self.mute:
            return None
        q = self.q[qn]
        deps = self._gather(reads, writes)
        if not q.dsems:
            q.dsems = [self.sem(f"{qn}_d{i}") for i in range(NDMASEM)]
        slot = q.dcount % NDMASEM
        sem = q.dsems[slot]
        val = 16 * (q.dcount // NDMASEM + 1)
        prev = q.dlast.get(slot)
        if prev is not None:
            deps.append(prev)
        waits = q.pending + self._need(q, deps)
        q.pending = []
        q.dcount += 1
        dep = (qn, sem, val, True)
        q.dlast[slot] = dep
        if fn is None:
            def fn(e, out=out, in_=in_):
                return e.dma_start(out=out, in_=in_)
        q.ops.append((waits, fn, (sem, 16)))
        self._commit(dep, reads, writes, True)
        self.n_ops += 1
        return dep

    def all_deps(self):
        deps = []
        for q in self.q.values():
            deps.append(q.cur_dep())
            deps.extend(q.dlast.values())
        return deps

    def barrier(self):
        deps = self.all_deps()
        for q in self.q.values():
            q.pending = q.pending + self._need(q, deps)
        for b in self.bufs:
            b.w = None
            b.ws = []
            b.rc = {}
            b.rd = []

    def finish(self, qn="sp"):
        q = self.q[qn]
        waits = q.pending + self._need(q, self.all_deps())
        q.pending = []
        q.ops.append((waits, None, None))

    def emit(self):
        nc = self.nc
        with nc.Block() as block:
            def run(q):
                def body(e):
                    for waits, fn, inc in q.ops:
                        for sem, val in waits:
                            e.wait_ge(sem, val)
                        if fn is not None:
                            ins = fn(e)
                            ins.then_inc(inc[0], inc[1])
                    for sem, val in q.pending:
                        e.wait_ge(sem, val)
                return body
            block.sync(run(self.q["sp"]))
            block.tensor(run(self.q["pe"]))
            block.vector(run(self.q["dve"]))
            block.scalar(run(self.q["act"]))
            block.gpsimd(run(self.q["pool"]))


class Rot:
    def __init__(self, items):
        self.items = items
        self.i = 0

    def next(self):
        b = self.items[self.i % len(self.items)]
        self.i += 1
        return b


class Cfg:
    def __init__(self, T=2048, NL=4, NPOOL=2560, dbg=False):
        self.T = T
        self.NL = NL
        self.NN = (NL + 1) // 2
        self.NS = NL // 2
        self.NPOOL = NPOOL
        self.TT = T + 16
        self.NQT = T // 512
        self.NKC = T // 128
        self.NCB = (T - 32) // 16 + 1
        self.NSEL = T // 64
        self.NCH = T // 256
        self.dbg = dbg


def make_consts(cfg):
    T = cfg.T
    c = {}
    c["c_ident"] = np.eye(128, dtype=np.float32)
    c["c_ones"] = np.ones((128, 128), np.float32)
    i = np.arange(128)[:, None]
    t = np.arange(T)[None, :]
    cmpb = np.where((16 * i + 31 <= t) & (i < cfg.NCB), 0.0, NEGB).astype(np.float32)
    c["c_cmpb"] = cmpb
    p = np.arange(128)[:, None]
    f = np.arange(1024)[None, :]
    c["c_cb"] = np.where(f - p >= 512, 0.0, NEGB).astype(np.float32)
    f = np.arange(896)[None, :]
    c["c_wb"] = np.where(f - p <= 384, 0.0, NEGB).astype(np.float32)
    e = np.zeros((32, cfg.NKC, 128), np.float32)
    for kc in range(cfg.NKC):
        for pp in range(128):
            j = 2 * kc + pp // 64
            if j < 32:
                e[j, kc, pp] = 1.0
    c["c_eall"] = e
    ovl = np.zeros((128, 32), np.float32)
    cs = np.arange(cfg.NCB) * 16
    ss = np.arange(cfg.NSEL) * 64
    o = (cs[:, None] < ss[None, :] + 64) & (cs[:, None] + 32 > ss[None, :])
    ovl[:cfg.NCB, :cfg.NSEL] = o
    c["c_ovl"] = ovl
    ovs = np.zeros((128, 32), np.float32)
    cs = np.arange(127) * 16
    ss = np.arange(32) * 64
    ovs[:127, :] = (cs[:, None] < ss[None, :] + 64) & (cs[:, None] + 32 > ss[None, :])
    c["c_ovs"] = ovs
    nq = T // 128
    tt = (np.arange(nq)[None, :, None] * 128 + np.arange(128)[:, None, None])
    j = np.arange(32)[None, None, :]
    valid = (j < cfg.NSEL) & (j * 64 <= tt)
    cur = tt // 64
    forced = (j == 0) | (j == cur) | (j == cur - 1)
    c["c_m1"] = (valid & ~forced).astype(np.float32)
    c["c_m2"] = np.where(~valid, -BIGV, np.where(forced, BIGV, 0.0)).astype(np.float32)
    c["c_valid"] = valid.astype(np.float32)
    jj = np.arange(128)[:, None]
    ll = np.arange(128)[None, :]
    tri = (jj <= ll).astype(np.float32)
    c["c_tri"] = tri
    tf = np.zeros((128, 2, 256), np.float32)
    tf[:, 0, :128] = tri
    tf[:, 0, 128:] = 1.0
    tf[:, 1, 128:] = tri
    c["c_trifull"] = tf
    ca = np.zeros((128, 2, 256), np.float32)
    l = np.arange(256)[None, :]
    for st in range(2):
        s = st * 128 + np.arange(128)[:, None]
        ca[:, st, :] = (l >= s)
    c["c_causal01"] = ca
    e2 = np.zeros((8, 4, 128), np.float32)
    for m in range(4):
        for pp in range(128):
            e2[2 * m + pp // 64, m, pp] = 1.0
    c["c_e2"] = e2
    c["c_p64"] = (np.arange(128) % 64).astype(np.float32).reshape(128, 1)
    c["c_p128"] = np.arange(128).astype(np.float32).reshape(128, 1)
    c["c_iota32"] = np.tile(np.arange(32, dtype=np.float32)[None, :], (64, 1))
    negh = np.zeros((128, 4), np.float32)
    negh[64:, :] = NEGB
    c["c_negh"] = negh
    selb = np.zeros((16, 16, 128), np.float32)
    for b in range(16):
        selb[b, b, :] = 1.0
    c["c_selb"] = selb
    ehp = np.zeros((32, 16, 128), np.float32)
    for hp in range(16):
        for pp in range(128):
            ehp[2 * hp + pp // 64, hp, pp] = 1.0
    c["c_ehp"] = ehp
    m1s = np.ones((64, 32), np.float32)
    m1s[:, 0] = 0
    m1s[:, 31] = 0
    m2s = np.zeros((64, 32), np.float32)
    m2s[:, 0] = BIGV
    m2s[:, 31] = BIGV
    c["c_m1s"] = m1s
    c["c_m2s"] = m2s
    return c


def build(cfg):
    T, TT, NQT, NKC, NCB, NL, NN, NS = cfg.T, cfg.TT, cfg.NQT, cfg.NKC, cfg.NCB, cfg.NL, cfg.NN, cfg.NS
    NQ128 = T // 128
    NROWS = cfg.NPOOL * 128
    nc = bass.Bass("TRN2", target_bir_lowering=False)
    root = contextlib.ExitStack()
    S = Sched(nc, root)
    S.init_psum()

    I = {}
    O = {}

    def inp(name, shape, dt=F32):
        I[name] = nc.dram_tensor(name, list(shape), dt, kind="ExternalInput").ap()
        return I[name]

    def outp(name, shape, dt=F32):
        b = S.dram(name, shape, dt, kind="ExternalOutput")
        O[name] = b
        return b

    inp("xT", [1024, T]); inp("xsT", [1024, 16]); inp("cT", [1024, 17]); inp("pt", [16, 16], I32)
    inp("ada_w", [NL, 1024, 6144]); inp("ada_bT", [NL, 128, 48]); inp("norm_wT", [NL, 128, 16])
    inp("mlp_w1", [NL, 1024, 4096]); inp("mlp_w2", [NL, 4096, 1024]); inp("final_wT", [128, 8])
    if NN:
        inp("cwin", [NN, 16, 512, 512]); inp("pool_c", [NN, NROWS, 512]); inp("pool_s", [NN, NROWS, 512])
        inp("nsa_w_in", [NN, 1024, 2608]); inp("nsa_w_out", [NN, 1024, 1024])
        inp("cmp_pe2", [NN, 128, 2, 16]); inp("cmp_w1std", [NN, 128, 2, 16, 64]); inp("cmp_w1d", [NN, 128, 2, 32, 64]); inp("cmp_w1bd", [NN, 128, 2, 32, 128])
        inp("cmp_w2", [NN, 64, 2, 64])
    if NS:
        inp("sssm", [NS, 16, 2048, 128]); inp("sconvT", [NS, 3072, 16, 3])
        inp("ssd_w_in", [NS, 1024, 5152]); inp("ssd_conv_wT", [NS, 128, 24, 4]); inp("ssd_conv_bT", [NS, 128, 24])
        inp("ssd_dtb", [NS, 1, 32]); inp("ssd_alog", [NS, 1, 32]); inp("ssd_dT", [NS, 128, 16])
        inp("ssd_dtbT", [NS, 32, 1]); inp("ssd_alogT", [NS, 32, 1])
        inp("ssd_norm_wT", [NS, 128, 16]); inp("ssd_w_out", [NS, 2048, 1024])
    consts = make_consts(cfg)
    for k, v in consts.items():
        inp(k, list(v.shape))

    outp("yT", [1024, TT])
    for j in range(NN):
        for cn in ("c", "s", "w"):
            outp(f"kvp_{cn}{j}", [512, T])
            outp(f"kvs_{cn}{j}", [512, 16])
        outp(f"kvwin_p{j}", [512, min(512, T)])
        outp(f"kvwin_old{j}", [16, 511, 512])
    for j in range(NS):
        outp(f"ssmp{j}", [16, 128, 128])
        outp(f"ssms{j}", [16, 2048, 128])
        outp(f"convp{j}", [3072, 3])
        outp(f"convs{j}", [3072, 16, 3])
    qT_scr = S.dram("qT_scr", [1024, T])
    g_scr = S.dram("g_scr", [48, T])
    vtok_scr = [S.dram(f"vtok_scr{a}", [T, 256], BF16) for a in range(2)]
    oT_scr = S.dram("oT_scr", [1024, T], BF16)
    z_scr = S.dram("z_scr", [2048, TT])
    y_scr = S.dram("y_scr", [2048, TT])
    xbc_scr = S.dram("xbc_scr", [3072, T])
    xbcA_scr = S.dram("xbcA_scr", [3072, T])

    def ntok(tt):
        return 512 if tt < NQT else 16

    def tsl(tt):
        return slice(tt * 512, tt * 512 + ntok(tt))

    xT = S.sbuf("xT", [128, 8, TT])
    xt = [S.view(xT.t[:, :, tsl(tt)], f"xt{tt}") for tt in range(NQT + 1)]
    ident = S.sbuf("ident", [128, 128]); ones = S.sbuf("ones", [128, 128])
    identb = S.sbuf("identb", [128, 128], BF16)
    scT = S.sbuf("scT", [128, 8, 17]); scTb = S.sbuf("scTb", [128, 8, 17], BF16)
    mod = S.sbuf("mod", [128, 48, 17])
    A1 = S.sbuf("A1", [128, 8, 17]); A2 = S.sbuf("A2", [128, 8, 17])
    nw = S.sbuf("nw", [128, 16]); adab = S.sbuf("adab", [128, 48])
    fw = S.sbuf("fw", [128, 8])
    eps_t = S.sbuf("eps_t", [128, 1])

    S.dma("sp", xT[:, :, 0:T], I["xT"].rearrange("(c p) t -> p c t", p=128), writes=[xT])
    S.dma("sp", xT[:, :, T:TT], I["xsT"].rearrange("(c p) t -> p c t", p=128), writes=[xT])
    S.dma("sp", scT[:], I["cT"].rearrange("(c p) t -> p c t", p=128), writes=[scT])
    S.dma("sp", ident[:], I["c_ident"], writes=[ident])
    S.dma("sp", ones[:], I["c_ones"], writes=[ones])
    S.dma("sp", fw[:], I["final_wT"], writes=[fw])
    S.op("dve", lambda e: e.memset(eps_t[:], EPS), writes=[eps_t])
    S.op("act", lambda e: e.activation(out=scT[:], in_=scT[:], func=AF.Silu), reads=[scT], writes=[scT])
    S.op("dve", lambda e: e.tensor_copy(out=scTb[:], in_=scT[:]), reads=[scT], writes=[scTb])
    S.op("dve", lambda e: e.tensor_copy(out=identb[:], in_=ident[:]), reads=[ident], writes=[identb])
    S.barrier()
    ident.const = True
    identb.const = True
    ones.const = True
    scTb.const = True

    def mm(out_b, out_ap, l_b, l_ap, r_b, r_ap, start=True, stop=True):
        S.op("pe", lambda e: e.matmul(out_ap, lhsT=l_ap, rhs=r_ap, start=start, stop=stop),
             reads=[l_b, r_b], writes=[out_b])

    def tr(out_b, out_ap, in_b, in_ap, np_in):
        S.op("pe", lambda e: e.transpose(out=out_ap, in_=in_ap, identity=ident[0:np_in, 0:np_in]),
             reads=[in_b], writes=[out_b])

    def act(out_b, out_ap, in_b, in_ap, func, bias=None, scale=1.0, extra_reads=()):
        if bias is None:
            S.op("act", lambda e: e.activation(out=out_ap, in_=in_ap, func=func, scale=scale),
                 reads=[in_b, *extra_reads], writes=[out_b])
        else:
            S.op("act", lambda e: e.activation(out=out_ap, in_=in_ap, func=func, bias=bias, scale=scale),
                 reads=[in_b, *extra_reads], writes=[out_b])

    def tt_op(eng, out_b, out_ap, a_b, a_ap, b_b, b_ap, op):
        S.op(eng, lambda e: e.tensor_tensor(out=out_ap, in0=a_ap, in1=b_ap, op=op), reads=[a_b, b_b], writes=[out_b])

    def ts_op(eng, out_b, out_ap, a_b, a_ap, s1, s2, op0, op1=None, extra_reads=()):
        if op1 is None:
            S.op(eng, lambda e: e.tensor_scalar(out=out_ap, in0=a_ap, scalar1=s1, scalar2=None, op0=op0),
                 reads=[a_b, *extra_reads], writes=[out_b])
        else:
            S.op(eng, lambda e: e.tensor_scalar(out=out_ap, in0=a_ap, scalar1=s1, scalar2=s2, op0=op0, op1=op1),
                 reads=[a_b, *extra_reads], writes=[out_b])

    def stt(out_b, out_ap, a_b, a_ap, scalar, b_b, b_ap, op0, op1, extra_reads=(), accum=None):
        if accum is None:
            S.op("dve", lambda e: e.scalar_tensor_tensor(out=out_ap, in0=a_ap, scalar=scalar, in1=b_ap, op0=op0, op1=op1),
                 reads=[a_b, b_b, *extra_reads], writes=[out_b])
        else:
            ab, aap = accum
            S.op("dve", lambda e: e.scalar_tensor_tensor(out=out_ap, in0=a_ap, scalar=scalar, in1=b_ap, op0=op0, op1=op1,
                                                         accum_out=aap),
                 reads=[a_b, b_b, *extra_reads], writes=[out_b, ab])

    def cp(eng, out_b, out_ap, in_b, in_ap):
        if eng == "act":
            S.op("act", lambda e: e.copy(out=out_ap, in_=in_ap), reads=[in_b], writes=[out_b])
        else:
            S.op(eng, lambda e: e.tensor_copy(out=out_ap, in_=in_ap), reads=[in_b], writes=[out_b])

    def recip(out_b, out_ap, in_b, in_ap):
        S.op("dve", lambda e: e.reciprocal(out=out_ap, in_=in_ap), reads=[in_b], writes=[out_b])

    def load_w(wt, W_ap, r0, nrow, c0, ncol):
        S.dma("pool", wt[:, 0:nrow // 128, 0:ncol], W_ap[r0:r0 + nrow, c0:c0 + ncol].rearrange("(c p) n -> p c n", p=128),
              writes=[wt])

    def rms_rinv(src_b, src_ap_fn, KC, N, sq_rot, r_b, denom):
        ps = S.ps()
        for k in range(KC):
            sq = sq_rot.next()
            S.op("act", lambda e, k=k, sq=sq: e.activation(out=sq[:, 0:N], in_=src_ap_fn(k), func=AF.Square),
                 reads=[src_b], writes=[sq])
            mm(ps, ps[:, 0:N], ones, ones[:, :], sq, sq[:, 0:N], start=(k == 0), stop=(k == KC - 1))
        act(r_b, r_b[:, 0:N], ps, ps[:, 0:N], AF.Sqrt, bias=eps_t[:, 0:1], scale=1.0 / denom, extra_reads=[eps_t])
        recip(r_b, r_b[:, 0:N], r_b, r_b[:, 0:N])

    def modulate(tt, A_b, Bsl, h, sq_rot, r_b):
        N = ntok(tt)
        x = xt[tt]
        rms_rinv(x, lambda k: x[:, k, :], 8, N, sq_rot, r_b, 1024.0)
        if tt < NQT:
            for k in range(8):
                tm = S._mtmp.next()
                stt(tm, tm[:, 0:N], x, x[:, k, :], A_b[:, k, 0:1], r_b, r_b[:, 0:N], ALU.mult, ALU.mult, extra_reads=[A_b])
                act(h, h[:, k, 0:N], tm, tm[:, 0:N], AF.Identity, bias=mod[:, Bsl + k, 0:1], extra_reads=[mod])
        else:
            tm = S._mtmp16
            tt_op("dve", tm, tm[:, :, 0:N], x, x[:, :, :], r_b, r_b[:, 0:N].unsqueeze(1).to_broadcast([128, 8, N]), ALU.mult)
            tt_op("dve", tm, tm[:, :, 0:N], tm, tm[:, :, 0:N], A_b, A_b[:, :, 1:17], ALU.mult)
            tt_op("dve", h, h[:, :, 0:N], tm, tm[:, :, 0:N], mod, mod[:, Bsl:Bsl + 8, 1:17], ALU.add)

    def resid_update(tt, kchunk, ps, N, Gsl):
        x = xt[tt]
        if tt < NQT:
            stt(x, x[:, kchunk, :], ps, ps[:, 0:N], mod[:, Gsl + kchunk, 0:1], x, x[:, kchunk, :], ALU.mult, ALU.add,
                extra_reads=[mod])
        else:
            tmp = S._tmp16.next()
            tt_op("dve", tmp, tmp[:, 0:N], ps, ps[:, 0:N], mod, mod[:, Gsl + kchunk, 1:17], ALU.mult)
            tt_op("dve", x, x[:, kchunk, :], x, x[:, kchunk, :], tmp, tmp[:, 0:N], ALU.add)

    S._tmp16 = Rot([S.sbuf(f"tmp16_{a}", [128, 16]) for a in range(4)])
    S._mtmp = Rot([S.sbuf(f"mtmp{a}", [128, 512]) for a in range(3)])
    S._mtmp16 = S.sbuf("mtmp16", [128, 8, 16])

    def out_proj(W_ap, KC, src_fn, tt, wts, Gsl):
        N = ntok(tt)
        for blk in range(2):
            pss = [S.ps() for _ in range(4)]
            nkg = (KC + 7) // 8
            for kg in range(nkg):
                kn = min(8, KC - kg * 8)
                wt = wts.next()
                load_w(wt, W_ap, kg * 1024, kn * 128, blk * 512, 512)
                for jc in range(4):
                    for k in range(kn):
                        sb, sap = src_fn(kg * 8 + k)
                        mm(pss[jc], pss[jc][:, 0:N], wt, wt[:, k, jc * 128:(jc + 1) * 128], sb, sap,
                           start=(kg == 0 and k == 0), stop=(kg == nkg - 1 and k == kn - 1))
            for jc in range(4):
                resid_update(tt, blk * 4 + jc, pss[jc], N, Gsl)

    def adaln(i):
        with S.scope():
            wts = Rot([S.sbuf(f"wt{a}", [128, 8, 512], BF16) for a in range(4)])
            S.dma("sp", adab[:], I["ada_bT"][i], writes=[adab])
            S.dma("sp", nw[:], I["norm_wT"][i], writes=[nw])
            for blk in range(12):
                wt = wts.next()
                load_w(wt, I["ada_w"][i], 0, 1024, blk * 512, 512)
                ps = S.ps()
                for jc in range(4):
                    for k in range(8):
                        mm(ps, ps[:, jc * 17:(jc + 1) * 17], wt, wt[:, k, jc * 128:(jc + 1) * 128], scTb, scTb[:, k, :],
                           start=(k == 0), stop=(k == 7))
                tt_op("dve", mod, mod[:, blk * 4:(blk + 1) * 4, :], ps, ps[:, 0:68].rearrange("p (a b) -> p a b", a=4),
                      adab, adab[:, blk * 4:(blk + 1) * 4].unsqueeze(2).to_broadcast([128, 4, 17]), ALU.add)
            for (Ab, sc0, w0) in ((A1, 8, 0), (A2, 32, 8)):
                ts_op("dve", Ab, Ab[:], mod, mod[:, sc0:sc0 + 8, :], 1.0, None, ALU.add)
                tt_op("dve", Ab, Ab[:], Ab, Ab[:], nw, nw[:, w0:w0 + 8].unsqueeze(2).to_broadcast([128, 8, 17]), ALU.mult)

    def mlp(i):
        with S.scope():
            wts = Rot([S.sbuf(f"wt{a}", [128, 8, 512], BF16) for a in range(4)])
            hb = Rot([S.sbuf(f"h{a}", [128, 8, 512], BF16) for a in range(2)])
            hid = S.sbuf("hid", [128, 32, 512], BF16)
            rls = Rot([S.sbuf(f"rl{a}", [128, 512]) for a in range(3)])
            sq_rot = Rot([S.sbuf(f"sq{a}", [128, 512]) for a in range(2)])
            r_b = S.sbuf("r", [128, 512])
            for tt in range(NQT + 1):
                N = ntok(tt)
                h = hb.next()
                modulate(tt, A2, 24, h, sq_rot, r_b)
                for blk in range(8):
                    wt = wts.next()
                    load_w(wt, I["mlp_w1"][i], 0, 1024, blk * 512, 512)
                    for jc in range(4):
                        ps = S.ps()
                        for k in range(8):
                            mm(ps, ps[:, 0:N], wt, wt[:, k, jc * 128:(jc + 1) * 128], h, h[:, k, 0:N], start=(k == 0), stop=(k == 7))
                        c = blk * 4 + jc
                        rl = rls.next()
                        act(rl, rl[:, 0:N], ps, ps[:, 0:N], AF.Relu)
                        tt_op("dve", hid, hid[:, c, 0:N], rl, rl[:, 0:N], rl, rl[:, 0:N], ALU.mult)
                out_proj(I["mlp_w2"][i], 32, lambda k, N=N: (hid, hid[:, k, 0:N]), tt, wts, 40)

    def nsa_layer(i, j):
        kvp = [O[f"kvp_{cn}{j}"] for cn in ("c", "s", "w")]
        kvs_o = [O[f"kvs_{cn}{j}"] for cn in ("c", "s", "w")]
        W_in = I["nsa_w_in"][j]
        with S.scope():
            qsT = S.sbuf("qsT", [128, 8, 16])
            kvs = S.sbuf("kvs", [128, 12, 16])
            sigGs = S.sbuf("sigGs", [48, 16])
            c0 = S.sbuf("c0", [64, 2]); c0d = S.sbuf("c0d", [128, 2]); w2d = S.sbuf("w2d", [128, 2, 64])
            w2 = S.sbuf("w2", [64, 2, 64]); w2dup = S.sbuf("w2dup", [64, 128])
            oTs = S.sbuf("oTs", [128, 8, 16], BF16)
            S.dma("sp", w2[:], I["cmp_w2"][j], writes=[w2])
            S.dma("sp", w2dup[:, 0:64], I["cmp_w2"][j][:, 0, :], writes=[w2dup])
            S.dma("sp", w2dup[:, 64:128], I["cmp_w2"][j][:, 0, :], writes=[w2dup])
            S.dma("sp", w2d[0:64], I["cmp_w2"][j], writes=[w2d])
            S.dma("sp", w2d[64:128], I["cmp_w2"][j], writes=[w2d])
            with S.scope():
                wts = Rot([S.sbuf(f"wt{a}", [128, 8, 512], BF16) for a in range(4)])
                hb = Rot([S.sbuf(f"h{a}", [128, 8, 512], BF16) for a in range(2)])
                sq_rot = Rot([S.sbuf(f"sq{a}", [128, 512]) for a in range(2)])
                r_b = S.sbuf("r", [128, 512])
                evs = Rot([S.sbuf(f"ev{a}", [128, 512]) for a in range(6)])
                vts = Rot([S.sbuf(f"vt{a}", [128, 256], BF16) for a in range(3)])
                w1s = S.sbuf("w1s", [128, 2, 16, 64]); pe2 = S.sbuf("pe2", [128, 2, 16])
                S.dma("sp", w1s[:], I["cmp_w1std"][j], writes=[w1s])
                S.dma("sp", pe2[:], I["cmp_pe2"][j], writes=[pe2])
                for kv in range(2):
                    ps = S.ps()
                    for cch in range(16):
                        mm(ps, ps[0:64, 0:1], w1s, w1s[:, kv, cch, :], pe2, pe2[:, kv, cch:cch + 1], start=(cch == 0), stop=(cch == 15))
                    cp("dve", c0, c0[:, kv:kv + 1], ps, ps[0:64, 0:1])
                S.dma("sp", c0d[0:64, :], c0[:, :], reads=[c0], writes=[c0d])
                S.dma("sp", c0d[64:128, :], c0[:, :], reads=[c0], writes=[c0d])
                for tt in range(NQT + 1):
                    N = ntok(tt)
                    prompt = tt < NQT
                    h = hb.next()
                    modulate(tt, A1, 0, h, sq_rot, r_b)
                    for blk in range(6):
                        ncol = min(512, 2608 - blk * 512)
                        wt = wts.next()
                        load_w(wt, W_in, 0, 1024, blk * 512, ncol)
                        for jc in range((ncol + 127) // 128):
                            cw = min(128, ncol - jc * 128)
                            ps = S.ps()
                            for k in range(8):
                                mm(ps, ps[0:cw, 0:N], wt, wt[:, k, jc * 128:jc * 128 + cw], h, h[:, k, 0:N], start=(k == 0), stop=(k == 7))
                            gc = blk * 4 + jc
                            if gc < 8:
                                if prompt:
                                    ev = evs.next()
                                    cp("act", ev, ev[:, 0:N], ps, ps[:, 0:N])
                                    S.dma("sp", qT_scr[gc * 128:(gc + 1) * 128, tsl(tt)], ev[:, 0:N], reads=[ev], writes=[qT_scr])
                                else:
                                    cp("act", qsT, qsT[:, gc, :], ps, ps[:, 0:N])
                            elif gc < 20:
                                cache = (gc - 8) // 4
                                rr = (gc - 8) % 4
                                if prompt:
                                    ev = evs.next()
                                    cp("act", ev, ev[:, 0:N], ps, ps[:, 0:N])
                                    S.dma("sp", kvp[cache][rr * 128:(rr + 1) * 128, tsl(tt)], ev[:, 0:N], reads=[ev], writes=[kvp[cache]])
                                    if cache == 2 and tt * 512 >= T - 512:
                                        wo = O[f"kvwin_p{j}"]
                                        c_off = tt * 512 - max(0, T - 512)
                                        S.dma("sp", wo[rr * 128:(rr + 1) * 128, c_off:c_off + N], ev[:, 0:N], reads=[ev], writes=[wo])
                                else:
                                    cp("act", kvs, kvs[:, gc - 8, :], ps, ps[:, 0:N])
                            else:
                                if prompt:
                                    ev = evs.next()
                                    act(ev, ev[0:48, 0:N], ps, ps[0:48, 0:N], AF.Sigmoid)
                                    S.dma("sp", g_scr[:, tsl(tt)], ev[0:48, 0:N], reads=[ev], writes=[g_scr])
                                else:
                                    act(sigGs, sigGs[:, :], ps, ps[0:48, 0:N], AF.Sigmoid)
                        if prompt and blk in (3, 4):
                            for sub in range(4):
                                ps = S.ps()
                                for k in range(8):
                                    mm(ps, ps[:, 0:256], h, h[:, k, sub * 128:(sub + 1) * 128], wt, wt[:, k, 256:512], start=(k == 0), stop=(k == 7))
                                vt = vts.next()
                                cp("dve", vt, vt[:, :], ps, ps[:, 0:256])
                                r0 = tt * 512 + sub * 128
                                S.dma("sp", vtok_scr[blk - 3][r0:r0 + 128, :], vt[:, :], reads=[vt], writes=[vtok_scr[blk - 3]])
                for cache in range(3):
                    S.dma("sp", kvs_o[cache][:, :].rearrange("(c p) b -> p c b", p=128), kvs[:, cache * 4:(cache + 1) * 4, :],
                          reads=[kvs], writes=[kvs_o[cache]])
                wold = O[f"kvwin_old{j}"]
                for b in range(16):
                    S.dma("sp", wold[b], I["cwin"][j, b, 1:512, :], writes=[wold])
            for g in range(4):
                nsa_group(j, g, kvp, c0, w2, w2dup)
            nsa_sample(j, qsT, kvs, sigGs, c0d, w2d, oTs)
            with S.scope():
                wts = Rot([S.sbuf(f"wt{a}", [128, 8, 512], BF16) for a in range(4)])
                ots = Rot([S.sbuf(f"ot{a}", [128, 8, 512], BF16) for a in range(2)])
                for tt in range(NQT + 1):
                    N = ntok(tt)
                    if tt < NQT:
                        ot = ots.next()
                        S.dma("sp", ot[:, :, 0:N], oT_scr[:, tsl(tt)].rearrange("(c p) t -> p c t", p=128), reads=[oT_scr], writes=[ot])
                        out_proj(I["nsa_w_out"][j], 8, lambda k, ot=ot, N=N: (ot, ot[:, k, 0:N]), tt, wts, 16)
                    else:
                        out_proj(I["nsa_w_out"][j], 8, lambda k: (oTs, oTs[:, k, :]), tt, wts, 16)

    def nsa_group(j, g, kvp, c0, w2, w2dup):
        with S.scope():
            acc = S.sbuf("acc", [128, 2, T])
            sigG = S.sbuf("sigG", [48, T])
            kcT = S.sbuf("kcT", [128, 128]); vc = S.sbuf("vc", [128, 64])
            selbT = S.sbuf("selbT", [32, T], BF16)
            pnsum = S.sbuf("pnsum", [128, T])
            S.dma("sp", sigG[:], g_scr[:, :], reads=[g_scr], writes=[sigG])
            with S.scope():
                w1 = S.sbuf("w1", [64, 2, 32, 64])
                KV = S.sbuf("kvc", [64, 2, T])
                hid = S.sbuf("hid", [64, 2, 128])
                S.dma("sp", w1[:], I["cmp_w1d"][j][0:64], writes=[w1])
                S.dma("sp", KV[:, 0, :], kvp[0][g * 64:(g + 1) * 64, :], reads=[kvp[0]], writes=[KV])
                S.dma("sp", KV[:, 1, :], kvp[0][256 + g * 64:256 + (g + 1) * 64, :], reads=[kvp[0]], writes=[KV])
                for kv in range(2):
                    ps = S.ps()
                    for l in range(32):
                        mm(ps, ps[0:64, 0:NCB], w1, w1[:, kv, l, :], KV, KV[:, kv, l:l + 16 * (NCB - 1) + 1:16], start=(l == 0), stop=(l == 31))
                    act(hid, hid[:, kv, 0:NCB], ps, ps[0:64, 0:NCB], AF.Silu, bias=c0[:, kv:kv + 1], extra_reads=[c0])
                ps = S.ps()
                mm(ps, ps[:, 0:NCB], w2dup, w2dup[:, :], hid, hid[:, 0, 0:NCB])
                cp("dve", kcT, kcT[:, 0:NCB], ps, ps[:, 0:NCB])
                ps = S.ps()
                mm(ps, ps[0:NCB, 0:64], hid, hid[:, 1, 0:NCB], w2, w2[:, 1, :])
                cp("dve", vc, vc[0:NCB, :], ps, ps[0:NCB, 0:64])
            with S.scope():
                qb = [S.sbuf(f"q{a}", [128, T]) for a in range(2)]
                cmpb = S.sbuf("cmpb", [128, T])
                Pts = Rot([S.sbuf(f"P{a}", [128, 512]) for a in range(2)])
                rDs = Rot([S.sbuf(f"rD{a}", [128, 512]) for a in range(2)])
                pns = Rot([S.sbuf(f"pn{a}", [128, 512]) for a in range(2)])
                ogs = Rot([S.sbuf(f"og{a}", [64, 512]) for a in range(2)])
                for c2 in range(2):
                    S.dma("sp", qb[c2][:], qT_scr[(2 * g + c2) * 128:(2 * g + c2 + 1) * 128, :], reads=[qT_scr], writes=[qb[c2]])
                S.dma("sp", cmpb[:], I["c_cmpb"], writes=[cmpb])
                for qt in range(NQT):
                    qs = slice(qt * 512, (qt + 1) * 512)
                    for hh in range(4):
                        c2 = hh // 2
                        hb_ = 64 * (hh % 2)
                        h = 4 * g + hh
                        ps = S.ps()
                        mm(ps, ps[0:NCB, :], kcT, kcT[hb_:hb_ + 64, 0:NCB], qb[c2], qb[c2][hb_:hb_ + 64, qs], start=True, stop=False)
                        mm(ps, ps[0:NCB, :], ident, ident[0:NCB, 0:NCB], cmpb, cmpb[0:NCB, qs], start=False, stop=True)
                        Pt = Pts.next()
                        act(Pt, Pt[0:NCB, :], ps, ps[0:NCB, :], AF.Exp, scale=SCALE)
                        psd = S.ps()
                        mm(psd, psd[:, :], ones, ones[0:NCB, :], Pt, Pt[0:NCB, :])
                        rD = rDs.next()
                        ts_op("dve", rD, rD[:, :], psd, psd[:, :], 1e-30, None, ALU.max)
                        recip(rD, rD[:, :], rD, rD[:, :])
                        pn = pns.next()
                        tt_op("dve", pn, pn[0:NCB, :], Pt, Pt[0:NCB, :], rD, rD[0:NCB, :], ALU.mult)
                        if hh == 0:
                            cp("pool", pnsum, pnsum[0:NCB, qs], pn, pn[0:NCB, :])
                        else:
                            tt_op("pool", pnsum, pnsum[0:NCB, qs], pnsum, pnsum[0:NCB, qs], pn, pn[0:NCB, :], ALU.add)
                        pso = S.ps()
                        mm(pso, pso[0:64, :], vc, vc[0:NCB, 0:64], pn, pn[0:NCB, :])
                        psg = S.ps()
                        gi = h * 3 + 0
                        mm(psg, psg[0:64, :], ident, ident[0:48, gi:gi + 1].to_broadcast([48, 64]), sigG, sigG[0:48, qs])
                        og = ogs.next()
                        cp("act", og, og[:, :], psg, psg[0:64, :])
                        tt_op("dve", acc, acc[hb_:hb_ + 64, c2, qs], pso, pso[0:64, :], og, og[:, :], ALU.mult)
            with S.scope():
                m1 = S.sbuf("m1", [128, NQ128 * 32]); m2 = S.sbuf("m2", [128, NQ128 * 32]); vl = S.sbuf("vl", [128, NQ128 * 32])
                ovl = S.sbuf("ovl", [128, 32])
                score = S.sbuf("score", [128, NQ128 * 32]); sel = S.sbuf("sel", [128, NQ128 * 32])
                top8 = S.sbuf("top8", [128, NQ128, 8])
                S.dma("sp", m1[:], I["c_m1"].rearrange("p a b -> p (a b)"), writes=[m1])
                S.dma("sp", m2[:], I["c_m2"].rearrange("p a b -> p (a b)"), writes=[m2])
                S.dma("sp", vl[:], I["c_valid"].rearrange("p a b -> p (a b)"), writes=[vl])
                S.dma("sp", ovl[:], I["c_ovl"], writes=[ovl])
                ps = S.ps()
                for sub in range(NQ128):
                    mm(ps, ps[:, sub * 32:(sub + 1) * 32], pnsum, pnsum[0:NCB, sub * 128:(sub + 1) * 128], ovl, ovl[0:NCB, :])
                W_ = NQ128 * 32
                tt_op("dve", score, score[:, :], ps, ps[:, 0:W_], m1, m1[:, :], ALU.mult)
                tt_op("dve", score, score[:, :], score, score[:, :], m2, m2[:, :], ALU.add)
                for sub in range(NQ128):
                    S.op("dve", lambda e, sub=sub: e.max(out=top8[:, sub, :], in_=score[:, sub * 32:(sub + 1) * 32]), reads=[score], writes=[top8])
                tt_op("dve", sel, sel[:, :].rearrange("p (a b) -> p a b", b=32), score, score[:, :].rearrange("p (a b) -> p a b", b=32),
                      top8, top8[:, :, 7:8].to_broadcast([128, NQ128, 32]), ALU.is_ge)
                tt_op("dve", sel, sel[:, :], sel, sel[:, :], vl, vl[:, :], ALU.mult)
                ts_op("dve", sel, sel[:, :], sel, sel[:, :], -NEGB, NEGB, ALU.mult, ALU.add)
                for qt in range(NQT):
                    ps = S.ps()
                    for s4 in range(4):
                        sub = qt * 4 + s4
                        tr(ps, ps[0:32, s4 * 128:(s4 + 1) * 128], sel, sel[:, sub * 32:(sub + 1) * 32], 128)
                    cp("act", selbT, selbT[:, qt * 512:(qt + 1) * 512], ps, ps[0:32, :])
            with S.scope():
                qb = [S.sbuf(f"q{a}", [128, T], BF16) for a in range(2)]
                Ks = S.sbuf("Ks", [128, T], BF16); Kw = S.sbuf("Kw", [128, T], BF16)
                VA = [S.sbuf(f"VA{a}", [128, NKC, 128], BF16) for a in range(2)]
                eall = S.sbuf("eall", [32, NKC, 128], BF16); cb = S.sbuf("cb", [128, 1024], BF16); wb = S.sbuf("wb", [128, 896], BF16)
                Pts = Rot([S.sbuf(f"P{a}", [128, 512], BF16) for a in range(5)])
                rDs = Rot([S.sbuf(f"rD{a}", [64, 512]) for a in range(2)])
                ogs = Rot([S.sbuf(f"og{a}", [64, 512]) for a in range(2)])
                tmps = Rot([S.sbuf(f"tmp{a}", [128, 512]) for a in range(2)])
                for c2 in range(2):
                    S.dma("pool", qb[c2][:], qT_scr[(2 * g + c2) * 128:(2 * g + c2 + 1) * 128, :], reads=[qT_scr], writes=[qb[c2]])
                for half in range(2):
                    S.dma("pool", Ks[half * 64:(half + 1) * 64, :], kvp[1][g * 64:(g + 1) * 64, :], reads=[kvp[1]], writes=[Ks])
                    S.dma("pool", Kw[half * 64:(half + 1) * 64, :], kvp[2][g * 64:(g + 1) * 64, :], reads=[kvp[2]], writes=[Kw])
                for a in range(2):
                    S.op("pool", lambda e, a=a: e.memset(VA[a][:, :, 64:128], 1.0), writes=[VA[a]])
                    S.dma("sp", VA[a][:, :, 0:64], vtok_scr[a][:, g * 64:(g + 1) * 64].rearrange("(c p) d -> p c d", p=128),
                          reads=[vtok_scr[a]], writes=[VA[a]])
                S.dma("pool", eall[:], I["c_eall"], writes=[eall])
                S.dma("pool", cb[:], I["c_cb"], writes=[cb])
                S.dma("pool", wb[:], I["c_wb"], writes=[wb])

                def finalize(psO, br, h, hb_, c2, qs):
                    rD = rDs.next()
                    ts_op("dve", rD, rD[:, :], psO, psO[64:128, :], 1e-30, None, ALU.max)
                    recip(rD, rD[:, :], rD, rD[:, :])
                    psg = S.ps()
                    gi = h * 3 + br
                    mm(psg, psg[0:64, :], ident, ident[0:48, gi:gi + 1].to_broadcast([48, 64]), sigG, sigG[0:48, qs])
                    og = ogs.next()
                    cp("act", og, og[:, :], psg, psg[0:64, :])
                    tt_op("pool", og, og[:, :], og, og[:, :], rD, rD[:, :], ALU.mult)
                    tmp = tmps.next()
                    tt_op("dve", tmp, tmp[hb_:hb_ + 64, :], psO, psO[0:64, :], og, og[:, :], ALU.mult)
                    tt_op("pool", acc, acc[hb_:hb_ + 64, c2, qs], acc, acc[hb_:hb_ + 64, c2, qs], tmp, tmp[hb_:hb_ + 64, :], ALU.add)

                LA = 2
                steps = []
                for qt in range(NQT):
                    for hh in range(4):
                        nk = 4 * qt + 4
                        for kc in range(nk):
                            steps.append((1, qt, hh, kc, kc == 0, kc == nk - 1))
                        kcs = list(range(max(0, 4 * qt - 4), 4 * qt + 4))
                        for ki, kc in enumerate(kcs):
                            steps.append((2, qt, hh, kc, ki == 0, ki == len(kcs) - 1))

                def emit_score(st):
                    br, qt, hh, kc, first, last = st
                    qs = slice(qt * 512, (qt + 1) * 512)
                    c2 = hh // 2
                    hb_ = 64 * (hh % 2)
                    qap = qb[c2][hb_:hb_ + 64, qs]
                    ks = slice(kc * 128, (kc + 1) * 128)
                    ps = S.ps()
                    if br == 1:
                        diag = kc >= 4 * qt
                        mm(ps, ps[:, :], Ks, Ks[hb_:hb_ + 64, ks], qb[c2], qap, start=True, stop=False)
                        mm(ps, ps[:, :], eall, eall[:, kc, :], selbT, selbT[:, qs], start=False, stop=not diag)
                        if diag:
                            d = 128 * kc - 512 * qt
                            mm(ps, ps[:, :], identb, identb[:, :], cb, cb[:, 512 - d:1024 - d], start=False, stop=True)
                    else:
                        mm(ps, ps[:, :], Kw, Kw[hb_:hb_ + 64, ks], qb[c2], qap, start=True, stop=False)
                        if kc >= 4 * qt:
                            d = 128 * kc - 512 * qt
                            mm(ps, ps[:, :], identb, identb[:, :], cb, cb[:, 512 - d:1024 - d], start=False, stop=True)
                        else:
                            m = kc - 4 * qt + 4
                            mm(ps, ps[:, :], identb, identb[:, :], wb, wb[:, 384 - 128 * m:896 - 128 * m], start=False, stop=True)
                    Pt = Pts.next()
                    act(Pt, Pt[:, :], ps, ps[:, :], AF.Exp, scale=SCALE)
                    return Pt

                cur = {}

                def emit_pv(st, Pt):
                    br, qt, hh, kc, first, last = st
                    if first:
                        cur[br] = S.ps_acc()
                    psO = cur[br]
                    mm(psO, psO[:, :], VA[br - 1], VA[br - 1][:, kc, :], Pt, Pt[:, :], start=first, stop=last)
                    if last:
                        qs = slice(qt * 512, (qt + 1) * 512)
                        finalize(psO, br, 4 * g + hh, 64 * (hh % 2), hh // 2, qs)

                pend = []
                for i in range(len(steps) + LA):
                    if i < len(steps):
                        pend.append((steps[i], emit_score(steps[i])))
                    if i >= LA:
                        st, Pt = pend.pop(0)
                        emit_pv(st, Pt)
            for c2 in range(2):
                S.dma("pool", oT_scr[(2 * g + c2) * 128:(2 * g + c2 + 1) * 128, :], acc[:, c2, :], reads=[acc], writes=[oT_scr])

    def nsa_sample(j, qsT, kvs, sigGs, c0d, w2d, oTs):
        pool_c = I["pool_c"].rearrange("n r c -> (n r) c")
        pool_s = I["pool_s"].rearrange("n r c -> (n r) c")
        roff = float(j * NROWS)
        with S.scope():
            qsg = S.sbuf("qsg", [64, 16, 16]); knew = S.sbuf("knew", [64, 24, 16])
            Gb = S.sbuf("Gb", [64, 48, 16])
            Oc = S.sbuf("Oc", [64, 16, 16])
            Osd = [S.sbuf(f"Osd{a}", [64, 16, 16]) for a in range(4)]
            PN = S.sbuf("PN", [128, 64])
            osa = S.sbuf("osa", [64, 16, 16])
            ptb = S.sbuf("ptb", [128, 256], I32); ptf = S.sbuf("ptf", [128, 256]); idxc = S.sbuf("idxc", [128, 256], I32)
            p128 = S.sbuf("p128", [128, 1]); p64 = S.sbuf("p64", [128, 1])
            idxs = S.sbuf("idxs", [128, 256], I32)
            S.dma("sp", p128[:], I["c_p128"], writes=[p128]); S.dma("sp", p64[:], I["c_p64"], writes=[p64])
            cp("dve", qsg, qsg[:, 0:16:2, :], qsT, qsT[0:64, :, :])
            cp("dve", qsg, qsg[:, 1:16:2, :], qsT, qsT[64:128, :, :])
            cp("dve", knew, knew[:, 0:24:2, :], kvs, kvs[0:64, :, :])
            cp("dve", knew, knew[:, 1:24:2, :], kvs, kvs[64:128, :, :])
            for half, n in ((0, 32), (1, 16)):
                ps = S.ps()
                for a in range(n):
                    gi = half * 32 + a
                    mm(ps, ps[0:64, a * 16:(a + 1) * 16], ident, ident[0:48, gi:gi + 1].to_broadcast([48, 64]), sigGs, sigGs[:, :])
                cp("act", Gb, Gb[:, half * 32:half * 32 + n, :], ps, ps[0:64, 0:n * 16].rearrange("p (a b) -> p a b", b=16))
            S.dma("sp", ptb[:], I["pt"].rearrange("b n -> (b n)").partition_broadcast(128), writes=[ptb])
            cp("dve", ptf, ptf[:], ptb, ptb[:])
            ts_op("dve", ptf, ptf[:], ptf, ptf[:], 128.0, p128[:, 0:1], ALU.mult, ALU.add, extra_reads=[p128])
            if j > 0:
                ts_op("dve", ptf, ptf[:], ptf, ptf[:], roff, None, ALU.add)
            cp("dve", idxc, idxc[:], ptf, ptf[:])
            with S.scope():
                w1 = S.sbuf("w1", [128, 2, 32, 128], BF16)
                rowsTs = Rot([S.sbuf(f"rowsT{a}", [128, 4, 2048], BF16) for a in range(2)])
                gts = Rot([S.sbuf(f"gt{a}", [128, 512]) for a in range(4)])
                hids = Rot([S.sbuf(f"hid{a}", [128, 2, 2, 128]) for a in range(2)])
                kcs4 = Rot([S.sbuf(f"kcs{a}", [64, 4, 128]) for a in range(2)])
                vcs4 = Rot([S.sbuf(f"vcs{a}", [128, 4, 64]) for a in range(2)])
                Pts_ = Rot([S.sbuf(f"Pt{a}", [128, 16]) for a in range(2)])
                rDs_ = Rot([S.sbuf(f"rD{a}", [128, 16]) for a in range(2)])
                pns_ = Rot([S.sbuf(f"pn{a}", [128, 16]) for a in range(2)])
                ovs = S.sbuf("ovs", [128, 32])
                S.dma("pool", w1[:], I["cmp_w1bd"][j], writes=[w1])
                S.dma("sp", ovs[:], I["c_ovs"], writes=[ovs])

                def cmp_tail(b, hid):
                    kc4 = kcs4.next(); vc4 = vcs4.next()
                    psk = S.ps()
                    psv = S.ps()
                    for g in range(4):
                        pr, hb_ = g // 2, 64 * (g % 2)
                        mm(psk, psk[0:64, g * 128:g * 128 + 127], w2d, w2d[hb_:hb_ + 64, 0, :], hid, hid[hb_:hb_ + 64, pr, 0, 0:127])
                        mm(psv, psv[0:127, g * 64:(g + 1) * 64], hid, hid[hb_:hb_ + 64, pr, 1, 0:127], w2d, w2d[hb_:hb_ + 64, 1, :])
                    cp("dve", kc4, kc4[:, :, 0:127], psk, psk[0:64, :].rearrange("p (a b) -> p a b", a=4)[:, :, 0:127])
                    cp("act", vc4, vc4[0:127, :, :], psv, psv[0:127, 0:256].rearrange("p (a b) -> p a b", a=4))
                    ps = S.ps()
                    for g in range(4):
                        mm(ps, ps[0:127, 4 * g:4 * g + 4], kc4, kc4[:, g, 0:127], qsg, qsg[:, 4 * g:4 * g + 4, b])
                    Pt = Pts_.next(); rD = rDs_.next(); pn = pns_.next()
                    act(Pt, Pt[0:127, :], ps, ps[0:127, 0:16], AF.Exp, scale=SCALE)
                    psd = S.ps()
                    mm(psd, psd[:, 0:16], ones, ones[0:127, :], Pt, Pt[0:127, :])
                    recip(rD, rD[:, :], psd, psd[:, 0:16])
                    tt_op("dve", pn, pn[0:127, :], Pt, Pt[0:127, :], rD, rD[0:127, :], ALU.mult)
                    S.op("dve", lambda e, b=b, pn=pn: e.tensor_reduce(out=PN[0:127, 4 * b:4 * b + 4],
                                                                      in_=pn[0:127, :].rearrange("p (a b) -> p a b", a=4), axis=AX.X, op=ALU.add),
                         reads=[pn], writes=[PN])
                    pso = S.ps()
                    for g in range(4):
                        mm(pso, pso[0:64, 4 * g:4 * g + 4], vc4, vc4[0:127, g, :], pn, pn[0:127, 4 * g:4 * g + 4])
                    cp("act", Oc, Oc[:, :, b], pso, pso[0:64, 0:16])

                prev = None
                for b in range(16):
                    rowsT = rowsTs.next()
                    for pg in range(16):
                        gt = gts.next()
                        col = b * 16 + pg
                        S.dma("pool", None, None, reads=[idxc], writes=[gt],
                              fn=lambda e, gt=gt, col=col: e.indirect_dma_start(
                                  out=gt[:, :], out_offset=None, in_=pool_c[:, :],
                                  in_offset=bass.IndirectOffsetOnAxis(ap=idxc[:, col:col + 1], axis=0)))
                        ps = S.ps()
                        for c4 in range(4):
                            tr(ps, ps[:, c4 * 128:(c4 + 1) * 128], gt, gt[:, c4 * 128:(c4 + 1) * 128], 128)
                        cp("act" if pg % 2 == 0 else "dve", rowsT, rowsT[:, :, pg * 128:(pg + 1) * 128],
                           ps, ps[:, :].rearrange("p (a b) -> p a b", a=4))
                    hid = hids.next()
                    for pr in range(2):
                        for kv in range(2):
                            c4 = kv * 2 + pr
                            ps = S.ps()
                            for l in range(32):
                                mm(ps, ps[:, 0:127], w1, w1[:, kv, l, :], rowsT, rowsT[:, c4, l:l + 16 * 126 + 1:16],
                                   start=(l == 0), stop=(l == 31))
                            act(hid, hid[:, pr, kv, 0:127], ps, ps[:, 0:127], AF.Silu, bias=c0d[:, kv:kv + 1], extra_reads=[c0d])
                    if prev is not None:
                        cmp_tail(*prev)
                    prev = (b, hid)
                cmp_tail(*prev)
                m1s = S.sbuf("m1s", [64, 32]); m2s = S.sbuf("m2s", [64, 32]); iota32 = S.sbuf("iota32", [64, 32])
                score = S.sbuf("score", [64, 32]); top8 = S.sbuf("top8", [64, 8]); idxu = S.sbuf("idxu", [64, 8], U32)
                idxf = S.sbuf("idxf", [64, 8]); oh = S.sbuf("oh", [64, 32]); junk = S.sbuf("junk", [64, 32])
                ptg = S.sbuf("ptg", [64, 16], I32); ptgf = S.sbuf("ptgf", [64, 16]); PB = S.sbuf("PB", [64, 16, 2])
                phys = S.sbuf("phys", [64, 8]); physT = S.sbuf("physT", [8, 64]); e2 = S.sbuf("e2", [8, 4, 128])
                S.dma("sp", m1s[:], I["c_m1s"], writes=[m1s]); S.dma("sp", m2s[:], I["c_m2s"], writes=[m2s])
                S.dma("sp", iota32[:], I["c_iota32"], writes=[iota32]); S.dma("sp", e2[:], I["c_e2"], writes=[e2])
                for b in range(16):
                    S.dma("sp", ptg[4 * b:4 * b + 4, :], I["pt"][b, :].partition_broadcast(4), writes=[ptg])
                cp("dve", ptgf, ptgf[:], ptg, ptg[:])
                ts_op("dve", PB, PB[:, :, 0], ptgf, ptgf[:, :], 2.0, None, ALU.mult)
                ts_op("dve", PB, PB[:, :, 1], ptgf, ptgf[:, :], 2.0, 1.0, ALU.mult, ALU.add)
                ps = S.ps()
                mm(ps, ps[0:64, 0:32], PN, PN[0:127, 0:64], ovs, ovs[0:127, :])
                tt_op("dve", score, score[:], ps, ps[0:64, 0:32], m1s, m1s[:], ALU.mult)
                tt_op("dve", score, score[:], score, score[:], m2s, m2s[:], ALU.add)
                S.op("dve", lambda e: e.max(out=top8[:], in_=score[:]), reads=[score], writes=[top8])
                S.op("dve", lambda e: e.max_index(out=idxu[:], in_max=top8[:], in_values=score[:]), reads=[top8, score], writes=[idxu])
                cp("dve", idxf, idxf[:], idxu, idxu[:])
                PBf = PB[:, :, :].rearrange("p a b -> p (a b)")
                for k in range(8):
                    ts_op("dve", oh, oh[:], iota32, iota32[:], idxf[:, k:k + 1], None, ALU.is_equal, extra_reads=[idxf])
                    stt(junk, junk[:], oh, oh[:], 1.0, PB, PBf, ALU.mult, ALU.mult, accum=(phys, phys[:, k:k + 1]))
                ps = S.ps()
                tr(ps, ps[0:8, 0:64], phys, phys[0:64, 0:8], 64)
                cp("dve", physT, physT[:], ps, ps[0:8, 0:64])
                ps = S.ps()
                for m in range(4):
                    mm(ps, ps[:, m * 64:(m + 1) * 64], e2, e2[:, m, :], physT, physT[:, :])
                idsf = S.sbuf("idsf", [128, 256])
                ts_op("dve", idsf, idsf[:], ps, ps[:, 0:256], 64.0, p64[:, 0:1], ALU.mult, ALU.add, extra_reads=[p64])
                if j > 0:
                    ts_op("dve", idsf, idsf[:], idsf, idsf[:], roff, None, ALU.add)
                cp("dve", idxs, idxs[:], idsf, idsf[:])
            with S.scope():
                gss = Rot([S.sbuf(f"gs{a}", [128, 512]) for a in range(6)])
                wrs = [S.sbuf(f"wr{a}", [128, 512]) for a in range(4)]
                kTs = Rot([S.sbuf(f"kT{a}", [64, 128]) for a in range(3)])
                Pts = Rot([S.sbuf(f"P{a}", [128, 4]) for a in range(3)])
                negh = S.sbuf("negh", [128, 4])
                S.dma("sp", negh[:], I["c_negh"], writes=[negh])

                def branch(srcs, b, g, Ob, Db, last_bias):
                    psO = S.ps_acc()
                    psD = S.ps_acc()
                    for m in range(4):
                        src = srcs[m]
                        pst = S.ps()
                        tr(pst, pst[0:64, 0:128], src, src[:, g * 64:(g + 1) * 64], 128)
                        kT = kTs.next()
                        cp("act", kT, kT[:, :], pst, pst[0:64, 0:128])
                        pss = S.ps()
                        lb = last_bias and m == 3
                        mm(pss, pss[:, 0:4], kT, kT[:, :], qsg, qsg[:, 4 * g:4 * g + 4, b], start=True, stop=not lb)
                        if lb:
                            mm(pss, pss[:, 0:4], ident, ident[:, :], negh, negh[:, :], start=False, stop=True)
                        Pt = Pts.next()
                        act(Pt, Pt[:, :], pss, pss[:, 0:4], AF.Exp, scale=SCALE)
                        mm(psO, psO[0:64, 0:4], src, src[:, 256 + g * 64:256 + (g + 1) * 64], Pt, Pt[:, :], start=(m == 0), stop=(m == 3))
                        mm(psD, psD[0:64, 0:4], ones, ones[:, 0:64], Pt, Pt[:, :], start=(m == 0), stop=(m == 3))
                    cp("act", Ob, Ob[:, 4 * g:4 * g + 4, b], psO, psO[0:64, 0:4])
                    cp("dve", Db, Db[:, 4 * g:4 * g + 4, b], psD, psD[0:64, 0:4])

                for b in range(16):
                    for m in range(4):
                        S.dma("sp", wrs[m][:, :], I["cwin"][j, b, m * 128:(m + 1) * 128, :], writes=[wrs[m]])
                    for g in range(4):
                        srcs = []
                        for m in range(4):
                            gs = gss.next()
                            col = m * 64 + b * 4 + g
                            S.dma("pool", None, None, reads=[idxs], writes=[gs],
                                  fn=lambda e, gs=gs, col=col: e.indirect_dma_start(
                                      out=gs[:, :], out_offset=None, in_=pool_s[:, :],
                                      in_offset=bass.IndirectOffsetOnAxis(ap=idxs[:, col:col + 1], axis=0)))
                            srcs.append(gs)
                        branch(srcs, b, g, Osd[0], Osd[1], True)
                        branch(wrs, b, g, Osd[2], Osd[3], False)
            with S.scope():
                prod = S.sbuf("prod", [64, 16, 16]); pnew = S.sbuf("pnew", [64, 16, 16]); t1 = S.sbuf("t1", [64, 16, 16])
                rDn = S.sbuf("rDn", [64, 16, 16])
                tt_op("dve", osa, osa[:], Oc, Oc[:], Gb, Gb[:, 0:48:3, :], ALU.mult)
                for bi, cache in ((0, 1), (1, 2)):
                    Ob, Db = Osd[2 * bi], Osd[2 * bi + 1]
                    for g in range(4):
                        ki = cache * 8 + g
                        tt_op("dve", prod, prod[:, 4 * g:4 * g + 4, :], qsg, qsg[:, 4 * g:4 * g + 4, :],
                              knew, knew[:, ki:ki + 1, :].to_broadcast([64, 4, 16]), ALU.mult)
                    ps = S.ps()
                    mm(ps, ps[0:64, 0:256], ones, ones[0:64, 0:64], prod, prod[:, :, :].rearrange("p a b -> p (a b)"))
                    act(pnew, pnew[:, :, :].rearrange("p a b -> p (a b)"), ps, ps[0:64, 0:256], AF.Exp, scale=SCALE)
                    tt_op("dve", Db, Db[:], Db, Db[:], pnew, pnew[:], ALU.add)
                    for g in range(4):
                        vi = cache * 8 + 4 + g
                        tt_op("dve", t1, t1[:, 4 * g:4 * g + 4, :], pnew, pnew[:, 4 * g:4 * g + 4, :],
                              knew, knew[:, vi:vi + 1, :].to_broadcast([64, 4, 16]), ALU.mult)
                    tt_op("dve", Ob, Ob[:], Ob, Ob[:], t1, t1[:], ALU.add)
                    recip(rDn, rDn[:], Db, Db[:])
                    tt_op("dve", t1, t1[:], Ob, Ob[:], rDn, rDn[:], ALU.mult)
                    tt_op("dve", t1, t1[:], t1, t1[:], Gb, Gb[:, 1 + bi:48:3, :], ALU.mult)
                    tt_op("dve", osa, osa[:], osa, osa[:], t1, t1[:], ALU.add)
                cp("dve", oTs, oTs[0:64, :, :], osa, osa[:, 0:16:2, :])
                cp("dve", oTs, oTs[64:128, :, :], osa, osa[:, 1:16:2, :])

    def ssd_layer(i, j):
        W_in = I["ssd_w_in"][j]
        with S.scope():
            dt_tok = S.sbuf("dt_tok", [128, NQ128, 32]); dta = S.sbuf("dta", [128, NQ128, 32])
            dtb = S.sbuf("dtb", [128, 32]); aneg = S.sbuf("aneg", [128, 32])
            dtbT = S.sbuf("dtbT", [32, 1]); anegT = S.sbuf("anegT", [32, 1])
            zs = S.sbuf("zs", [128, 16, 16]); xbcs = S.sbuf("xbcs", [128, 24, 16]); dtsT = S.sbuf("dtsT", [32, 16])
            dsk = S.sbuf("dsk", [128, 16]); snw = S.sbuf("snw", [128, 16])
            one_t = S.sbuf("one_t", [128, 1])
            S.op("dve", lambda e: e.memset(one_t[:], 1.0), writes=[one_t])
            S.dma("sp", dtb[:], I["ssd_dtb"][j, 0, :].partition_broadcast(128), writes=[dtb])
            S.dma("sp", aneg[:], I["ssd_alog"][j, 0, :].partition_broadcast(128), writes=[aneg])
            S.dma("sp", dtbT[:], I["ssd_dtbT"][j], writes=[dtbT])
            S.dma("sp", anegT[:], I["ssd_alogT"][j], writes=[anegT])
            S.dma("sp", dsk[:], I["ssd_dT"][j], writes=[dsk])
            S.dma("sp", snw[:], I["ssd_norm_wT"][j], writes=[snw])
            act(aneg, aneg[:], aneg, aneg[:], AF.Exp)
            ts_op("dve", aneg, aneg[:], aneg, aneg[:], -1.0, None, ALU.mult)
            act(anegT, anegT[:], anegT, anegT[:], AF.Exp)
            ts_op("dve", anegT, anegT[:], anegT, anegT[:], -1.0, None, ALU.mult)

            def softplus(buf, ap, tmp_b, tmp_ap):
                act(tmp_b, tmp_ap, buf, ap, AF.Abs)
                act(tmp_b, tmp_ap, tmp_b, tmp_ap, AF.Exp, scale=-1.0)
                act(tmp_b, tmp_ap, tmp_b, tmp_ap, AF.Ln, bias=one_t[0:tmp_ap.shape[0], 0:1], extra_reads=[one_t])
                stt(buf, ap, buf, ap, 0.0, tmp_b, tmp_ap, ALU.max, ALU.add)

            with S.scope():
                wts = Rot([S.sbuf(f"wt{a}", [128, 8, 512], BF16) for a in range(4)])
                hb = Rot([S.sbuf(f"h{a}", [128, 8, 512], BF16) for a in range(2)])
                sq_rot = Rot([S.sbuf(f"sq{a}", [128, 512]) for a in range(2)])
                r_b = S.sbuf("r", [128, 512])
                evs = Rot([S.sbuf(f"ev{a}", [128, 512]) for a in range(6)])
                for tt in range(NQT + 1):
                    N = ntok(tt)
                    prompt = tt < NQT
                    h = hb.next()
                    modulate(tt, A1, 0, h, sq_rot, r_b)
                    for blk in range(11):
                        ncol = min(512, 5152 - blk * 512)
                        wt = wts.next()
                        load_w(wt, W_in, 0, 1024, blk * 512, ncol)
                        if blk == 10:
                            if prompt:
                                for sub in range(4):
                                    ps = S.ps()
                                    for k in range(8):
                                        mm(ps, ps[:, 0:32], h, h[:, k, sub * 128:(sub + 1) * 128], wt, wt[:, k, 0:32], start=(k == 0), stop=(k == 7))
                                    tt_op("dve", dt_tok, dt_tok[:, tt * 4 + sub, :], ps, ps[:, 0:32], dtb, dtb[:, :], ALU.add)
                            else:
                                ps = S.ps()
                                for k in range(8):
                                    mm(ps, ps[0:32, 0:16], wt, wt[:, k, 0:32], h, h[:, k, 0:16], start=(k == 0), stop=(k == 7))
                                ts_op("dve", dtsT, dtsT[:, :], ps, ps[0:32, 0:16], dtbT[:, 0:1], None, ALU.add, extra_reads=[dtbT])
                            continue
                        for jc in range(4):
                            ps = S.ps()
                            for k in range(8):
                                mm(ps, ps[:, 0:N], wt, wt[:, k, jc * 128:(jc + 1) * 128], h, h[:, k, 0:N], start=(k == 0), stop=(k == 7))
                            gc = blk * 4 + jc
                            if prompt:
                                ev = evs.next()
                                cp("act", ev, ev[:, 0:N], ps, ps[:, 0:N])
                                if gc < 16:
                                    S.dma("sp", z_scr[gc * 128:(gc + 1) * 128, tsl(tt)], ev[:, 0:N], reads=[ev], writes=[z_scr])
                                else:
                                    r0 = (gc - 16) * 128
                                    S.dma("sp", xbc_scr[r0:r0 + 128, tsl(tt)], ev[:, 0:N], reads=[ev], writes=[xbc_scr])
                                    if tt == NQT - 1:
                                        cvo = O[f"convp{j}"]
                                        S.dma("sp", cvo[r0:r0 + 128, :], ev[:, N - 3:N], reads=[ev], writes=[cvo])
                            else:
                                if gc < 16:
                                    cp("act", zs, zs[:, gc, :], ps, ps[:, 0:N])
                                else:
                                    cp("act", xbcs, xbcs[:, gc - 16, :], ps, ps[:, 0:N])
                tmpd = S.sbuf("tmpd", [128, NQ128, 32])
                softplus(dt_tok, dt_tok[:], tmpd, tmpd[:])
                tt_op("dve", dta, dta[:], dt_tok, dt_tok[:], aneg, aneg[:, :].unsqueeze(1).to_broadcast([128, NQ128, 32]), ALU.mult)
            import os
            STG = int(os.environ.get("SSD_STAGE", "9"))
            cw = S.sbuf("cw", [128, 24, 4]); cbias = S.sbuf("cbias", [128, 24])
            S.dma("sp", cw[:], I["ssd_conv_wT"][j], writes=[cw])
            S.dma("sp", cbias[:], I["ssd_conv_bT"][j], writes=[cbias])
            S.mute = STG < 2
            with S.scope():
                xins = [S.sbuf(f"xin{a}", [128, T + 3]) for a in range(2)]
                accs = [S.sbuf(f"cacc{a}", [128, T]) for a in range(2)]
                for a in range(2):
                    S.op("pool", lambda e, a=a: e.memset(xins[a][:, 0:3], 0.0), writes=[xins[a]])
                for c in range(24):
                    xin = xins[c % 2]
                    ac = accs[c % 2]
                    S.dma("sp", xin[:, 3:T + 3], xbc_scr[c * 128:(c + 1) * 128, :], reads=[xbc_scr], writes=[xin])
                    ts_op("dve", ac, ac[:, :], xin, xin[:, 3:T + 3], cw[:, c, 3:4], cbias[:, c:c + 1], ALU.mult, ALU.add, extra_reads=[cw, cbias])
                    for k in range(3):
                        stt(ac, ac[:, :], xin, xin[:, k:T + k], cw[:, c, k:k + 1], ac, ac[:, :], ALU.mult, ALU.add, extra_reads=[cw])
                    act(ac, ac[:, :], ac, ac[:, :], AF.Silu)
                    S.dma("pool", xbcA_scr[c * 128:(c + 1) * 128, :], ac[:, :], reads=[ac], writes=[xbcA_scr])
            S.mute = STG < 3
            with S.scope():
                ST = S.sbuf("ST", [128, 16, 128])
                xAs = Rot([S.sbuf(f"xA{a}", [128, 16, 256]) for a in range(1)])
                BCs = Rot([S.sbuf(f"BC{a}", [128, 8, 256]) for a in range(1)])
                xtok = S.sbuf("xtok", [128, 2, 2048]); Btok = S.sbuf("Btok", [128, 2, 512])
                xdt = S.sbuf("xdt", [128, 2, 2048])
                nac = S.sbuf("nac", [128, 2, 32])
                cbm = S.sbuf("cbm", [128, 4, 2, 256])
                trifull = S.sbuf("trifull", [128, 2, 256]); tri = S.sbuf("tri", [128, 128]); causal = S.sbuf("causal", [128, 2, 256])
                Erows = Rot([S.sbuf(f"Erow{a}", [128, 256]) for a in range(2)])
                decs = Rot([S.sbuf(f"dec{a}", [128, 256]) for a in range(3)])
                MTs = Rot([S.sbuf(f"MT{a}", [128, 256]) for a in range(3)])
                xdtes = Rot([S.sbuf(f"xdte{a}", [128, 2, 128]) for a in range(2)])
                yts = Rot([S.sbuf(f"yt{a}", [128, 256]) for a in range(2)])
                t1s = Rot([S.sbuf(f"t1{a}", [128, 256]) for a in range(2)])
                cds = Rot([S.sbuf(f"cd{a}", [128, 2]) for a in range(2)])
                S.dma("sp", trifull[:], I["c_trifull"], writes=[trifull])
                S.dma("sp", tri[:], I["c_tri"], writes=[tri])
                S.dma("sp", causal[:], I["c_causal01"], writes=[causal])
                SP = int(os.environ.get("SCAN_PART", "9"))
                for c in range(cfg.NCH):
                    csl = slice(c * 256, (c + 1) * 256)
                    xA = xAs.next()
                    BC = BCs.next()
                    S.dma("sp", xA[:], xbcA_scr[0:2048, csl].rearrange("(c p) t -> p c t", p=128), reads=[xbcA_scr], writes=[xA])
                    S.dma("sp", BC[:], xbcA_scr[2048:3072, csl].rearrange("(c p) t -> p c t", p=128), reads=[xbcA_scr], writes=[BC])
                    for st in range(2):
                        for q4 in range(4):
                            ps = S.ps()
                            for a in range(4):
                                ch = q4 * 4 + a
                                tr(ps, ps[:, a * 128:(a + 1) * 128], xA, xA[:, ch, st * 128:(st + 1) * 128], 128)
                            cp("act" if q4 % 2 == 0 else "dve", xtok, xtok[:, st, q4 * 512:(q4 + 1) * 512], ps, ps[:, :])
                        ps = S.ps()
                        for a in range(4):
                            tr(ps, ps[:, a * 128:(a + 1) * 128], BC, BC[:, a, st * 128:(st + 1) * 128], 128)
                        cp("act", Btok, Btok[:, st, :], ps, ps[:, :])
                    if SP < 2:
                        continue
                    ps = S.ps()
                    mm(ps, ps[:, 0:32], tri, tri[:, :], dta, dta[:, 2 * c, :], start=True, stop=True)
                    mm(ps, ps[:, 32:64], ones, ones[:, :], dta, dta[:, 2 * c, :], start=True, stop=False)
                    mm(ps, ps[:, 32:64], tri, tri[:, :], dta, dta[:, 2 * c + 1, :], start=False, stop=True)
                    ts_op("dve", nac, nac[:, :, :], ps, ps[:, 0:64].rearrange("p (a b) -> p a b", a=2), -1.0, None, ALU.mult)
                    for st in range(2):
                        tt_op("dve", xdt, xdt[:, st, :].rearrange("p (h d) -> p h d", d=64), xtok, xtok[:, st, :].rearrange("p (h d) -> p h d", d=64),
                              dt_tok, dt_tok[:, 2 * c + st, :].unsqueeze(2).to_broadcast([128, 32, 64]), ALU.mult)
                    for g in range(4):
                        for st in range(2):
                            ps = S.ps()
                            mm(ps, ps[:, 0:256], BC, BC[:, g, st * 128:(st + 1) * 128], BC, BC[:, 4 + g, :])
                            tt_op("dve", cbm, cbm[:, g, st, :], ps, ps[:, 0:256], causal, causal[:, st, :], ALU.mult)
                    SUB = int(os.environ.get("SCAN_SUB", "9"))
                    HPL = int(os.environ.get("SCAN_HP", "16"))
                    for hp in range(HPL if SP >= 3 else 0):
                        g = hp // 4
                        psR = []
                        for x in range(2):
                            hx = 2 * hp + x
                            ps = S.ps()
                            for jt in range(2):
                                mm(ps, ps[:, 0:256], dta, dta[:, 2 * c + jt, hx:hx + 1].to_broadcast([128, 128]), trifull, trifull[:, jt, :],
                                   start=(jt == 0), stop=(jt == 1))
                            psR.append(ps)
                        Erow = Erows.next()
                        act(Erow, Erow[0:64, :], psR[0], psR[0][0:64, 0:256], AF.Exp)
                        act(Erow, Erow[64:128, :], psR[1], psR[1][64:128, 0:256], AF.Exp)
                        cd = cds.next()
                        for x in range(2):
                            act(cd, cd[:, x:x + 1], psR[x], psR[x][:, 255:256], AF.Exp)
                        if SUB < 2:
                            continue
                        xdte = xdtes.next()
                        psY = [S.ps_acc(), S.ps_acc()]
                        for x in range(2):
                            hx = 2 * hp + x
                            for st in range(2):
                                dec = decs.next()
                                ts_op("dve", dec, dec[:, :], psR[x], psR[x][:, 0:256], nac[:, st, hx:hx + 1], 0.0, ALU.add, ALU.min, extra_reads=[nac])
                                act(dec, dec[:, :], dec, dec[:, :], AF.Exp)
                                MT = MTs.next()
                                tt_op("pool", MT, MT[:, :], dec, dec[:, :], cbm, cbm[:, g, st, :], ALU.mult)
                                mm(psY[x], psY[x][0:64, 0:256], xdt, xdt[:, st, hx * 64:(hx + 1) * 64], MT, MT[:, :], start=(st == 0), stop=(st == 1))
                                ts_op("dve", xdte, xdte[:, st, x * 64:(x + 1) * 64], xdt, xdt[:, st, hx * 64:(hx + 1) * 64], dec[:, 255:256], None, ALU.mult,
                                      extra_reads=[dec])
                        if SUB < 3:
                            continue
                        psS = S.ps()
                        for st in range(2):
                            mm(psS, psS[:, 0:128], Btok, Btok[:, st, g * 128:(g + 1) * 128], xdte, xdte[:, st, :], start=(st == 0), stop=(st == 1))
                        if SUB < 4:
                            continue
                        yt = yts.next()
                        cp("act", yt, yt[0:64, :], psY[0], psY[0][0:64, 0:256])
                        cp("act", yt, yt[64:128, :], psY[1], psY[1][0:64, 0:256])
                        if c > 0 and SUB >= 5:
                            psF = S.ps()
                            mm(psF, psF[:, 0:256], ST, ST[:, hp, :], BC, BC[:, 4 + g, :])
                            t1 = t1s.next()
                            tt_op("dve", t1, t1[:, :], psF, psF[:, 0:256], Erow, Erow[:, :], ALU.mult)
                            tt_op("pool", yt, yt[:, :], yt, yt[:, :], t1, t1[:, :], ALU.add)
                        stt(yt, yt[:, :], xA, xA[:, hp, :], dsk[:, hp:hp + 1], yt, yt[:, :], ALU.mult, ALU.add, extra_reads=[dsk])
                        S.dma("pool", y_scr[hp * 128:(hp + 1) * 128, csl], yt[:, :], reads=[yt], writes=[y_scr])
                        if SUB < 6:
                            continue
                        if c == 0:
                            cp("dve", ST, ST[:, hp, :], psS, psS[:, 0:128])
                        else:
                            for x in range(2):
                                stt(ST, ST[:, hp, x * 64:(x + 1) * 64], ST, ST[:, hp, x * 64:(x + 1) * 64], cd[:, x:x + 1],
                                    psS, psS[:, x * 64:(x + 1) * 64], ALU.mult, ALU.add, extra_reads=[cd])
                if SP >= 4:
                    S.dma("pool", O[f"ssmp{j}"][:, :, :].rearrange("h n p -> n h p"), ST[:, :, :], reads=[ST], writes=[O[f"ssmp{j}"]])
            S.mute = STG < 4
            with S.scope():
                cvb = S.sbuf("cvb", [128, 24, 16, 3]); cvo = S.sbuf("cvo", [128, 24, 16, 3])
                xa = S.sbuf("xa", [128, 24, 16]); tmpx = S.sbuf("tmpx", [128, 24, 16])
                tmps_ = S.sbuf("tmps", [32, 16]); dtas = S.sbuf("dtas", [32, 16])
                ehp = S.sbuf("ehp", [32, 16, 128]); selb = S.sbuf("selb", [16, 16, 128])
                dtx = S.sbuf("dtx", [128, 16, 16]); cdx = S.sbuf("cdx", [128, 16, 16]); xdts = S.sbuf("xdts", [128, 16, 16])
                BCtok = S.sbuf("BCtok", [16, 1024])
                Bbs = Rot([S.sbuf(f"Bb{a}", [128, 4, 128]) for a in range(2)])
                Cbs = Rot([S.sbuf(f"Cb{a}", [128, 4, 128]) for a in range(2)])
                sts = Rot([S.sbuf(f"st{a}", [128, 16, 128]) for a in range(2)])
                t1b = Rot([S.sbuf(f"t1b{a}", [128, 16, 128]) for a in range(2)])
                ys = S.sbuf("ys", [128, 16, 16])
                S.dma("sp", cvb[:], I["sconvT"][j].rearrange("(c p) b k -> p c b k", p=128), writes=[cvb])
                S.dma("sp", ehp[:], I["c_ehp"], writes=[ehp])
                S.dma("sp", selb[:], I["c_selb"], writes=[selb])
                tt_op("dve", xa, xa[:], xbcs, xbcs[:], cw, cw[:, :, 3:4].to_broadcast([128, 24, 16]), ALU.mult)
                tt_op("dve", xa, xa[:], xa, xa[:], cbias, cbias[:, :].unsqueeze(2).to_broadcast([128, 24, 16]), ALU.add)
                for k in range(3):
                    tt_op("dve", tmpx, tmpx[:], cvb, cvb[:, :, :, k], cw, cw[:, :, k:k + 1].to_broadcast([128, 24, 16]), ALU.mult)
                    tt_op("dve", xa, xa[:], xa, xa[:], tmpx, tmpx[:], ALU.add)
                act(xa, xa[:], xa, xa[:], AF.Silu)
                cp("pool", cvo, cvo[:, :, :, 0], cvb, cvb[:, :, :, 1])
                cp("pool", cvo, cvo[:, :, :, 1], cvb, cvb[:, :, :, 2])
                cp("pool", cvo, cvo[:, :, :, 2], xbcs, xbcs[:])
                S.dma("pool", O[f"convs{j}"][:, :, :].rearrange("(c p) b k -> p c b k", p=128), cvo[:], reads=[cvo], writes=[O[f"convs{j}"]])
                softplus(dtsT, dtsT[:, :], tmps_, tmps_[:, :])
                ts_op("dve", dtas, dtas[:, :], dtsT, dtsT[:, :], anegT[:, 0:1], None, ALU.mult, extra_reads=[anegT])
                ps = S.ps()
                for hp in range(16):
                    mm(ps, ps[:, hp * 16:(hp + 1) * 16], ehp, ehp[:, hp, :], dtsT, dtsT[:, :])
                cp("dve", dtx, dtx[:], ps, ps[:, 0:256].rearrange("p (a b) -> p a b", b=16))
                ps = S.ps()
                for hp in range(16):
                    mm(ps, ps[:, hp * 16:(hp + 1) * 16], ehp, ehp[:, hp, :], dtas, dtas[:, :])
                act(cdx, cdx[:], ps, ps[:, 0:256].rearrange("p (a b) -> p a b", b=16), AF.Exp)
                tt_op("dve", xdts, xdts[:], xa, xa[:, 0:16, :], dtx, dtx[:], ALU.mult)
                for half in range(2):
                    ps = S.ps()
                    for a in range(4):
                        tr(ps, ps[0:16, a * 128:(a + 1) * 128], xa, xa[:, 16 + half * 4 + a, :], 128)
                    cp("dve", BCtok, BCtok[:, half * 512:(half + 1) * 512], ps, ps[0:16, :])
                sso = O[f"ssms{j}"]
                for b in range(16):
                    ps0 = S.ps()
                    mm(ps0, ps0[:, :], selb, selb[:, b, :], BCtok, BCtok[:, 0:512])
                    ps1 = S.ps()
                    mm(ps1, ps1[:, :], selb, selb[:, b, :], BCtok, BCtok[:, 512:1024])
                    Bb = Bbs.next(); Cb = Cbs.next()
                    cp("act", Bb, Bb[:, :, :].rearrange("p a b -> p (a b)"), ps0, ps0[:, :])
                    cp("act", Cb, Cb[:, :, :].rearrange("p a b -> p (a b)"), ps1, ps1[:, :])
                    st_ = sts.next()
                    S.dma("sp", st_[:], I["sssm"][j, b].rearrange("(c p) n -> p c n", p=128), writes=[st_])
                    t1 = t1b.next()
                    for g in range(4):
                        tt_op("pool", t1, t1[:, 4 * g:4 * g + 4, :], Bb, Bb[:, g:g + 1, :].to_broadcast([128, 4, 128]),
                              xdts, xdts[:, 4 * g:4 * g + 4, b:b + 1].to_broadcast([128, 4, 128]), ALU.mult)
                    tt_op("dve", st_, st_[:], st_, st_[:], cdx, cdx[:, :, b:b + 1].to_broadcast([128, 16, 128]), ALU.mult)
                    tt_op("dve", st_, st_[:], st_, st_[:], t1, t1[:], ALU.add)
                    S.dma("pool", sso[b].rearrange("(c p) n -> p c n", p=128), st_[:], reads=[st_], writes=[sso])
                    for g in range(4):
                        tt_op("pool", t1, t1[:, 4 * g:4 * g + 4, :], st_, st_[:, 4 * g:4 * g + 4, :],
                              Cb, Cb[:, g:g + 1, :].to_broadcast([128, 4, 128]), ALU.mult)
                    S.op("dve", lambda e, t1=t1, b=b: e.tensor_reduce(out=ys[:, :, b], in_=t1[:, :, :], axis=AX.X, op=ALU.add), reads=[t1], writes=[ys])
                tt_op("dve", tmpx, tmpx[:, 0:16, :], xa, xa[:, 0:16, :], dsk, dsk[:, :].unsqueeze(2).to_broadcast([128, 16, 16]), ALU.mult)
                tt_op("dve", ys, ys[:], ys, ys[:], tmpx, tmpx[:, 0:16, :], ALU.add)
                S.dma("pool", y_scr[:, T:TT].rearrange("(c p) b -> p c b", p=128), ys[:], reads=[ys], writes=[y_scr])
                S.dma("pool", z_scr[:, T:TT].rearrange("(c p) b -> p c b", p=128), zs[:], reads=[zs], writes=[z_scr])
            S.mute = STG < 5
            with S.scope():
                wts = Rot([S.sbuf(f"wt{a}", [128, 8, 512], BF16) for a in range(4)])
                yb = S.sbuf("yb", [128, 16, 512]); zb = S.sbuf("zb", [128, 16, 512])
                ybh = S.sbuf("ybh", [128, 16, 512], BF16)
                sq_rot = Rot([S.sbuf(f"sq{a}", [128, 512]) for a in range(2)])
                r_b = S.sbuf("r", [128, 512])
                for tt in range(NQT + 1):
                    N = ntok(tt)
                    S.dma("sp", yb[:, :, 0:N], y_scr[:, tsl(tt)].rearrange("(c p) t -> p c t", p=128), reads=[y_scr], writes=[yb])
                    S.dma("sp", zb[:, :, 0:N], z_scr[:, tsl(tt)].rearrange("(c p) t -> p c t", p=128), reads=[z_scr], writes=[zb])
                    act(zb, zb[:, :, 0:N], zb, zb[:, :, 0:N], AF.Silu)
                    tt_op("dve", yb, yb[:, :, 0:N], yb, yb[:, :, 0:N], zb, zb[:, :, 0:N], ALU.mult)
                    rms_rinv(yb, lambda k, N=N: yb[:, k, 0:N], 16, N, sq_rot, r_b, 2048.0)
                    for k in range(16):
                        stt(ybh, ybh[:, k, 0:N], yb, yb[:, k, 0:N], snw[:, k:k + 1], r_b, r_b[:, 0:N], ALU.mult, ALU.mult, extra_reads=[snw])
                    out_proj(I["ssd_w_out"][j], 16, lambda k, N=N: (ybh, ybh[:, k, 0:N]), tt, wts, 16)
            S.mute = False

    import os
    DBG = os.environ.get("KDBG", "")
    for i in range(NL):
        adaln(i)
        if i % 2 == 0:
            if "nonsa" not in DBG:
                nsa_layer(i, i // 2)
        else:
            ssd_layer(i, i // 2)
        if "nomlp" not in DBG:
            mlp(i)
    with S.scope():
        sq_rot = Rot([S.sbuf(f"sq{a}", [128, 512]) for a in range(2)])
        r_b = S.sbuf("r", [128, 512])
        yo = Rot([S.sbuf(f"yo{a}", [128, 8, 512]) for a in range(2)])
        for tt in range(NQT + 1):
            N = ntok(tt)
            x = xt[tt]
            rms_rinv(x, lambda k, x=x: x[:, k, :], 8, N, sq_rot, r_b, 1024.0)
            y = yo.next()
            for k in range(8):
                stt(y, y[:, k, 0:N], x, x[:, k, :], fw[:, k:k + 1], r_b, r_b[:, 0:N], ALU.mult, ALU.mult, extra_reads=[fw])
            S.dma("pool", O["yT"][:, tsl(tt)].rearrange("(c p) t -> p c t", p=128), y[:, :, 0:N], reads=[y], writes=[O["yT"]])
    S.finish("sp")
    S.emit()
    root.close()
    return nc, consts, S.n_ops


def fm(v, nch):
    return np.ascontiguousarray(np.asarray(v, np.float32).reshape(nch, 128).T)


def prep_shared(inp, cfg, consts):
    NL, NN, NS = cfg.NL, cfg.NN, cfg.NS
    f32 = np.float32
    sh = {}
    sh["ada_w"] = np.ascontiguousarray(inp["ada_w"], f32)
    sh["ada_bT"] = np.stack([fm(inp["ada_b"][l], 48) for l in range(NL)])
    sh["norm_wT"] = np.stack([np.concatenate([fm(inp["norm_w"][l, 0], 8), fm(inp["norm_w"][l, 1], 8)], axis=1) for l in range(NL)])
    sh["mlp_w1"] = np.ascontiguousarray(inp["mlp_w1"], f32)
    sh["mlp_w2"] = np.ascontiguousarray(inp["mlp_w2"], f32)
    sh["final_wT"] = fm(inp["final_norm_w"], 8)
    if NN:
        npool = inp["cache_kv_cmp"].shape[1]
        sh["pool_c"] = np.ascontiguousarray(inp["cache_kv_cmp"], f32).reshape(NN, npool * 128, 512)
        sh["pool_s"] = np.ascontiguousarray(inp["cache_kv_sel"], f32).reshape(NN, npool * 128, 512)
        sh["nsa_w_in"] = np.ascontiguousarray(inp["nsa_w_in"], f32)
        sh["nsa_w_out"] = np.ascontiguousarray(inp["nsa_w_out"], f32)
        pe = np.asarray(inp["nsa_cmp_pe"], f32)
        sh["cmp_pe2"] = np.ascontiguousarray(pe.reshape(NN, 2, 16, 2, 64).transpose(0, 3, 4, 1, 2).reshape(NN, 128, 2, 16))
        w1 = np.asarray(inp["nsa_cmp_w1"], f32)
        sh["cmp_w1std"] = np.ascontiguousarray(w1.reshape(NN, 2, 16, 128, 64).transpose(0, 3, 1, 2, 4))
        w1d = w1.reshape(NN, 2, 32, 64, 64).transpose(0, 3, 1, 2, 4)
        sh["cmp_w1d"] = np.ascontiguousarray(np.concatenate([w1d, w1d], axis=1))
        bd = np.zeros((NN, 128, 2, 32, 128), f32)
        bd[:, 0:64, :, :, 0:64] = w1d
        bd[:, 64:128, :, :, 64:128] = w1d
        sh["cmp_w1bd"] = bd
        sh["cmp_w2"] = np.ascontiguousarray(np.asarray(inp["nsa_cmp_w2"], f32).transpose(0, 2, 1, 3))
    if NS:
        sh["ssd_w_in"] = np.ascontiguousarray(inp["ssd_w_in"], f32)
        cw = np.asarray(inp["ssd_conv_w"], f32)
        sh["ssd_conv_wT"] = np.ascontiguousarray(cw.reshape(NS, 4, 24, 128).transpose(0, 3, 2, 1))
        sh["ssd_conv_bT"] = np.stack([fm(inp["ssd_conv_b"][l], 24) for l in range(NS)])
        sh["ssd_dtb"] = np.ascontiguousarray(np.asarray(inp["ssd_dt_bias"], f32).reshape(NS, 1, 32))
        sh["ssd_alog"] = np.ascontiguousarray(np.asarray(inp["ssd_a_log"], f32).reshape(NS, 1, 32))
        sh["ssd_dtbT"] = np.ascontiguousarray(np.asarray(inp["ssd_dt_bias"], f32).reshape(NS, 32, 1))
        sh["ssd_alogT"] = np.ascontiguousarray(np.asarray(inp["ssd_a_log"], f32).reshape(NS, 32, 1))
        sh["ssd_dT"] = np.stack([fm(np.repeat(np.asarray(inp["ssd_d"][l], f32), 64), 16) for l in range(NS)])
        sh["ssd_norm_wT"] = np.stack([fm(inp["ssd_norm_w"][l], 16) for l in range(NS)])
        sh["ssd_w_out"] = np.ascontiguousarray(inp["ssd_w_out"], f32)
    sh.update(consts)
    return sh


def prep_core(inp, cfg, c):
    f32 = np.float32
    NN, NS = cfg.NN, cfg.NS
    bs = slice(16 * c, 16 * c + 16)
    m = {}
    m["xT"] = np.ascontiguousarray(np.asarray(inp["x_prompt"][c], f32).T)
    m["xsT"] = np.ascontiguousarray(np.asarray(inp["x_sample"][bs, 0], f32).T)
    cc = np.concatenate([np.asarray(inp["c_prompt"][c:c + 1], f32), np.asarray(inp["c_sample"][bs], f32)], axis=0)
    m["cT"] = np.ascontiguousarray(cc.T)
    m["pt"] = np.ascontiguousarray(inp["page_table"][bs], np.int32)
    if NN:
        m["cwin"] = np.ascontiguousarray(np.asarray(inp["cache_kv_win"], f32)[:, bs]).reshape(NN, 16, 512, 512)
    if NS:
        m["sssm"] = np.ascontiguousarray(np.asarray(inp["state_ssm"], f32)[:, bs]).reshape(NS, 16, 2048, 128)
        sc = np.asarray(inp["state_conv"], f32)[:, bs]
        m["sconvT"] = np.ascontiguousarray(sc.transpose(0, 3, 1, 2))
    return m


def assemble(res, cfg, ncores):
    T, NN, NS = cfg.T, cfg.NN, cfg.NS
    f32 = np.float32
    B = ncores
    y_p = np.stack([res[c]["yT"][:, :T].T for c in range(B)]).astype(f32)
    y_s = np.concatenate([res[c]["yT"][:, T:].T for c in range(B)])[:, None, :].astype(f32)

    def kv_p(cn):
        return np.stack([np.stack([res[c][f"kvp_{cn}{j}"].T.reshape(T, 2, 4, 64) for c in range(B)]) for j in range(NN)])

    def kv_s(cn):
        return np.stack([np.concatenate([res[c][f"kvs_{cn}{j}"].T.reshape(16, 1, 2, 4, 64) for c in range(B)]) for j in range(NN)])

    wk = min(512, T)
    win_p = np.stack([np.stack([res[c][f"kvwin_p{j}"].T.reshape(wk, 2, 4, 64) for c in range(B)]) for j in range(NN)])
    win_s = np.stack([np.concatenate([np.concatenate([res[c][f"kvwin_old{j}"].reshape(16, 511, 2, 4, 64),
                                                      res[c][f"kvs_w{j}"].T.reshape(16, 1, 2, 4, 64)], axis=1)
                                      for c in range(B)]) for j in range(NN)])
    outs = [y_p, y_s, kv_p("c"), kv_s("c"), kv_p("s"), kv_s("s"), win_p, win_s]
    if NS:
        ssm_p = np.stack([np.stack([res[c][f"ssmp{j}"].reshape(16, 128, 2, 64).transpose(0, 2, 3, 1).reshape(32, 64, 128)
                                    for c in range(B)]) for j in range(NS)])
        ssm_s = np.stack([np.concatenate([res[c][f"ssms{j}"].reshape(16, 32, 64, 128) for c in range(B)]) for j in range(NS)])
        conv_p = np.stack([np.stack([res[c][f"convp{j}"].T for c in range(B)]) for j in range(NS)])
        conv_s = np.stack([np.concatenate([res[c][f"convs{j}"].transpose(1, 2, 0) for c in range(B)]) for j in range(NS)])
        outs += [ssm_p, ssm_s, conv_p, conv_s]
    return tuple(np.ascontiguousarray(o, dtype=f32) for o in outs)


def run(inp, cfg, ncores):
    nc, consts, _ = build(cfg)
    sh = prep_shared(inp, cfg, consts)
    in_maps = []
    for c in range(ncores):
        m = dict(sh)
        m.update(prep_core(inp, cfg, c))
        in_maps.append(m)
    res = run_bass_kernel_spmd(nc, in_maps, core_ids=list(range(ncores)))
    return assemble(res.results, cfg, ncores)


def kernel(**inputs):
    cfg = Cfg(T=2048, NL=4, NPOOL=int(inputs["cache_kv_cmp"].shape[1]))
    return run(inputs, cfg, 8)
```

```python
import contextlib
import math
import numpy as np
import concourse.bass as bass
import concourse.mybir as mybir
from concourse.bass_utils import run_bass_kernel_spmd

F32 = mybir.dt.float32
BF16 = mybir.dt.bfloat16
I32 = mybir.dt.int32
U32 = mybir.dt.uint32
AF = mybir.ActivationFunctionType
ALU = mybir.AluOpType
AX = mybir.AxisListType

EPOCH = 40000
NDMASEM = 8
NEGB = -1.0e5
SCALE = 0.125
EPS = 1e-6
BIGV = 1e30


class Buf:
    __slots__ = ("t", "name", "w", "rc", "rd", "const", "nowaw", "ws", "psum")

    def __init__(self, t, name):
        self.t = t
        self.name = name
        self.w = None
        self.rc = {}
        self.rd = []
        self.const = False
        self.psum = False
        self.nowaw = False
        self.ws = []

    def __getitem__(self, idx):
        return self.t[idx]


class Q:
    def __init__(self, name):
        self.name = name
        self.ops = []
        self.sems = []
        self.count = 0
        self.known = {}
        self.pending = []
        self.dsems = []
        self.dcount = 0
        self.dlast = {}

    def cur_dep(self):
        if not self.sems or self.count == 0:
            return None
        return (self.name, self.sems[-1], self.count, False)


class Sched:
    def __init__(self, nc, root):
        self.nc = nc
        self.root = root
        self.stack = root
        self.q = {n: Q(n) for n in ("pe", "dve", "act", "pool", "sp")}
        self.bufs = []
        self.n_ops = 0
        self.psums = []
        self.pi = 0
        self.mute = False
        self.pa = 0
        self.uid = 0

    def sem(self, name):
        return self.root.enter_context(self.nc.semaphore(name))

    def sbuf(self, name, shape, dt=F32):
        self.uid += 1
        t = self.stack.enter_context(self.nc.sbuf_tensor(f"{name}_{self.uid}", list(shape), dt))
        b = Buf(t, name)
        self.bufs.append(b)
        return b

    def view(self, ap, name):
        b = Buf(ap, name)
        self.bufs.append(b)
        return b

    def dram(self, name, shape, dt=F32, kind="Internal"):
        t = self.nc.dram_tensor(name, list(shape), dt, kind=kind)
        b = Buf(t.ap(), name)
        b.nowaw = True
        self.bufs.append(b)
        return b

    def init_psum(self):
        for i in range(8):
            t = self.root.enter_context(self.nc.psum_tensor(f"psum{i}", [128, 512], F32))
            b = Buf(t, f"psum{i}")
            b.psum = True
            self.psums.append(b)
            self.bufs.append(b)

    def ps(self):
        b = self.psums[self.pi % 6]
        self.pi += 1
        return b

    def ps_acc(self):
        b = self.psums[6 + self.pa % 2]
        self.pa += 1
        return b

    @contextlib.contextmanager
    def scope(self):
        st = contextlib.ExitStack()
        prev = self.stack
        self.stack = st
        nb = len(self.bufs)
        try:
            yield self
        finally:
            self.barrier()
            del self.bufs[nb:]
            st.close()
            self.stack = prev

    def _need(self, q, deps):
        waits = []
        for d in deps:
            if d is None:
                continue
            qn, sem, val, is_dma = d
            if qn == q.name and not is_dma and q.name == "pe":
                continue
            k = id(sem)
            if q.known.get(k, 0) >= val:
                continue
            q.known[k] = val
            waits.append((sem, val))
        return waits

    def _gather(self, reads, writes, qn=None):
        deps = []
        for b in reads:
            if b.const:
                continue
            if b.nowaw:
                deps.extend(b.ws)
            else:
                deps.append(b.w)
            if b.psum:
                deps.extend(d for k, d in b.rc.items() if k != qn)
        for b in writes:
            if not b.nowaw:
                deps.append(b.w)
            deps.extend(b.rc.values())
            deps.extend(b.rd)
        return deps

    def _commit(self, dep, reads, writes, is_dma):
        for b in reads:
            if b.const:
                continue
            if is_dma:
                b.rd.append(dep)
            else:
                b.rc[dep[0]] = dep
        for b in writes:
            if b.nowaw:
                b.ws.append(dep)
            else:
                b.w = dep
            b.rc = {}
            b.rd = []

    def op(self, qn, fn, reads=(), writes=()):
        if self.mute:
            return None
        q = self.q[qn]
        deps = self._gather(reads, writes, qn)
        waits = q.pending + self._need(q, deps)
        q.pending = []
        if not q.sems or q.count >= EPOCH:
            q.sems.append(self.sem(f"{qn}_e{len(q.sems)}"))
            q.count = 0
        q.count += 1
        sem = q.sems[-1]
        dep = (qn, sem, q.count, False)
        q.ops.append((waits, fn, (sem, 1)))
        self._commit(dep, reads, writes, False)
        self.n_ops += 1
        return dep

    def dma(self, qn, out, in_, reads=(), writes=(), fn=None):
        if self.mute:
            return None
        q = self.q[qn]
        deps = self._gather(reads, writes)
        if not q.dsems:
            q.dsems = [self.sem(f"{qn}_d{i}") for i in range(NDMASEM)]
        slot = q.dcount % NDMASEM
        sem = q.dsems[slot]
        val = 16 * (q.dcount // NDMASEM + 1)
        prev = q.dlast.get(slot)
        if prev is not None:
            deps.append(prev)
        waits = q.pending + self._need(q, deps)
        q.pending = []
        q.dcount += 1
        dep = (qn, sem, val, True)
        q.dlast[slot] = dep
        if fn is None:
            def fn(e, out=out, in_=in_):
                return e.dma_start(out=out, in_=in_)
        q.ops.append((waits, fn, (sem, 16)))
        self._commit(dep, reads, writes, True)
        self.n_ops += 1
        return dep

    def all_deps(self):
        deps = []
        for q in self.q.values():
            deps.append(q.cur_dep())
            deps.extend(q.dlast.values())
        return deps

    def barrier(self):
        deps = self.all_deps()
        for q in self.q.values():
            q.pending = q.pending + self._need(q, deps)
        for b in self.bufs:
            b.w = None
            b.ws = []
            b.rc = {}
            b.rd = []

    def finish(self, qn="sp"):
        q = self.q[qn]
        waits = q.pending + self._need(q, self.all_deps())
        q.pending = []
        q.ops.append((waits, None, None))

    def emit(self):
        nc = self.nc
        with nc.Block() as block:
            def run(q):
                def body(e):
                    for waits, fn, inc in q.ops:
                        for sem, val in waits:
                            e.wait_ge(sem, val)
                        if fn is not None:
                            ins = fn(e)
                            ins.then_inc(inc[0], inc[1])
                    for sem, val in q.pending:
                        e.wait_ge(sem, val)
                return body
            block.sync(run(self.q["sp"]))
            block.tensor(run(self.q["pe"]))
            block.vector(run(self.q["dve"]))
            block.scalar(run(self.q["act"]))
            block.gpsimd(run(self.q["pool"]))


class Rot:
    def __init__(self, items):
        self.items = items
        self.i = 0

    def next(self):
        b = self.items[self.i % len(self.items)]
        self.i += 1
        return b


class Cfg:
    def __init__(self, T=2048, NL=4, NPOOL=2560, dbg=False):
        self.T = T
        self.NL = NL
        self.NN = (NL + 1) // 2
        self.NS = NL // 2
        self.NPOOL = NPOOL
        self.TT = T + 16
        self.NQT = T // 512
        self.NKC = T // 128
        self.NCB = (T - 32) // 16 + 1
        self.NSEL = T // 64
        self.NCH = T // 256
        self.dbg = dbg


def make_consts(cfg):
    T = cfg.T
    c = {}
    c["c_ident"] = np.eye(128, dtype=np.float32)
    c["c_ones"] = np.ones((128, 128), np.float32)
    i = np.arange(128)[:, None]
    t = np.arange(T)[None, :]
    cmpb = np.where((16 * i + 31 <= t) & (i < cfg.NCB), 0.0, NEGB).astype(np.float32)
    c["c_cmpb"] = cmpb
    p = np.arange(128)[:, None]
    f = np.arange(1024)[None, :]
    c["c_cb"] = np.where(f - p >= 512, 0.0, NEGB).astype(np.float32)
    f = np.arange(896)[None, :]
    c["c_wb"] = np.where(f - p <= 384, 0.0, NEGB).astype(np.float32)
    e = np.zeros((32, cfg.NKC, 128), np.float32)
    for kc in range(cfg.NKC):
        for pp in range(128):
            j = 2 * kc + pp // 64
            if j < 32:
                e[j, kc, pp] = 1.0
    c["c_eall"] = e
    ovl = np.zeros((128, 32), np.float32)
    cs = np.arange(cfg.NCB) * 16
    ss = np.arange(cfg.NSEL) * 64
    o = (cs[:, None] < ss[None, :] + 64) & (cs[:, None] + 32 > ss[None, :])
    ovl[:cfg.NCB, :cfg.NSEL] = o
    c["c_ovl"] = ovl
    ovs = np.zeros((128, 32), np.float32)
    cs = np.arange(127) * 16
    ss = np.arange(32) * 64
    ovs[:127, :] = (cs[:, None] < ss[None, :] + 64) & (cs[:, None] + 32 > ss[None, :])
    c["c_ovs"] = ovs
    nq = T // 128
    tt = (np.arange(nq)[None, :, None] * 128 + np.arange(128)[:, None, None])
    j = np.arange(32)[None, None, :]
    valid = (j < cfg.NSEL) & (j * 64 <= tt)
    cur = tt // 64
    forced = (j == 0) | (j == cur) | (j == cur - 1)
    c["c_m1"] = (valid & ~forced).astype(np.float32)
    c["c_m2"] = np.where(~valid, -BIGV, np.where(forced, BIGV, 0.0)).astype(np.float32)
    c["c_valid"] = valid.astype(np.float32)
    jj = np.arange(128)[:, None]
    ll = np.arange(128)[None, :]
    tri = (jj <= ll).astype(np.float32)
    c["c_tri"] = tri
    tf = np.zeros((128, 2, 256), np.float32)
    tf[:, 0, :128] = tri
    tf[:, 0, 128:] = 1.0
    tf[:, 1, 128:] = tri
    c["c_trifull"] = tf
    ca = np.zeros((128, 2, 256), np.float32)
    l = np.arange(256)[None, :]
    for st in range(2):
        s = st * 128 + np.arange(128)[:, None]
        ca[:, st, :] = (l >= s)
    c["c_causal01"] = ca
    e2 = np.zeros((8, 4, 128), np.float32)
    for m in range(4):
        for pp in range(128):
            e2[2 * m + pp // 64, m, pp] = 1.0
    c["c_e2"] = e2
    c["c_p64"] = (np.arange(128) % 64).astype(np.float32).reshape(128, 1)
    c["c_p128"] = np.arange(128).astype(np.float32).reshape(128, 1)
    c["c_iota32"] = np.tile(np.arange(32, dtype=np.float32)[None, :], (64, 1))
    negh = np.zeros((128, 4), np.float32)
    negh[64:, :] = NEGB
    c["c_negh"] = negh
    selb = np.zeros((16, 16, 128), np.float32)
    for b in range(16):
        selb[b, b, :] = 1.0
    c["c_selb"] = selb
    ehp = np.zeros((32, 16, 128), np.float32)
    for hp in range(16):
        for pp in range(128):
            ehp[2 * hp + pp // 64, hp, pp] = 1.0
    c["c_ehp"] = ehp
    m1s = np.ones((64, 32), np.float32)
    m1s[:, 0] = 0
    m1s[:, 31] = 0
    m2s = np.zeros((64, 32), np.float32)
    m2s[:, 0] = BIGV
    m2s[:, 31] = BIGV
    c["c_m1s"] = m1s
    c["c_m2s"] = m2s
    return c


def build(cfg):
    T, TT, NQT, NKC, NCB, NL, NN, NS = cfg.T, cfg.TT, cfg.NQT, cfg.NKC, cfg.NCB, cfg.NL, cfg.NN, cfg.NS
    NQ128 = T // 128
    NROWS = cfg.NPOOL * 128
    nc = bass.Bass("TRN2", target_bir_lowering=False)
    root = contextlib.ExitStack()
    S = Sched(nc, root)
    S.init_psum()

    I = {}
    O = {}

    def inp(name, shape, dt=F32):
        I[name] = nc.dram_tensor(name, list(shape), dt, kind="ExternalInput").ap()
        return I[name]

    def outp(name, shape, dt=F32):
        b = S.dram(name, shape, dt, kind="ExternalOutput")
        O[name] = b
        return b

    inp("xT", [1024, T]); inp("xsT", [1024, 16]); inp("cT", [1024, 17]); inp("pt", [16, 16], I32)
    inp("ada_w", [NL, 1024, 6144]); inp("ada_bT", [NL, 128, 48]); inp("norm_wT", [NL, 128, 16])
    inp("mlp_w1", [NL, 1024, 4096]); inp("mlp_w2", [NL, 4096, 1024]); inp("final_wT", [128, 8])
    if NN:
        inp("cwin", [NN, 16, 512, 512]); inp("pool_c", [NN, NROWS, 512]); inp("pool_s", [NN, NROWS, 512])
        inp("nsa_w_in", [NN, 1024, 2608]); inp("nsa_w_out", [NN, 1024, 1024])
        inp("cmp_pe2", [NN, 128, 2, 16]); inp("cmp_w1std", [NN, 128, 2, 16, 64]); inp("cmp_w1d", [NN, 128, 2, 32, 64]); inp("cmp_w1bd", [NN, 128, 2, 32, 128])
        inp("cmp_w2", [NN, 64, 2, 64])
    if NS:
        inp("sssm", [NS, 16, 2048, 128]); inp("sconvT", [NS, 3072, 16, 3])
        inp("ssd_w_in", [NS, 1024, 5152]); inp("ssd_conv_wT", [NS, 128, 24, 4]); inp("ssd_conv_bT", [NS, 128, 24])
        inp("ssd_dtb", [NS, 1, 32]); inp("ssd_alog", [NS, 1, 32]); inp("ssd_dT", [NS, 128, 16])
        inp("ssd_dtbT", [NS, 32, 1]); inp("ssd_alogT", [NS, 32, 1])
        inp("ssd_norm_wT", [NS, 128, 16]); inp("ssd_w_out", [NS, 2048, 1024])
    consts = make_consts(cfg)
    for k, v in consts.items():
        inp(k, list(v.shape))

    outp("yT", [1024, TT])
    for j in range(NN):
        for cn in ("c", "s", "w"):
            outp(f"kvp_{cn}{j}", [512, T])
            outp(f"kvs_{cn}{j}", [512, 16])
        outp(f"kvwin_p{j}", [512, min(512, T)])
        outp(f"kvwin_old{j}", [16, 511, 512])
    for j in range(NS):
        outp(f"ssmp{j}", [16, 128, 128])
        outp(f"ssms{j}", [16, 2048, 128])
        outp(f"convp{j}", [3072, 3])
        outp(f"convs{j}", [3072, 16, 3])
    qT_scr = S.dram("qT_scr", [1024, T])
    g_scr = S.dram("g_scr", [48, T])
    vtok_scr = [S.dram(f"vtok_scr{a}", [T, 256], BF16) for a in range(2)]
    oT_scr = S.dram("oT_scr", [1024, T], BF16)
    z_scr = S.dram("z_scr", [2048, TT])
    y_scr = S.dram("y_scr", [2048, TT])
    xbc_scr = S.dram("xbc_scr", [3072, T])
    xbcA_scr = S.dram("xbcA_scr", [3072, T])

    def ntok(tt):
        return 512 if tt < NQT else 16

    def tsl(tt):
        return slice(tt * 512, tt * 512 + ntok(tt))

    xT = S.sbuf("xT", [128, 8, TT])
    xt = [S.view(xT.t[:, :, tsl(tt)], f"xt{tt}") for tt in range(NQT + 1)]
    ident = S.sbuf("ident", [128, 128]); ones = S.sbuf("ones", [128, 128])
    identb = S.sbuf("identb", [128, 128], BF16)
    scT = S.sbuf("scT", [128, 8, 17]); scTb = S.sbuf("scTb", [128, 8, 17], BF16)
    mod = S.sbuf("mod", [128, 48, 17])
    A1 = S.sbuf("A1", [128, 8, 17]); A2 = S.sbuf("A2", [128, 8, 17])
    nw = S.sbuf("nw", [128, 16]); adab = S.sbuf("adab", [128, 48])
    fw = S.sbuf("fw", [128, 8])
    eps_t = S.sbuf("eps_t", [128, 1])

    S.dma("sp", xT[:, :, 0:T], I["xT"].rearrange("(c p) t -> p c t", p=128), writes=[xT])
    S.dma("sp", xT[:, :, T:TT], I["xsT"].rearrange("(c p) t -> p c t", p=128), writes=[xT])
    S.dma("sp", scT[:], I["cT"].rearrange("(c p) t -> p c t", p=128), writes=[scT])
    S.dma("sp", ident[:], I["c_ident"], writes=[ident])
    S.dma("sp", ones[:], I["c_ones"], writes=[ones])
    S.dma("sp", fw[:], I["final_wT"], writes=[fw])
    S.op("dve", lambda e: e.memset(eps_t[:], EPS), writes=[eps_t])
    S.op("act", lambda e: e.activation(out=scT[:], in_=scT[:], func=AF.Silu), reads=[scT], writes=[scT])
    S.op("dve", lambda e: e.tensor_copy(out=scTb[:], in_=scT[:]), reads=[scT], writes=[scTb])
    S.op("dve", lambda e: e.tensor_copy(out=identb[:], in_=ident[:]), reads=[ident], writes=[identb])
    S.barrier()
    ident.const = True
    identb.const = True
    ones.const = True
    scTb.const = True

    def mm(out_b, out_ap, l_b, l_ap, r_b, r_ap, start=True, stop=True):
        S.op("pe", lambda e: e.matmul(out_ap, lhsT=l_ap, rhs=r_ap, start=start, stop=stop),
             reads=[l_b, r_b], writes=[out_b])

    def tr(out_b, out_ap, in_b, in_ap, np_in):
        S.op("pe", lambda e: e.transpose(out=out_ap, in_=in_ap, identity=ident[0:np_in, 0:np_in]),
             reads=[in_b], writes=[out_b])

    def act(out_b, out_ap, in_b, in_ap, func, bias=None, scale=1.0, extra_reads=()):
        if bias is None:
            S.op("act", lambda e: e.activation(out=out_ap, in_=in_ap, func=func, scale=scale),
                 reads=[in_b, *extra_reads], writes=[out_b])
        else:
            S.op("act", lambda e: e.activation(out=out_ap, in_=in_ap, func=func, bias=bias, scale=scale),
                 reads=[in_b, *extra_reads], writes=[out_b])

    def tt_op(eng, out_b, out_ap, a_b, a_ap, b_b, b_ap, op):
        S.op(eng, lambda e: e.tensor_tensor(out=out_ap, in0=a_ap, in1=b_ap, op=op), reads=[a_b, b_b], writes=[out_b])

    def ts_op(eng, out_b, out_ap, a_b, a_ap, s1, s2, op0, op1=None, extra_reads=()):
        if op1 is None:
            S.op(eng, lambda e: e.tensor_scalar(out=out_ap, in0=a_ap, scalar1=s1, scalar2=None, op0=op0),
                 reads=[a_b, *extra_reads], writes=[out_b])
        else:
            S.op(eng, lambda e: e.tensor_scalar(out=out_ap, in0=a_ap, scalar1=s1, scalar2=s2, op0=op0, op1=op1),
                 reads=[a_b, *extra_reads], writes=[out_b])

    def stt(out_b, out_ap, a_b, a_ap, scalar, b_b, b_ap, op0, op1, extra_reads=(), accum=None):
        if accum is None:
            S.op("dve", lambda e: e.scalar_tensor_tensor(out=out_ap, in0=a_ap, scalar=scalar, in1=b_ap, op0=op0, op1=op1),
                 reads=[a_b, b_b, *extra_reads], writes=[out_b])
        else:
            ab, aap = accum
            S.op("dve", lambda e: e.scalar_tensor_tensor(out=out_ap, in0=a_ap, scalar=scalar, in1=b_ap, op0=op0, op1=op1,
                                                         accum_out=aap),
                 reads=[a_b, b_b, *extra_reads], writes=[out_b, ab])

    def cp(eng, out_b, out_ap, in_b, in_ap):
        if eng == "act":
            S.op("act", lambda e: e.copy(out=out_ap, in_=in_ap), reads=[in_b], writes=[out_b])
        else:
            S.op(eng, lambda e: e.tensor_copy(out=out_ap, in_=in_ap), reads=[in_b], writes=[out_b])

    def recip(out_b, out_ap, in_b, in_ap):
        S.op("dve", lambda e: e.reciprocal(out=out_ap, in_=in_ap), reads=[in_b], writes=[out_b])

    def load_w(wt, W_ap, r0, nrow, c0, ncol):
        S.dma("pool", wt[:, 0:nrow // 128, 0:ncol], W_ap[r0:r0 + nrow, c0:c0 + ncol].rearrange("(c p) n -> p c n", p=128),
              writes=[wt])

    def rms_rinv(src_b, src_ap_fn, KC, N, sq_rot, r_b, denom):
        ps = S.ps()
        for k in range(KC):
            sq = sq_rot.next()
            S.op("act", lambda e, k=k, sq=sq: e.activation(out=sq[:, 0:N], in_=src_ap_fn(k), func=AF.Square),
                 reads=[src_b], writes=[sq])
            mm(ps, ps[:, 0:N], ones, ones[:, :], sq, sq[:, 0:N], start=(k == 0), stop=(k == KC - 1))
        act(r_b, r_b[:, 0:N], ps, ps[:, 0:N], AF.Sqrt, bias=eps_t[:, 0:1], scale=1.0 / denom, extra_reads=[eps_t])
        recip(r_b, r_b[:, 0:N], r_b, r_b[:, 0:N])

    def modulate(tt, A_b, Bsl, h, sq_rot, r_b):
        N = ntok(tt)
        x = xt[tt]
        rms_rinv(x, lambda k: x[:, k, :], 8, N, sq_rot, r_b, 1024.0)
        if tt < NQT:
            for k in range(8):
                tm = S._mtmp.next()
                stt(tm, tm[:, 0:N], x, x[:, k, :], A_b[:, k, 0:1], r_b, r_b[:, 0:N], ALU.mult, ALU.mult, extra_reads=[A_b])
                act(h, h[:, k, 0:N], tm, tm[:, 0:N], AF.Identity, bias=mod[:, Bsl + k, 0:1], extra_reads=[mod])
        else:
            tm = S._mtmp16
            tt_op("dve", tm, tm[:, :, 0:N], x, x[:, :, :], r_b, r_b[:, 0:N].unsqueeze(1).to_broadcast([128, 8, N]), ALU.mult)
            tt_op("dve", tm, tm[:, :, 0:N], tm, tm[:, :, 0:N], A_b, A_b[:, :, 1:17], ALU.mult)
            tt_op("dve", h, h[:, :, 0:N], tm, tm[:, :, 0:N], mod, mod[:, Bsl:Bsl + 8, 1:17], ALU.add)

    def resid_update(tt, kchunk, ps, N, Gsl):
        x = xt[tt]
        if tt < NQT:
            stt(x, x[:, kchunk, :], ps, ps[:, 0:N], mod[:, Gsl + kchunk, 0:1], x, x[:, kchunk, :], ALU.mult, ALU.add,
                extra_reads=[mod])
        else:
            tmp = S._tmp16.next()
            tt_op("dve", tmp, tmp[:, 0:N], ps, ps[:, 0:N], mod, mod[:, Gsl + kchunk, 1:17], ALU.mult)
            tt_op("dve", x, x[:, kchunk, :], x, x[:, kchunk, :], tmp, tmp[:, 0:N], ALU.add)

    S._tmp16 = Rot([S.sbuf(f"tmp16_{a}", [128, 16]) for a in range(4)])
    S._mtmp = Rot([S.sbuf(f"mtmp{a}", [128, 512]) for a in range(3)])
    S._mtmp16 = S.sbuf("mtmp16", [128, 8, 16])

    def out_proj(W_ap, KC, src_fn, tt, wts, Gsl):
        N = ntok(tt)
        for blk in range(2):
            pss = [S.ps() for _ in range(4)]
            nkg = (KC + 7) // 8
            for kg in range(nkg):
                kn = min(8, KC - kg * 8)
                wt = wts.next()
                load_w(wt, W_ap, kg * 1024, kn * 128, blk * 512, 512)
                for jc in range(4):
                    for k in range(kn):
                        sb, sap = src_fn(kg * 8 + k)
                        mm(pss[jc], pss[jc][:, 0:N], wt, wt[:, k, jc * 128:(jc + 1) * 128], sb, sap,
                           start=(kg == 0 and k == 0), stop=(kg == nkg - 1 and k == kn - 1))
            for jc in range(4):
                resid_update(tt, blk * 4 + jc, pss[jc], N, Gsl)

    def adaln(i):
        with S.scope():
            wts = Rot([S.sbuf(f"wt{a}", [128, 8, 512], BF16) for a in range(4)])
            S.dma("sp", adab[:], I["ada_bT"][i], writes=[adab])
            S.dma("sp", nw[:], I["norm_wT"][i], writes=[nw])
            for blk in range(12):
                wt = wts.next()
                load_w(wt, I["ada_w"][i], 0, 1024, blk * 512, 512)
                ps = S.ps()
                for jc in range(4):
                    for k in range(8):
                        mm(ps, ps[:, jc * 17:(jc + 1) * 17], wt, wt[:, k, jc * 128:(jc + 1) * 128], scTb, scTb[:, k, :],
                           start=(k == 0), stop=(k == 7))
                tt_op("dve", mod, mod[:, blk * 4:(blk + 1) * 4, :], ps, ps[:, 0:68].rearrange("p (a b) -> p a b", a=4),
                      adab, adab[:, blk * 4:(blk + 1) * 4].unsqueeze(2).to_broadcast([128, 4, 17]), ALU.add)
            for (Ab, sc0, w0) in ((A1, 8, 0), (A2, 32, 8)):
                ts_op("dve", Ab, Ab[:], mod, mod[:, sc0:sc0 + 8, :], 1.0, None, ALU.add)
                tt_op("dve", Ab, Ab[:], Ab, Ab[:], nw, nw[:, w0:w0 + 8].unsqueeze(2).to_broadcast([128, 8, 17]), ALU.mult)

    def mlp(i):
        with S.scope():
            wts = Rot([S.sbuf(f"wt{a}", [128, 8, 512], BF16) for a in range(4)])
            hb = Rot([S.sbuf(f"h{a}", [128, 8, 512], BF16) for a in range(2)])
            hid = S.sbuf("hid", [128, 32, 512], BF16)
            rls = Rot([S.sbuf(f"rl{a}", [128, 512]) for a in range(3)])
            sq_rot = Rot([S.sbuf(f"sq{a}", [128, 512]) for a in range(2)])
            r_b = S.sbuf("r", [128, 512])
            for tt in range(NQT + 1):
                N = ntok(tt)
                h = hb.next()
                modulate(tt, A2, 24, h, sq_rot, r_b)
                for blk in range(8):
                    wt = wts.next()
                    load_w(wt, I["mlp_w1"][i], 0, 1024, blk * 512, 512)
                    for jc in range(4):
                        ps = S.ps()
                        for k in range(8):
                            mm(ps, ps[:, 0:N], wt, wt[:, k, jc * 128:(jc + 1) * 128], h, h[:, k, 0:N], start=(k == 0), stop=(k == 7))
                        c = blk * 4 + jc
                        rl = rls.next()
                        act(rl, rl[:, 0:N], ps, ps[:, 0:N], AF.Relu)
                        tt_op("dve", hid, hid[:, c, 0:N], rl, rl[:, 0:N], rl, rl[:, 0:N], ALU.mult)
                out_proj(I["mlp_w2"][i], 32, lambda k, N=N: (hid, hid[:, k, 0:N]), tt, wts, 40)

    def nsa_layer(i, j):
        kvp = [O[f"kvp_{cn}{j}"] for cn in ("c", "s", "w")]
        kvs_o = [O[f"kvs_{cn}{j}"] for cn in ("c", "s", "w")]
        W_in = I["nsa_w_in"][j]
        with S.scope():
            qsT = S.sbuf("qsT", [128, 8, 16])
            kvs = S.sbuf("kvs", [128, 12, 16])
            sigGs = S.sbuf("sigGs", [48, 16])
            c0 = S.sbuf("c0", [64, 2]); c0d = S.sbuf("c0d", [128, 2]); w2d = S.sbuf("w2d", [128, 2, 64])
            w2 = S.sbuf("w2", [64, 2, 64]); w2dup = S.sbuf("w2dup", [64, 128])
            oTs = S.sbuf("oTs", [128, 8, 16], BF16)
            S.dma("sp", w2[:], I["cmp_w2"][j], writes=[w2])
            S.dma("sp", w2dup[:, 0:64], I["cmp_w2"][j][:, 0, :], writes=[w2dup])
            S.dma("sp", w2dup[:, 64:128], I["cmp_w2"][j][:, 0, :], writes=[w2dup])
            S.dma("sp", w2d[0:64], I["cmp_w2"][j], writes=[w2d])
            S.dma("sp", w2d[64:128], I["cmp_w2"][j], writes=[w2d])
            with S.scope():
                wts = Rot([S.sbuf(f"wt{a}", [128, 8, 512], BF16) for a in range(4)])
                hb = Rot([S.sbuf(f"h{a}", [128, 8, 512], BF16) for a in range(2)])
                sq_rot = Rot([S.sbuf(f"sq{a}", [128, 512]) for a in range(2)])
                r_b = S.sbuf("r", [128, 512])
                evs = Rot([S.sbuf(f"ev{a}", [128, 512]) for a in range(6)])
                vts = Rot([S.sbuf(f"vt{a}", [128, 256], BF16) for a in range(3)])
                w1s = S.sbuf("w1s", [128, 2, 16, 64]); pe2 = S.sbuf("pe2", [128, 2, 16])
                S.dma("sp", w1s[:], I["cmp_w1std"][j], writes=[w1s])
                S.dma("sp", pe2[:], I["cmp_pe2"][j], writes=[pe2])
                for kv in range(2):
                    ps = S.ps()
                    for cch in range(16):
                        mm(ps, ps[0:64, 0:1], w1s, w1s[:, kv, cch, :], pe2, pe2[:, kv, cch:cch + 1], start=(cch == 0), stop=(cch == 15))
                    cp("dve", c0, c0[:, kv:kv + 1], ps, ps[0:64, 0:1])
                S.dma("sp", c0d[0:64, :], c0[:, :], reads=[c0], writes=[c0d])
                S.dma("sp", c0d[64:128, :], c0[:, :], reads=[c0], writes=[c0d])
                for tt in range(NQT + 1):
                    N = ntok(tt)
                    prompt = tt < NQT
                    h = hb.next()
                    modulate(tt, A1, 0, h, sq_rot, r_b)
                    for blk in range(6):
                        ncol = min(512, 2608 - blk * 512)
                        wt = wts.next()
                        load_w(wt, W_in, 0, 1024, blk * 512, ncol)
                        for jc in range((ncol + 127) // 128):
                            cw = min(128, ncol - jc * 128)
                            ps = S.ps()
                            for k in range(8):
                                mm(ps, ps[0:cw, 0:N], wt, wt[:, k, jc * 128:jc * 128 + cw], h, h[:, k, 0:N], start=(k == 0), stop=(k == 7))
                            gc = blk * 4 + jc
                            if gc < 8:
                                if prompt:
                                    ev = evs.next()
                                    cp("act", ev, ev[:, 0:N], ps, ps[:, 0:N])
                                    S.dma("sp", qT_scr[gc * 128:(gc + 1) * 128, tsl(tt)], ev[:, 0:N], reads=[ev], writes=[qT_scr])
                                else:
                                    cp("act", qsT, qsT[:, gc, :], ps, ps[:, 0:N])
                            elif gc < 20:
                                cache = (gc - 8) // 4
                                rr = (gc - 8) % 4
                                if prompt:
                                    ev = evs.next()
                                    cp("act", ev, ev[:, 0:N], ps, ps[:, 0:N])
                                    S.dma("sp", kvp[cache][rr * 128:(rr + 1) * 128, tsl(tt)], ev[:, 0:N], reads=[ev], writes=[kvp[cache]])
                                    if cache == 2 and tt * 512 >= T - 512:
                                        wo = O[f"kvwin_p{j}"]
                                        c_off = tt * 512 - max(0, T - 512)
                                        S.dma("sp", wo[rr * 128:(rr + 1) * 128, c_off:c_off + N], ev[:, 0:N], reads=[ev], writes=[wo])
                                else:
                                    cp("act", kvs, kvs[:, gc - 8, :], ps, ps[:, 0:N])
                            else:
                                if prompt:
                                    ev = evs.next()
                                    act(ev, ev[0:48, 0:N], ps, ps[0:48, 0:N], AF.Sigmoid)
                                    S.dma("sp", g_scr[:, tsl(tt)], ev[0:48, 0:N], reads=[ev], writes=[g_scr])
                                else:
                                    act(sigGs, sigGs[:, :], ps, ps[0:48, 0:N], AF.Sigmoid)
                        if prompt and blk in (3, 4):
                            for sub in range(4):
                                ps = S.ps()
                                for k in range(8):
                                    mm(ps, ps[:, 0:256], h, h[:, k, sub * 128:(sub + 1) * 128], wt, wt[:, k, 256:512], start=(k == 0), stop=(k == 7))
                                vt = vts.next()
                                cp("dve", vt, vt[:, :], ps, ps[:, 0:256])
                                r0 = tt * 512 + sub * 128
                                S.dma("sp", vtok_scr[blk - 3][r0:r0 + 128, :], vt[:, :], reads=[vt], writes=[vtok_scr[blk - 3]])
                for cache in range(3):
                    S.dma("sp", kvs_o[cache][:, :].rearrange("(c p) b -> p c b", p=128), kvs[:, cache * 4:(cache + 1) * 4, :],
                          reads=[kvs], writes=[kvs_o[cache]])
                wold = O[f"kvwin_old{j}"]
                for b in range(16):
                    S.dma("sp", wold[b], I["cwin"][j, b, 1:512, :], writes=[wold])
            for g in range(4):
                nsa_group(j, g, kvp, c0, w2, w2dup)
            nsa_sample(j, qsT, kvs, sigGs, c0d, w2d, oTs)
            with S.scope():
                wts = Rot([S.sbuf(f"wt{a}", [128, 8, 512], BF16) for a in range(4)])
                ots = Rot([S.sbuf(f"ot{a}", [128, 8, 512], BF16) for a in range(2)])
                for tt in range(NQT + 1):
                    N = ntok(tt)
                    if tt < NQT:
                        ot = ots.next()
                        S.dma("sp", ot[:, :, 0:N], oT_scr[:, tsl(tt)].rearrange("(c p) t -> p c t", p=128), reads=[oT_scr], writes=[ot])
                        out_proj(I["nsa_w_out"][j], 8, lambda k, ot=ot, N=N: (ot, ot[:, k, 0:N]), tt, wts, 16)
                    else:
                        out_proj(I["nsa_w_out"][j], 8, lambda k: (oTs, oTs[:, k, :]), tt, wts, 16)

    def nsa_group(j, g, kvp, c0, w2, w2dup):
        with S.scope():
            acc = S.sbuf("acc", [128, 2, T])
            sigG = S.sbuf("sigG", [48, T])
            kcT = S.sbuf("kcT", [128, 128]); vc = S.sbuf("vc", [128, 64])
            selq = S.sbuf("selq", [128, T], BF16)
            pnsum = S.sbuf("pnsum", [128, T])
            S.dma("sp", sigG[:], g_scr[:, :], reads=[g_scr], writes=[sigG])
            with S.scope():
                w1 = S.sbuf("w1", [64, 2, 32, 64])
                KV = S.sbuf("kvc", [64, 2, T])
                hid = S.sbuf("hid", [64, 2, 128])
                S.dma("sp", w1[:], I["cmp_w1d"][j][0:64], writes=[w1])
                S.dma("sp", KV[:, 0, :], kvp[0][g * 64:(g + 1) * 64, :], reads=[kvp[0]], writes=[KV])
                S.dma("sp", KV[:, 1, :], kvp[0][256 + g * 64:256 + (g + 1) * 64, :], reads=[kvp[0]], writes=[KV])
                for kv in range(2):
                    ps = S.ps()
                    for l in range(32):
                        mm(ps, ps[0:64, 0:NCB], w1, w1[:, kv, l, :], KV, KV[:, kv, l:l + 16 * (NCB - 1) + 1:16], start=(l == 0), stop=(l == 31))
                    act(hid, hid[:, kv, 0:NCB], ps, ps[0:64, 0:NCB], AF.Silu, bias=c0[:, kv:kv + 1], extra_reads=[c0])
                ps = S.ps()
                mm(ps, ps[:, 0:NCB], w2dup, w2dup[:, :], hid, hid[:, 0, 0:NCB])
                cp("dve", kcT, kcT[:, 0:NCB], ps, ps[:, 0:NCB])
                ps = S.ps()
                mm(ps, ps[0:NCB, 0:64], hid, hid[:, 1, 0:NCB], w2, w2[:, 1, :])
                cp("dve", vc, vc[0:NCB, :], ps, ps[0:NCB, 0:64])
            with S.scope():
                qb = [S.sbuf(f"q{a}", [128, T]) for a in range(2)]
                cmpb = S.sbuf("cmpb", [128, T])
                Pts = Rot([S.sbuf(f"P{a}", [128, 512]) for a in range(2)])
                rDs = Rot([S.sbuf(f"rD{a}", [128, 512]) for a in range(2)])
                pns = Rot([S.sbuf(f"pn{a}", [128, 512]) for a in range(2)])
                ogs = Rot([S.sbuf(f"og{a}", [64, 512]) for a in range(2)])
                for c2 in range(2):
                    S.dma("sp", qb[c2][:], qT_scr[(2 * g + c2) * 128:(2 * g + c2 + 1) * 128, :], reads=[qT_scr], writes=[qb[c2]])
                S.dma("sp", cmpb[:], I["c_cmpb"], writes=[cmpb])
                for qt in range(NQT):
                    qs = slice(qt * 512, (qt + 1) * 512)
                    for hh in range(4):
                        c2 = hh // 2
                        hb_ = 64 * (hh % 2)
                        h = 4 * g + hh
                        ps = S.ps()
                        mm(ps, ps[0:NCB, :], kcT, kcT[hb_:hb_ + 64, 0:NCB], qb[c2], qb[c2][hb_:hb_ + 64, qs], start=True, stop=False)
                        mm(ps, ps[0:NCB, :], ident, ident[0:NCB, 0:NCB], cmpb, cmpb[0:NCB, qs], start=False, stop=True)
                        Pt = Pts.next()
                        act(Pt, Pt[0:NCB, :], ps, ps[0:NCB, :], AF.Exp, scale=SCALE)
                        psd = S.ps()
                        mm(psd, psd[:, :], ones, ones[0:NCB, :], Pt, Pt[0:NCB, :])
                        rD = rDs.next()
                        ts_op("dve", rD, rD[:, :], psd, psd[:, :], 1e-30, None, ALU.max)
                        recip(rD, rD[:, :], rD, rD[:, :])
                        pn = pns.next()
                        tt_op("dve", pn, pn[0:NCB, :], Pt, Pt[0:NCB, :], rD, rD[0:NCB, :], ALU.mult)
                        if hh == 0:
                            cp("pool", pnsum, pnsum[0:NCB, qs], pn, pn[0:NCB, :])
                        else:
                            tt_op("pool", pnsum, pnsum[0:NCB, qs], pnsum, pnsum[0:NCB, qs], pn, pn[0:NCB, :], ALU.add)
                        pso = S.ps()
                        mm(pso, pso[0:64, :], vc, vc[0:NCB, 0:64], pn, pn[0:NCB, :])
                        psg = S.ps()
                        gi = h * 3 + 0
                        mm(psg, psg[0:64, :], ident, ident[0:48, gi:gi + 1].to_broadcast([48, 64]), sigG, sigG[0:48, qs])
                        og = ogs.next()
                        cp("act", og, og[:, :], psg, psg[0:64, :])
                        tt_op("dve", acc, acc[hb_:hb_ + 64, c2, qs], pso, pso[0:64, :], og, og[:, :], ALU.mult)
            with S.scope():
                m1 = S.sbuf("m1", [128, NQ128 * 32]); m2 = S.sbuf("m2", [128, NQ128 * 32]); vl = S.sbuf("vl", [128, NQ128 * 32])
                ovl = S.sbuf("ovl", [128, 32])
                score = S.sbuf("score", [128, NQ128 * 32]); sel = S.sbuf("sel", [128, NQ128 * 32])
                top8 = S.sbuf("top8", [128, NQ128, 8])
                selpad = S.sbuf("selpad", [128, NQ128, 128])
                S.op("pool", lambda e: e.memset(selpad[:, :, :], 0.0), writes=[selpad])
                S.dma("sp", m1[:], I["c_m1"].rearrange("p a b -> p (a b)"), writes=[m1])
                S.dma("sp", m2[:], I["c_m2"].rearrange("p a b -> p (a b)"), writes=[m2])
                S.dma("sp", vl[:], I["c_valid"].rearrange("p a b -> p (a b)"), writes=[vl])
                S.dma("sp", ovl[:], I["c_ovl"], writes=[ovl])
                ps = S.ps()
                for sub in range(NQ128):
                    mm(ps, ps[:, sub * 32:(sub + 1) * 32], pnsum, pnsum[0:NCB, sub * 128:(sub + 1) * 128], ovl, ovl[0:NCB, :])
                W_ = NQ128 * 32
                tt_op("dve", score, score[:, :], ps, ps[:, 0:W_], m1, m1[:, :], ALU.mult)
                tt_op("dve", score, score[:, :], score, score[:, :], m2, m2[:, :], ALU.add)
                for sub in range(NQ128):
                    S.op("dve", lambda e, sub=sub: e.max(out=top8[:, sub, :], in_=score[:, sub * 32:(sub + 1) * 32]), reads=[score], writes=[top8])
                tt_op("dve", sel, sel[:, :].rearrange("p (a b) -> p a b", b=32), score, score[:, :].rearrange("p (a b) -> p a b", b=32),
                      top8, top8[:, :, 7:8].to_broadcast([128, NQ128, 32]), ALU.is_ge)
                tt_op("dve", sel, sel[:, :], sel, sel[:, :], vl, vl[:, :], ALU.mult)
                ts_op("dve", sel, sel[:, :], sel, sel[:, :], -NEGB, NEGB, ALU.mult, ALU.add)
                cp("dve", selpad, selpad[:, :, 64:96], sel, sel[:, :].rearrange("p (a b) -> p a b", b=32))
                for qt in range(NQT):
                    ps = S.ps()
                    for s4 in range(4):
                        sub = qt * 4 + s4
                        tr(ps, ps[:, s4 * 128:(s4 + 1) * 128], selpad, selpad[:, sub, :], 128)
                    cp("act", selq, selq[64:96, qt * 512:(qt + 1) * 512], ps, ps[64:96, :])
            with S.scope():
                qS = [S.sbuf(f"qS{a}", [128, T], BF16) for a in range(4)]
                KsE = S.sbuf("KsE", [128, T], BF16); KwP = S.sbuf("KwP", [128, T], BF16)
                VA = [S.sbuf(f"VA{a}", [128, NKC, 128], BF16) for a in range(2)]
                cb = S.sbuf("cb", [128, 1024], BF16); wb = S.sbuf("wb", [128, 896], BF16)
                Pts = Rot([S.sbuf(f"P{a}", [128, 512], BF16) for a in range(5)])
                rDs = Rot([S.sbuf(f"rD{a}", [64, 512]) for a in range(2)])
                ogs = Rot([S.sbuf(f"og{a}", [64, 512]) for a in range(2)])
                tmps = Rot([S.sbuf(f"tmp{a}", [128, 512]) for a in range(2)])
                for hh in range(4):
                    h_ = 4 * g + hh
                    S.op("pool", lambda e, hh=hh: e.memset(qS[hh][64:128, :], 0.0), writes=[qS[hh]])
                    S.dma("pool", qS[hh][0:64, :], qT_scr[h_ * 64:(h_ + 1) * 64, :], reads=[qT_scr], writes=[qS[hh]])
                    cp("pool", qS[hh], qS[hh][64:96, :], selq, selq[64:96, :])
                S.op("pool", lambda e: e.memset(KsE[64:128, :], 0.0), writes=[KsE])
                S.op("pool", lambda e: e.memset(KwP[64:128, :], 0.0), writes=[KwP])
                S.dma("pool", KsE[0:64, :], kvp[1][g * 64:(g + 1) * 64, :], reads=[kvp[1]], writes=[KsE])
                S.dma("pool", KsE[64:96, :], I["c_eall"].rearrange("j c p -> j (c p)"), writes=[KsE])
                S.dma("pool", KwP[0:64, :], kvp[2][g * 64:(g + 1) * 64, :], reads=[kvp[2]], writes=[KwP])
                for a in range(2):
                    S.op("pool", lambda e, a=a: e.memset(VA[a][:, :, 64:128], 1.0), writes=[VA[a]])
                    S.dma("sp", VA[a][:, :, 0:64], vtok_scr[a][:, g * 64:(g + 1) * 64].rearrange("(c p) d -> p c d", p=128),
                          reads=[vtok_scr[a]], writes=[VA[a]])
                S.dma("pool", cb[:], I["c_cb"], writes=[cb])
                S.dma("pool", wb[:], I["c_wb"], writes=[wb])

                def finalize(psO, br, h, hb_, c2, qs):
                    rD = rDs.next()
                    ts_op("dve", rD, rD[:, :], psO, psO[64:128, :], 1e-30, None, ALU.max)
                    recip(rD, rD[:, :], rD, rD[:, :])
                    psg = S.ps()
                    gi = h * 3 + br
                    mm(psg, psg[0:64, :], ident, ident[0:48, gi:gi + 1].to_broadcast([48, 64]), sigG, sigG[0:48, qs])
                    og = ogs.next()
                    cp("act", og, og[:, :], psg, psg[0:64, :])
                    tt_op("pool", og, og[:, :], og, og[:, :], rD, rD[:, :], ALU.mult)
                    tmp = tmps.next()
                    tt_op("dve", tmp, tmp[hb_:hb_ + 64, :], psO, psO[0:64, :], og, og[:, :], ALU.mult)
                    tt_op("pool", acc, acc[hb_:hb_ + 64, c2, qs], acc, acc[hb_:hb_ + 64, c2, qs], tmp, tmp[hb_:hb_ + 64, :], ALU.add)

                LA = 2
                steps = []
                for qt in range(NQT):
                    for hh in range(4):
                        nk = 4 * qt + 4
                        for kc in range(nk):
                            steps.append((1, qt, hh, kc, kc == 0, kc == nk - 1))
                        kcs = list(range(max(0, 4 * qt - 4), 4 * qt + 4))
                        for ki, kc in enumerate(kcs):
                            steps.append((2, qt, hh, kc, ki == 0, ki == len(kcs) - 1))

                def emit_score(st):
                    br, qt, hh, kc, first, last = st
                    qs = slice(qt * 512, (qt + 1) * 512)
                    qap = qS[hh][:, qs]
                    ks = slice(kc * 128, (kc + 1) * 128)
                    ps = S.ps()
                    if br == 1:
                        diag = kc >= 4 * qt
                        mm(ps, ps[:, :], KsE, KsE[:, ks], qS[hh], qap, start=True, stop=not diag)
                        if diag:
                            d = 128 * kc - 512 * qt
                            mm(ps, ps[:, :], identb, identb[:, :], cb, cb[:, 512 - d:1024 - d], start=False, stop=True)
                    else:
                        mm(ps, ps[:, :], KwP, KwP[:, ks], qS[hh], qap, start=True, stop=False)
                        if kc >= 4 * qt:
                            d = 128 * kc - 512 * qt
                            mm(ps, ps[:, :], identb, identb[:, :], cb, cb[:, 512 - d:1024 - d], start=False, stop=True)
                        else:
                            m = kc - 4 * qt + 4
                            mm(ps, ps[:, :], identb, identb[:, :], wb, wb[:, 384 - 128 * m:896 - 128 * m], start=False, stop=True)
                    Pt = Pts.next()
                    act(Pt, Pt[:, :], ps, ps[:, :], AF.Exp, scale=SCALE)
                    return Pt

                cur = {}

                def emit_pv(st, Pt):
                    br, qt, hh, kc, first, last = st
                    if first:
                        cur[br] = S.ps_acc()
                    psO = cur[br]
                    mm(psO, psO[:, :], VA[br - 1], VA[br - 1][:, kc, :], Pt, Pt[:, :], start=first, stop=last)
                    if last:
                        qs = slice(qt * 512, (qt + 1) * 512)
                        finalize(psO, br, 4 * g + hh, 64 * (hh % 2), hh // 2, qs)

                pend = []
                for i in range(len(steps) + LA):
                    if i < len(steps):
                        pend.append((steps[i], emit_score(steps[i])))
                    if i >= LA:
                        st, Pt = pend.pop(0)
                        emit_pv(st, Pt)
            for c2 in range(2):
                S.dma("pool", oT_scr[(2 * g + c2) * 128:(2 * g + c2 + 1) * 128, :], acc[:, c2, :], reads=[acc], writes=[oT_scr])

    def nsa_sample(j, qsT, kvs, sigGs, c0d, w2d, oTs):
        pool_c = I["pool_c"].rearrange("n r c -> (n r) c")
        pool_s = I["pool_s"].rearrange("n r c -> (n r) c")
        roff = float(j * NROWS)
        with S.scope():
            qsg = S.sbuf("qsg", [64, 16, 16]); knew = S.sbuf("knew", [64, 24, 16])
            Gb = S.sbuf("Gb", [64, 48, 16])
            Oc = S.sbuf("Oc", [64, 16, 16])
            Osd = [S.sbuf(f"Osd{a}", [64, 16, 16]) for a in range(4)]
            PN = S.sbuf("PN", [128, 64])
            osa = S.sbuf("osa", [64, 16, 16])
            ptb = S.sbuf("ptb", [128, 256], I32); ptf = S.sbuf("ptf", [128, 256]); idxc = S.sbuf("idxc", [128, 256], I32)
            p128 = S.sbuf("p128", [128, 1]); p64 = S.sbuf("p64", [128, 1])
            idxs = S.sbuf("idxs", [128, 256], I32)
            S.dma("sp", p128[:], I["c_p128"], writes=[p128]); S.dma("sp", p64[:], I["c_p64"], writes=[p64])
            cp("dve", qsg, qsg[:, 0:16:2, :], qsT, qsT[0:64, :, :])
            cp("dve", qsg, qsg[:, 1:16:2, :], qsT, qsT[64:128, :, :])
            cp("dve", knew, knew[:, 0:24:2, :], kvs, kvs[0:64, :, :])
            cp("dve", knew, knew[:, 1:24:2, :], kvs, kvs[64:128, :, :])
            for half, n in ((0, 32), (1, 16)):
                ps = S.ps()
                for a in range(n):
                    gi = half * 32 + a
                    mm(ps, ps[0:64, a * 16:(a + 1) * 16], ident, ident[0:48, gi:gi + 1].to_broadcast([48, 64]), sigGs, sigGs[:, :])
                cp("act", Gb, Gb[:, half * 32:half * 32 + n, :], ps, ps[0:64, 0:n * 16].rearrange("p (a b) -> p a b", b=16))
            S.dma("sp", ptb[:], I["pt"].rearrange("b n -> (b n)").partition_broadcast(128), writes=[ptb])
            cp("dve", ptf, ptf[:], ptb, ptb[:])
            ts_op("dve", ptf, ptf[:], ptf, ptf[:], 128.0, p128[:, 0:1], ALU.mult, ALU.add, extra_reads=[p128])
            if j > 0:
                ts_op("dve", ptf, ptf[:], ptf, ptf[:], roff, None, ALU.add)
            cp("dve", idxc, idxc[:], ptf, ptf[:])
            with S.scope():
                w1 = S.sbuf("w1", [128, 2, 32, 128], BF16)
                rowsTs = Rot([S.sbuf(f"rowsT{a}", [128, 4, 2048], BF16) for a in range(2)])
                gts = Rot([S.sbuf(f"gt{a}", [128, 512]) for a in range(4)])
                hids = Rot([S.sbuf(f"hid{a}", [128, 2, 2, 128]) for a in range(2)])
                kcs4 = Rot([S.sbuf(f"kcs{a}", [64, 4, 128]) for a in range(2)])
                vcs4 = Rot([S.sbuf(f"vcs{a}", [128, 4, 64]) for a in range(2)])
                Pts_ = Rot([S.sbuf(f"Pt{a}", [128, 16]) for a in range(2)])
                rDs_ = Rot([S.sbuf(f"rD{a}", [128, 16]) for a in range(2)])
                pns_ = Rot([S.sbuf(f"pn{a}", [128, 16]) for a in range(2)])
                ovs = S.sbuf("ovs", [128, 32])
                S.dma("pool", w1[:], I["cmp_w1bd"][j], writes=[w1])
                S.dma("sp", ovs[:], I["c_ovs"], writes=[ovs])

                def cmp_tail(b, hid):
                    kc4 = kcs4.next(); vc4 = vcs4.next()
                    psk = S.ps()
                    psv = S.ps()
                    for g in range(4):
                        pr, hb_ = g // 2, 64 * (g % 2)
                        mm(psk, psk[0:64, g * 128:g * 128 + 127], w2d, w2d[hb_:hb_ + 64, 0, :], hid, hid[hb_:hb_ + 64, pr, 0, 0:127])
                        mm(psv, psv[0:127, g * 64:(g + 1) * 64], hid, hid[hb_:hb_ + 64, pr, 1, 0:127], w2d, w2d[hb_:hb_ + 64, 1, :])
                    cp("dve", kc4, kc4[:, :, 0:127], psk, psk[0:64, :].rearrange("p (a b) -> p a b", a=4)[:, :, 0:127])
                    cp("act", vc4, vc4[0:127, :, :], psv, psv[0:127, 0:256].rearrange("p (a b) -> p a b", a=4))
                    ps = S.ps()
                    for g in range(4):
                        mm(ps, ps[0:127, 4 * g:4 * g + 4], kc4, kc4[:, g, 0:127], qsg, qsg[:, 4 * g:4 * g + 4, b])
                    Pt = Pts_.next(); rD = rDs_.next(); pn = pns_.next()
                    act(Pt, Pt[0:127, :], ps, ps[0:127, 0:16], AF.Exp, scale=SCALE)
                    psd = S.ps()
                    mm(psd, psd[:, 0:16], ones, ones[0:127, :], Pt, Pt[0:127, :])
                    recip(rD, rD[:, :], psd, psd[:, 0:16])
                    tt_op("dve", pn, pn[0:127, :], Pt, Pt[0:127, :], rD, rD[0:127, :], ALU.mult)
                    S.op("dve", lambda e, b=b, pn=pn: e.tensor_reduce(out=PN[0:127, 4 * b:4 * b + 4],
                                                                      in_=pn[0:127, :].rearrange("p (a b) -> p a b", a=4), axis=AX.X, op=ALU.add),
                         reads=[pn], writes=[PN])
                    pso = S.ps()
                    for g in range(4):
                        mm(pso, pso[0:64, 4 * g:4 * g + 4], vc4, vc4[0:127, g, :], pn, pn[0:127, 4 * g:4 * g + 4])
                    cp("act", Oc, Oc[:, :, b], pso, pso[0:64, 0:16])

                prev = None
                for b in range(16):
                    rowsT = rowsTs.next()
                    for pg in range(16):
                        gt = gts.next()
                        col = b * 16 + pg
                        S.dma("pool", None, None, reads=[idxc], writes=[gt],
                              fn=lambda e, gt=gt, col=col: e.indirect_dma_start(
                                  out=gt[:, :], out_offset=None, in_=pool_c[:, :],
                                  in_offset=bass.IndirectOffsetOnAxis(ap=idxc[:, col:col + 1], axis=0)))
                        ps = S.ps()
                        for c4 in range(4):
                            tr(ps, ps[:, c4 * 128:(c4 + 1) * 128], gt, gt[:, c4 * 128:(c4 + 1) * 128], 128)
                        cp("act" if pg % 2 == 0 else "dve", rowsT, rowsT[:, :, pg * 128:(pg + 1) * 128],
                           ps, ps[:, :].rearrange("p (a b) -> p a b", a=4))
                    hid = hids.next()
                    for pr in range(2):
                        for kv in range(2):
                            c4 = kv * 2 + pr
                            ps = S.ps()
                            for l in range(32):
                                mm(ps, ps[:, 0:127], w1, w1[:, kv, l, :], rowsT, rowsT[:, c4, l:l + 16 * 126 + 1:16],
                                   start=(l == 0), stop=(l == 31))
                            act(hid, hid[:, pr, kv, 0:127], ps, ps[:, 0:127], AF.Silu, bias=c0d[:, kv:kv + 1], extra_reads=[c0d])
                    if prev is not None:
                        cmp_tail(*prev)
                    prev = (b, hid)
                cmp_tail(*prev)
                m1s = S.sbuf("m1s", [64, 32]); m2s = S.sbuf("m2s", [64, 32]); iota32 = S.sbuf("iota32", [64, 32])
                score = S.sbuf("score", [64, 32]); top8 = S.sbuf("top8", [64, 8]); idxu = S.sbuf("idxu", [64, 8], U32)
                idxf = S.sbuf("idxf", [64, 8]); oh = S.sbuf("oh", [64, 32]); junk = S.sbuf("junk", [64, 32])
                ptg = S.sbuf("ptg", [64, 16], I32); ptgf = S.sbuf("ptgf", [64, 16]); PB = S.sbuf("PB", [64, 16, 2])
                phys = S.sbuf("phys", [64, 8]); physT = S.sbuf("physT", [8, 64]); e2 = S.sbuf("e2", [8, 4, 128])
                S.dma("sp", m1s[:], I["c_m1s"], writes=[m1s]); S.dma("sp", m2s[:], I["c_m2s"], writes=[m2s])
                S.dma("sp", iota32[:], I["c_iota32"], writes=[iota32]); S.dma("sp", e2[:], I["c_e2"], writes=[e2])
                for b in range(16):
                    S.dma("sp", ptg[4 * b:4 * b + 4, :], I["pt"][b, :].partition_broadcast(4), writes=[ptg])
                cp("dve", ptgf, ptgf[:], ptg, ptg[:])
                ts_op("dve", PB, PB[:, :, 0], ptgf, ptgf[:, :], 2.0, None, ALU.mult)
                ts_op("dve", PB, PB[:, :, 1], ptgf, ptgf[:, :], 2.0, 1.0, ALU.mult, ALU.add)
                ps = S.ps()
                mm(ps, ps[0:64, 0:32], PN, PN[0:127, 0:64], ovs, ovs[0:127, :])
                tt_op("dve", score, score[:], ps, ps[0:64, 0:32], m1s, m1s[:], ALU.mult)
                tt_op("dve", score, score[:], score, score[:], m2s, m2s[:], ALU.add)
                S.op("dve", lambda e: e.max(out=top8[:], in_=score[:]), reads=[score], writes=[top8])
                S.op("dve", lambda e: e.max_index(out=idxu[:], in_max=top8[:], in_values=score[:]), reads=[top8, score], writes=[idxu])
                cp("dve", idxf, idxf[:], idxu, idxu[:])
                PBf = PB[:, :, :].rearrange("p a b -> p (a b)")
                for k in range(8):
                    ts_op("dve", oh, oh[:], iota32, iota32[:], idxf[:, k:k + 1], None, ALU.is_equal, extra_reads=[idxf])
                    stt(junk, junk[:], oh, oh[:], 1.0, PB, PBf, ALU.mult, ALU.mult, accum=(phys, phys[:, k:k + 1]))
                ps = S.ps()
                tr(ps, ps[0:8, 0:64], phys, phys[0:64, 0:8], 64)
                cp("dve", physT, physT[:], ps, ps[0:8, 0:64])
                ps = S.ps()
                for m in range(4):
                    mm(ps, ps[:, m * 64:(m + 1) * 64], e2, e2[:, m, :], physT, physT[:, :])
                idsf = S.sbuf("idsf", [128, 256])
                ts_op("dve", idsf, idsf[:], ps, ps[:, 0:256], 64.0, p64[:, 0:1], ALU.mult, ALU.add, extra_reads=[p64])
                if j > 0:
                    ts_op("dve", idsf, idsf[:], idsf, idsf[:], roff, None, ALU.add)
                cp("dve", idxs, idxs[:], idsf, idsf[:])
            with S.scope():
                gss = Rot([S.sbuf(f"gs{a}", [128, 512]) for a in range(6)])
                wrs = [S.sbuf(f"wr{a}", [128, 512]) for a in range(4)]
                kTs = Rot([S.sbuf(f"kT{a}", [64, 128]) for a in range(3)])
                Pts = Rot([S.sbuf(f"P{a}", [128, 4]) for a in range(3)])
                negh = S.sbuf("negh", [128, 4])
                S.dma("sp", negh[:], I["c_negh"], writes=[negh])

                def branch(srcs, b, g, Ob, Db, last_bias):
                    psO = S.ps_acc()
                    psD = S.ps_acc()
                    for m in range(4):
                        src = srcs[m]
                        pst = S.ps()
                        tr(pst, pst[0:64, 0:128], src, src[:, g * 64:(g + 1) * 64], 128)
                        kT = kTs.next()
                        cp("act", kT, kT[:, :], pst, pst[0:64, 0:128])
                        pss = S.ps()
                        lb = last_bias and m == 3
                        mm(pss, pss[:, 0:4], kT, kT[:, :], qsg, qsg[:, 4 * g:4 * g + 4, b], start=True, stop=not lb)
                        if lb:
                            mm(pss, pss[:, 0:4], ident, ident[:, :], negh, negh[:, :], start=False, stop=True)
                        Pt = Pts.next()
                        act(Pt, Pt[:, :], pss, pss[:, 0:4], AF.Exp, scale=SCALE)
                        mm(psO, psO[0:64, 0:4], src, src[:, 256 + g * 64:256 + (g + 1) * 64], Pt, Pt[:, :], start=(m == 0), stop=(m == 3))
                        mm(psD, psD[0:64, 0:4], ones, ones[:, 0:64], Pt, Pt[:, :], start=(m == 0), stop=(m == 3))
                    cp("act", Ob, Ob[:, 4 * g:4 * g + 4, b], psO, psO[0:64, 0:4])
                    cp("dve", Db, Db[:, 4 * g:4 * g + 4, b], psD, psD[0:64, 0:4])

                for b in range(16):
                    for m in range(4):
                        S.dma("sp", wrs[m][:, :], I["cwin"][j, b, m * 128:(m + 1) * 128, :], writes=[wrs[m]])
                    for g in range(4):
                        srcs = []
                        for m in range(4):
                            gs = gss.next()
                            col = m * 64 + b * 4 + g
                            S.dma("pool", None, None, reads=[idxs], writes=[gs],
                                  fn=lambda e, gs=gs, col=col: e.indirect_dma_start(
                                      out=gs[:, :], out_offset=None, in_=pool_s[:, :],
                                      in_offset=bass.IndirectOffsetOnAxis(ap=idxs[:, col:col + 1], axis=0)))
                            srcs.append(gs)
                        branch(srcs, b, g, Osd[0], Osd[1], True)
                        branch(wrs, b, g, Osd[2], Osd[3], False)
            with S.scope():
                prod = S.sbuf("prod", [64, 16, 16]); pnew = S.sbuf("pnew", [64, 16, 16]); t1 = S.sbuf("t1", [64, 16, 16])
                rDn = S.sbuf("rDn", [64, 16, 16])
                tt_op("dve", osa, osa[:], Oc, Oc[:], Gb, Gb[:, 0:48:3, :], ALU.mult)
                for bi, cache in ((0, 1), (1, 2)):
                    Ob, Db = Osd[2 * bi], Osd[2 * bi + 1]
                    for g in range(4):
                        ki = cache * 8 + g
                        tt_op("dve", prod, prod[:, 4 * g:4 * g + 4, :], qsg, qsg[:, 4 * g:4 * g + 4, :],
                              knew, knew[:, ki:ki + 1, :].to_broadcast([64, 4, 16]), ALU.mult)
                    ps = S.ps()
                    mm(ps, ps[0:64, 0:256], ones, ones[0:64, 0:64], prod, prod[:, :, :].rearrange("p a b -> p (a b)"))
                    act(pnew, pnew[:, :, :].rearrange("p a b -> p (a b)"), ps, ps[0:64, 0:256], AF.Exp, scale=SCALE)
                    tt_op("dve", Db, Db[:], Db, Db[:], pnew, pnew[:], ALU.add)
                    for g in range(4):
                        vi = cache * 8 + 4 + g
                        tt_op("dve", t1, t1[:, 4 * g:4 * g + 4, :], pnew, pnew[:, 4 * g:4 * g + 4, :],
                              knew, knew[:, vi:vi + 1, :].to_broadcast([64, 4, 16]), ALU.mult)
                    tt_op("dve", Ob, Ob[:], Ob, Ob[:], t1, t1[:], ALU.add)
                    recip(rDn, rDn[:], Db, Db[:])
                    tt_op("dve", t1, t1[:], Ob, Ob[:], rDn, rDn[:], ALU.mult)
                    tt_op("dve", t1, t1[:], t1, t1[:], Gb, Gb[:, 1 + bi:48:3, :], ALU.mult)
                    tt_op("dve", osa, osa[:], osa, osa[:], t1, t1[:], ALU.add)
                cp("dve", oTs, oTs[0:64, :, :], osa, osa[:, 0:16:2, :])
                cp("dve", oTs, oTs[64:128, :, :], osa, osa[:, 1:16:2, :])

    def ssd_layer(i, j):
        W_in = I["ssd_w_in"][j]
        with S.scope():
            dt_tok = S.sbuf("dt_tok", [128, NQ128, 32]); dta = S.sbuf("dta", [128, NQ128, 32])
            dtb = S.sbuf("dtb", [128, 32]); aneg = S.sbuf("aneg", [128, 32])
            dtbT = S.sbuf("dtbT", [32, 1]); anegT = S.sbuf("anegT", [32, 1])
            zs = S.sbuf("zs", [128, 16, 16]); xbcs = S.sbuf("xbcs", [128, 24, 16]); dtsT = S.sbuf("dtsT", [32, 16])
            dsk = S.sbuf("dsk", [128, 16]); snw = S.sbuf("snw", [128, 16])
            one_t = S.sbuf("one_t", [128, 1])
            S.op("dve", lambda e: e.memset(one_t[:], 1.0), writes=[one_t])
            S.dma("sp", dtb[:], I["ssd_dtb"][j, 0, :].partition_broadcast(128), writes=[dtb])
            S.dma("sp", aneg[:], I["ssd_alog"][j, 0, :].partition_broadcast(128), writes=[aneg])
            S.dma("sp", dtbT[:], I["ssd_dtbT"][j], writes=[dtbT])
            S.dma("sp", anegT[:], I["ssd_alogT"][j], writes=[anegT])
            S.dma("sp", dsk[:], I["ssd_dT"][j], writes=[dsk])
            S.dma("sp", snw[:], I["ssd_norm_wT"][j], writes=[snw])
            act(aneg, aneg[:], aneg, aneg[:], AF.Exp)
            ts_op("dve", aneg, aneg[:], aneg, aneg[:], -1.0, None, ALU.mult)
            act(anegT, anegT[:], anegT, anegT[:], AF.Exp)
            ts_op("dve", anegT, anegT[:], anegT, anegT[:], -1.0, None, ALU.mult)

            def softplus(buf, ap, tmp_b, tmp_ap):
                act(tmp_b, tmp_ap, buf, ap, AF.Abs)
                act(tmp_b, tmp_ap, tmp_b, tmp_ap, AF.Exp, scale=-1.0)
                act(tmp_b, tmp_ap, tmp_b, tmp_ap, AF.Ln, bias=one_t[0:tmp_ap.shape[0], 0:1], extra_reads=[one_t])
                stt(buf, ap, buf, ap, 0.0, tmp_b, tmp_ap, ALU.max, ALU.add)

            with S.scope():
                wts = Rot([S.sbuf(f"wt{a}", [128, 8, 512], BF16) for a in range(4)])
                hb = Rot([S.sbuf(f"h{a}", [128, 8, 512], BF16) for a in range(2)])
                sq_rot = Rot([S.sbuf(f"sq{a}", [128, 512]) for a in range(2)])
                r_b = S.sbuf("r", [128, 512])
                evs = Rot([S.sbuf(f"ev{a}", [128, 512]) for a in range(6)])
                for tt in range(NQT + 1):
                    N = ntok(tt)
                    prompt = tt < NQT
                    h = hb.next()
                    modulate(tt, A1, 0, h, sq_rot, r_b)
                    for blk in range(11):
                        ncol = min(512, 5152 - blk * 512)
                        wt = wts.next()
                        load_w(wt, W_in, 0, 1024, blk * 512, ncol)
                        if blk == 10:
                            if prompt:
                                for sub in range(4):
                                    ps = S.ps()
                                    for k in range(8):
                                        mm(ps, ps[:, 0:32], h, h[:, k, sub * 128:(sub + 1) * 128], wt, wt[:, k, 0:32], start=(k == 0), stop=(k == 7))
                                    tt_op("dve", dt_tok, dt_tok[:, tt * 4 + sub, :], ps, ps[:, 0:32], dtb, dtb[:, :], ALU.add)
                            else:
                                ps = S.ps()
                                for k in range(8):
                                    mm(ps, ps[0:32, 0:16], wt, wt[:, k, 0:32], h, h[:, k, 0:16], start=(k == 0), stop=(k == 7))
                                ts_op("dve", dtsT, dtsT[:, :], ps, ps[0:32, 0:16], dtbT[:, 0:1], None, ALU.add, extra_reads=[dtbT])
                            continue
                        for jc in range(4):
                            ps = S.ps()
                            for k in range(8):
                                mm(ps, ps[:, 0:N], wt, wt[:, k, jc * 128:(jc + 1) * 128], h, h[:, k, 0:N], start=(k == 0), stop=(k == 7))
                            gc = blk * 4 + jc
                            if prompt:
                                ev = evs.next()
                                cp("act", ev, ev[:, 0:N], ps, ps[:, 0:N])
                                if gc < 16:
                                    S.dma("sp", z_scr[gc * 128:(gc + 1) * 128, tsl(tt)], ev[:, 0:N], reads=[ev], writes=[z_scr])
                                else:
                                    r0 = (gc - 16) * 128
                                    S.dma("sp", xbc_scr[r0:r0 + 128, tsl(tt)], ev[:, 0:N], reads=[ev], writes=[xbc_scr])
                                    if tt == NQT - 1:
                                        cvo = O[f"convp{j}"]
                                        S.dma("sp", cvo[r0:r0 + 128, :], ev[:, N - 3:N], reads=[ev], writes=[cvo])
                            else:
                                if gc < 16:
                                    cp("act", zs, zs[:, gc, :], ps, ps[:, 0:N])
                                else:
                                    cp("act", xbcs, xbcs[:, gc - 16, :], ps, ps[:, 0:N])
                tmpd = S.sbuf("tmpd", [128, NQ128, 32])
                softplus(dt_tok, dt_tok[:], tmpd, tmpd[:])
                tt_op("dve", dta, dta[:], dt_tok, dt_tok[:], aneg, aneg[:, :].unsqueeze(1).to_broadcast([128, NQ128, 32]), ALU.mult)
            import os
            STG = int(os.environ.get("SSD_STAGE", "9"))
            cw = S.sbuf("cw", [128, 24, 4]); cbias = S.sbuf("cbias", [128, 24])
            S.dma("sp", cw[:], I["ssd_conv_wT"][j], writes=[cw])
            S.dma("sp", cbias[:], I["ssd_conv_bT"][j], writes=[cbias])
            S.mute = STG < 2
            with S.scope():
                xins = [S.sbuf(f"xin{a}", [128, T + 3]) for a in range(2)]
                accs = [S.sbuf(f"cacc{a}", [128, T]) for a in range(2)]
                for a in range(2):
                    S.op("pool", lambda e, a=a: e.memset(xins[a][:, 0:3], 0.0), writes=[xins[a]])
                for c in range(24):
                    xin = xins[c % 2]
                    ac = accs[c % 2]
                    S.dma("sp", xin[:, 3:T + 3], xbc_scr[c * 128:(c + 1) * 128, :], reads=[xbc_scr], writes=[xin])
                    ts_op("dve", ac, ac[:, :], xin, xin[:, 3:T + 3], cw[:, c, 3:4], cbias[:, c:c + 1], ALU.mult, ALU.add, extra_reads=[cw, cbias])
                    for k in range(3):
                        stt(ac, ac[:, :], xin, xin[:, k:T + k], cw[:, c, k:k + 1], ac, ac[:, :], ALU.mult, ALU.add, extra_reads=[cw])
                    act(ac, ac[:, :], ac, ac[:, :], AF.Silu)
                    S.dma("pool", xbcA_scr[c * 128:(c + 1) * 128, :], ac[:, :], reads=[ac], writes=[xbcA_scr])
            S.mute = STG < 3
            with S.scope():
                ST = S.sbuf("ST", [128, 16, 128])
                xAs = Rot([S.sbuf(f"xA{a}", [128, 16, 256]) for a in range(1)])
                BCs = Rot([S.sbuf(f"BC{a}", [128, 8, 256]) for a in range(1)])
                xtok = S.sbuf("xtok", [128, 2, 2048]); Btok = S.sbuf("Btok", [128, 2, 512])
                xdt = S.sbuf("xdt", [128, 2, 2048])
                nac = S.sbuf("nac", [128, 2, 32])
                cbm = S.sbuf("cbm", [128, 4, 2, 256])
                trifull = S.sbuf("trifull", [128, 2, 256]); tri = S.sbuf("tri", [128, 128]); causal = S.sbuf("causal", [128, 2, 256])
                Erows = Rot([S.sbuf(f"Erow{a}", [128, 256]) for a in range(2)])
                decs = Rot([S.sbuf(f"dec{a}", [128, 256]) for a in range(3)])
                MTs = Rot([S.sbuf(f"MT{a}", [128, 256]) for a in range(3)])
                xdtes = Rot([S.sbuf(f"xdte{a}", [128, 2, 128]) for a in range(2)])
                yts = Rot([S.sbuf(f"yt{a}", [128, 256]) for a in range(2)])
                t1s = Rot([S.sbuf(f"t1{a}", [128, 256]) for a in range(2)])
                cds = Rot([S.sbuf(f"cd{a}", [128, 2]) for a in range(2)])
                S.dma("sp", trifull[:], I["c_trifull"], writes=[trifull])
                S.dma("sp", tri[:], I["c_tri"], writes=[tri])
                S.dma("sp", causal[:], I["c_causal01"], writes=[causal])
                SP = int(os.environ.get("SCAN_PART", "9"))
                for c in range(cfg.NCH):
                    csl = slice(c * 256, (c + 1) * 256)
                    xA = xAs.next()
                    BC = BCs.next()
                    S.dma("sp", xA[:], xbcA_scr[0:2048, csl].rearrange("(c p) t -> p c t", p=128), reads=[xbcA_scr], writes=[xA])
                    S.dma("sp", BC[:], xbcA_scr[2048:3072, csl].rearrange("(c p) t -> p c t", p=128), reads=[xbcA_scr], writes=[BC])
                    for st in range(2):
                        for q4 in range(4):
                            ps = S.ps()
                            for a in range(4):
                                ch = q4 * 4 + a
                                tr(ps, ps[:, a * 128:(a + 1) * 128], xA, xA[:, ch, st * 128:(st + 1) * 128], 128)
                            cp("act" if q4 % 2 == 0 else "dve", xtok, xtok[:, st, q4 * 512:(q4 + 1) * 512], ps, ps[:, :])
                        ps = S.ps()
                        for a in range(4):
                            tr(ps, ps[:, a * 128:(a + 1) * 128], BC, BC[:, a, st * 128:(st + 1) * 128], 128)
                        cp("act", Btok, Btok[:, st, :], ps, ps[:, :])
                    if SP < 2:
                        continue
                    ps = S.ps()
                    mm(ps, ps[:, 0:32], tri, tri[:, :], dta, dta[:, 2 * c, :], start=True, stop=True)
                    mm(ps, ps[:, 32:64], ones, ones[:, :], dta, dta[:, 2 * c, :], start=True, stop=False)
                    mm(ps, ps[:, 32:64], tri, tri[:, :], dta, dta[:, 2 * c + 1, :], start=False, stop=True)
                    ts_op("dve", nac, nac[:, :, :], ps, ps[:, 0:64].rearrange("p (a b) -> p a b", a=2), -1.0, None, ALU.mult)
                    for st in range(2):
                        tt_op("dve", xdt, xdt[:, st, :].rearrange("p (h d) -> p h d", d=64), xtok, xtok[:, st, :].rearrange("p (h d) -> p h d", d=64),
                              dt_tok, dt_tok[:, 2 * c + st, :].unsqueeze(2).to_broadcast([128, 32, 64]), ALU.mult)
                    for g in range(4):
                        for st in range(2):
                            ps = S.ps()
                            mm(ps, ps[:, 0:256], BC, BC[:, g, st * 128:(st + 1) * 128], BC, BC[:, 4 + g, :])
                            tt_op("dve", cbm, cbm[:, g, st, :], ps, ps[:, 0:256], causal, causal[:, st, :], ALU.mult)
                    SUB = int(os.environ.get("SCAN_SUB", "9"))
                    HPL = int(os.environ.get("SCAN_HP", "16"))
                    for hp in range(HPL if SP >= 3 else 0):
                        g = hp // 4
                        psR = []
                        for x in range(2):
                            hx = 2 * hp + x
                            ps = S.ps()
                            for jt in range(2):
                                mm(ps, ps[:, 0:256], dta, dta[:, 2 * c + jt, hx:hx + 1].to_broadcast([128, 128]), trifull, trifull[:, jt, :],
                                   start=(jt == 0), stop=(jt == 1))
                            psR.append(ps)
                        Erow = Erows.next()
                        act(Erow, Erow[0:64, :], psR[0], psR[0][0:64, 0:256], AF.Exp)
                        act(Erow, Erow[64:128, :], psR[1], psR[1][64:128, 0:256], AF.Exp)
                        cd = cds.next()
                        for x in range(2):
                            act(cd, cd[:, x:x + 1], psR[x], psR[x][:, 255:256], AF.Exp)
                        if SUB < 2:
                            continue
                        xdte = xdtes.next()
                        psY = [S.ps_acc(), S.ps_acc()]
                        for x in range(2):
                            hx = 2 * hp + x
                            for st in range(2):
                                dec = decs.next()
                                ts_op("dve", dec, dec[:, :], psR[x], psR[x][:, 0:256], nac[:, st, hx:hx + 1], 0.0, ALU.add, ALU.min, extra_reads=[nac])
                                act(dec, dec[:, :], dec, dec[:, :], AF.Exp)
                                MT = MTs.next()
                                tt_op("pool", MT, MT[:, :], dec, dec[:, :], cbm, cbm[:, g, st, :], ALU.mult)
                                mm(psY[x], psY[x][0:64, 0:256], xdt, xdt[:, st, hx * 64:(hx + 1) * 64], MT, MT[:, :], start=(st == 0), stop=(st == 1))
                                ts_op("dve", xdte, xdte[:, st, x * 64:(x + 1) * 64], xdt, xdt[:, st, hx * 64:(hx + 1) * 64], dec[:, 255:256], None, ALU.mult,
                                      extra_reads=[dec])
                        if SUB < 3:
                            continue
                        psS = S.ps()
                        for st in range(2):
                            mm(psS, psS[:, 0:128], Btok, Btok[:, st, g * 128:(g + 1) * 128], xdte, xdte[:, st, :], start=(st == 0), stop=(st == 1))
                        if SUB < 4:
                            continue
                        yt = yts.next()
                        cp("act", yt, yt[0:64, :], psY[0], psY[0][0:64, 0:256])
                        cp("act", yt, yt[64:128, :], psY[1], psY[1][0:64, 0:256])
                        if c > 0 and SUB >= 5:
                            psF = S.ps()
                            mm(psF, psF[:, 0:256], ST, ST[:, hp, :], BC, BC[:, 4 + g, :])
                            t1 = t1s.next()
                            tt_op("dve", t1, t1[:, :], psF, psF[:, 0:256], Erow, Erow[:, :], ALU.mult)
                            tt_op("pool", yt, yt[:, :], yt, yt[:, :], t1, t1[:, :], ALU.add)
                        stt(yt, yt[:, :], xA, xA[:, hp, :], dsk[:, hp:hp + 1], yt, yt[:, :], ALU.mult, ALU.add, extra_reads=[dsk])
                        S.dma("pool", y_scr[hp * 128:(hp + 1) * 128, csl], yt[:, :], reads=[yt], writes=[y_scr])
                        if SUB < 6:
                            continue
                        if c == 0:
                            cp("dve", ST, ST[:, hp, :], psS, psS[:, 0:128])
                        else:
                            for x in range(2):
                                stt(ST, ST[:, hp, x * 64:(x + 1) * 64], ST, ST[:, hp, x * 64:(x + 1) * 64], cd[:, x:x + 1],
                                    psS, psS[:, x * 64:(x + 1) * 64], ALU.mult, ALU.add, extra_reads=[cd])
                if SP >= 4:
                    S.dma("pool", O[f"ssmp{j}"][:, :, :].rearrange("h n p -> n h p"), ST[:, :, :], reads=[ST], writes=[O[f"ssmp{j}"]])
            S.mute = STG < 4
            with S.scope():
                cvb = S.sbuf("cvb", [128, 24, 16, 3]); cvo = S.sbuf("cvo", [128, 24, 16, 3])
                xa = S.sbuf("xa", [128, 24, 16]); tmpx = S.sbuf("tmpx", [128, 24, 16])
                tmps_ = S.sbuf("tmps", [32, 16]); dtas = S.sbuf("dtas", [32, 16])
                ehp = S.sbuf("ehp", [32, 16, 128]); selb = S.sbuf("selb", [16, 16, 128])
                dtx = S.sbuf("dtx", [128, 16, 16]); cdx = S.sbuf("cdx", [128, 16, 16]); xdts = S.sbuf("xdts", [128, 16, 16])
                BCtok = S.sbuf("BCtok", [16, 1024])
                Bbs = Rot([S.sbuf(f"Bb{a}", [128, 4, 128]) for a in range(2)])
                Cbs = Rot([S.sbuf(f"Cb{a}", [128, 4, 128]) for a in range(2)])
                sts = Rot([S.sbuf(f"st{a}", [128, 16, 128]) for a in range(2)])
                t1b = Rot([S.sbuf(f"t1b{a}", [128, 16, 128]) for a in range(2)])
                ys = S.sbuf("ys", [128, 16, 16])
                S.dma("sp", cvb[:], I["sconvT"][j].rearrange("(c p) b k -> p c b k", p=128), writes=[cvb])
                S.dma("sp", ehp[:], I["c_ehp"], writes=[ehp])
                S.dma("sp", selb[:], I["c_selb"], writes=[selb])
                tt_op("dve", xa, xa[:], xbcs, xbcs[:], cw, cw[:, :, 3:4].to_broadcast([128, 24, 16]), ALU.mult)
                tt_op("dve", xa, xa[:], xa, xa[:], cbias, cbias[:, :].unsqueeze(2).to_broadcast([128, 24, 16]), ALU.add)
                for k in range(3):
                    tt_op("dve", tmpx, tmpx[:], cvb, cvb[:, :, :, k], cw, cw[:, :, k:k + 1].to_broadcast([128, 24, 16]), ALU.mult)
                    tt_op("dve", xa, xa[:], xa, xa[:], tmpx, tmpx[:], ALU.add)
                act(xa, xa[:], xa, xa[:], AF.Silu)
                cp("pool", cvo, cvo[:, :, :, 0], cvb, cvb[:, :, :, 1])
                cp("pool", cvo, cvo[:, :, :, 1], cvb, cvb[:, :, :, 2])
                cp("pool", cvo, cvo[:, :, :, 2], xbcs, xbcs[:])
                S.dma("pool", O[f"convs{j}"][:, :, :].rearrange("(c p) b k -> p c b k", p=128), cvo[:], reads=[cvo], writes=[O[f"convs{j}"]])
                softplus(dtsT, dtsT[:, :], tmps_, tmps_[:, :])
                ts_op("dve", dtas, dtas[:, :], dtsT, dtsT[:, :], anegT[:, 0:1], None, ALU.mult, extra_reads=[anegT])
                ps = S.ps()
                for hp in range(16):
                    mm(ps, ps[:, hp * 16:(hp + 1) * 16], ehp, ehp[:, hp, :], dtsT, dtsT[:, :])
                cp("dve", dtx, dtx[:], ps, ps[:, 0:256].rearrange("p (a b) -> p a b", b=16))
                ps = S.ps()
                for hp in range(16):
                    mm(ps, ps[:, hp * 16:(hp + 1) * 16], ehp, ehp[:, hp, :], dtas, dtas[:, :])
                act(cdx, cdx[:], ps, ps[:, 0:256].rearrange("p (a b) -> p a b", b=16), AF.Exp)
                tt_op("dve", xdts, xdts[:], xa, xa[:, 0:16, :], dtx, dtx[:], ALU.mult)
                for half in range(2):
                    ps = S.ps()
                    for a in range(4):
                        tr(ps, ps[0:16, a * 128:(a + 1) * 128], xa, xa[:, 16 + half * 4 + a, :], 128)
                    cp("dve", BCtok, BCtok[:, half * 512:(half + 1) * 512], ps, ps[0:16, :])
                sso = O[f"ssms{j}"]
                for b in range(16):
                    ps0 = S.ps()
                    mm(ps0, ps0[:, :], selb, selb[:, b, :], BCtok, BCtok[:, 0:512])
                    ps1 = S.ps()
                    mm(ps1, ps1[:, :], selb, selb[:, b, :], BCtok, BCtok[:, 512:1024])
                    Bb = Bbs.next(); Cb = Cbs.next()
                    cp("act", Bb, Bb[:, :, :].rearrange("p a b -> p (a b)"), ps0, ps0[:, :])
                    cp("act", Cb, Cb[:, :, :].rearrange("p a b -> p (a b)"), ps1, ps1[:, :])
                    st_ = sts.next()
                    S.dma("sp", st_[:], I["sssm"][j, b].rearrange("(c p) n -> p c n", p=128), writes=[st_])
                    t1 = t1b.next()
                    for g in range(4):
                        tt_op("pool", t1, t1[:, 4 * g:4 * g + 4, :], Bb, Bb[:, g:g + 1, :].to_broadcast([128, 4, 128]),
                              xdts, xdts[:, 4 * g:4 * g + 4, b:b + 1].to_broadcast([128, 4, 128]), ALU.mult)
                    tt_op("dve", st_, st_[:], st_, st_[:], cdx, cdx[:, :, b:b + 1].to_broadcast([128, 16, 128]), ALU.mult)
                    tt_op("dve", st_, st_[:], st_, st_[:], t1, t1[:], ALU.add)
                    S.dma("pool", sso[b].rearrange("(c p) n -> p c n", p=128), st_[:], reads=[st_], writes=[sso])
                    for g in range(4):
                        tt_op("pool", t1, t1[:, 4 * g:4 * g + 4, :], st_, st_[:, 4 * g:4 * g + 4, :],
                              Cb, Cb[:, g:g + 1, :].to_broadcast([128, 4, 128]), ALU.mult)
                    S.op("dve", lambda e, t1=t1, b=b: e.tensor_reduce(out=ys[:, :, b], in_=t1[:, :, :], axis=AX.X, op=ALU.add), reads=[t1], writes=[ys])
                tt_op("dve", tmpx, tmpx[:, 0:16, :], xa, xa[:, 0:16, :], dsk, dsk[:, :].unsqueeze(2).to_broadcast([128, 16, 16]), ALU.mult)
                tt_op("dve", ys, ys[:], ys, ys[:], tmpx, tmpx[:, 0:16, :], ALU.add)
                S.dma("pool", y_scr[:, T:TT].rearrange("(c p) b -> p c b", p=128), ys[:], reads=[ys], writes=[y_scr])
                S.dma("pool", z_scr[:, T:TT].rearrange("(c p) b -> p c b", p=128), zs[:], reads=[zs], writes=[z_scr])
            S.mute = STG < 5
            with S.scope():
                wts = Rot([S.sbuf(f"wt{a}", [128, 8, 512], BF16) for a in range(4)])
                yb = S.sbuf("yb", [128, 16, 512]); zb = S.sbuf("zb", [128, 16, 512])
                ybh = S.sbuf("ybh", [128, 16, 512], BF16)
                sq_rot = Rot([S.sbuf(f"sq{a}", [128, 512]) for a in range(2)])
                r_b = S.sbuf("r", [128, 512])
                for tt in range(NQT + 1):
                    N = ntok(tt)
                    S.dma("sp", yb[:, :, 0:N], y_scr[:, tsl(tt)].rearrange("(c p) t -> p c t", p=128), reads=[y_scr], writes=[yb])
                    S.dma("sp", zb[:, :, 0:N], z_scr[:, tsl(tt)].rearrange("(c p) t -> p c t", p=128), reads=[z_scr], writes=[zb])
                    act(zb, zb[:, :, 0:N], zb, zb[:, :, 0:N], AF.Silu)
                    tt_op("dve", yb, yb[:, :, 0:N], yb, yb[:, :, 0:N], zb, zb[:, :, 0:N], ALU.mult)
                    rms_rinv(yb, lambda k, N=N: yb[:, k, 0:N], 16, N, sq_rot, r_b, 2048.0)
                    for k in range(16):
                        stt(ybh, ybh[:, k, 0:N], yb, yb[:, k, 0:N], snw[:, k:k + 1], r_b, r_b[:, 0:N], ALU.mult, ALU.mult, extra_reads=[snw])
                    out_proj(I["ssd_w_out"][j], 16, lambda k, N=N: (ybh, ybh[:, k, 0:N]), tt, wts, 16)
            S.mute = False

    import os
    DBG = os.environ.get("KDBG", "")
    for i in range(NL):
        adaln(i)
        if i % 2 == 0:
            if "nonsa" not in DBG:
                nsa_layer(i, i // 2)
        else:
            ssd_layer(i, i // 2)
        if "nomlp" not in DBG:
            mlp(i)
    with S.scope():
        sq_rot = Rot([S.sbuf(f"sq{a}", [128, 512]) for a in range(2)])
        r_b = S.sbuf("r", [128, 512])
        yo = Rot([S.sbuf(f"yo{a}", [128, 8, 512]) for a in range(2)])
        for tt in range(NQT + 1):
            N = ntok(tt)
            x = xt[tt]
            rms_rinv(x, lambda k, x=x: x[:, k, :], 8, N, sq_rot, r_b, 1024.0)
            y = yo.next()
            for k in range(8):
                stt(y, y[:, k, 0:N], x, x[:, k, :], fw[:, k:k + 1], r_b, r_b[:, 0:N], ALU.mult, ALU.mult, extra_reads=[fw])
            S.dma("pool", O["yT"][:, tsl(tt)].rearrange("(c p) t -> p c t", p=128), y[:, :, 0:N], reads=[y], writes=[O["yT"]])
    S.finish("sp")
    S.emit()
    root.close()
    return nc, consts, S.n_ops


def fm(v, nch):
    return np.ascontiguousarray(np.asarray(v, np.float32).reshape(nch, 128).T)


def prep_shared(inp, cfg, consts):
    NL, NN, NS = cfg.NL, cfg.NN, cfg.NS
    f32 = np.float32
    sh = {}
    sh["ada_w"] = np.ascontiguousarray(inp["ada_w"], f32)
    sh["ada_bT"] = np.stack([fm(inp["ada_b"][l], 48) for l in range(NL)])
    sh["norm_wT"] = np.stack([np.concatenate([fm(inp["norm_w"][l, 0], 8), fm(inp["norm_w"][l, 1], 8)], axis=1) for l in range(NL)])
    sh["mlp_w1"] = np.ascontiguousarray(inp["mlp_w1"], f32)
    sh["mlp_w2"] = np.ascontiguousarray(inp["mlp_w2"], f32)
    sh["final_wT"] = fm(inp["final_norm_w"], 8)
    if NN:
        npool = inp["cache_kv_cmp"].shape[1]
        sh["pool_c"] = np.ascontiguousarray(inp["cache_kv_cmp"], f32).reshape(NN, npool * 128, 512)
        sh["pool_s"] = np.ascontiguousarray(inp["cache_kv_sel"], f32).reshape(NN, npool * 128, 512)
        sh["nsa_w_in"] = np.ascontiguousarray(inp["nsa_w_in"], f32)
        sh["nsa_w_out"] = np.ascontiguousarray(inp["nsa_w_out"], f32)
        pe = np.asarray(inp["nsa_cmp_pe"], f32)
        sh["cmp_pe2"] = np.ascontiguousarray(pe.reshape(NN, 2, 16, 2, 64).transpose(0, 3, 4, 1, 2).reshape(NN, 128, 2, 16))
        w1 = np.asarray(inp["nsa_cmp_w1"], f32)
        sh["cmp_w1std"] = np.ascontiguousarray(w1.reshape(NN, 2, 16, 128, 64).transpose(0, 3, 1, 2, 4))
        w1d = w1.reshape(NN, 2, 32, 64, 64).transpose(0, 3, 1, 2, 4)
        sh["cmp_w1d"] = np.ascontiguousarray(np.concatenate([w1d, w1d], axis=1))
        bd = np.zeros((NN, 128, 2, 32, 128), f32)
        bd[:, 0:64, :, :, 0:64] = w1d
        bd[:, 64:128, :, :, 64:128] = w1d
        sh["cmp_w1bd"] = bd
        sh["cmp_w2"] = np.ascontiguousarray(np.asarray(inp["nsa_cmp_w2"], f32).transpose(0, 2, 1, 3))
    if NS:
        sh["ssd_w_in"] = np.ascontiguousarray(inp["ssd_w_in"], f32)
        cw = np.asarray(inp["ssd_conv_w"], f32)
        sh["ssd_conv_wT"] = np.ascontiguousarray(cw.reshape(NS, 4, 24, 128).transpose(0, 3, 2, 1))
        sh["ssd_conv_bT"] = np.stack([fm(inp["ssd_conv_b"][l], 24) for l in range(NS)])
        sh["ssd_dtb"] = np.ascontiguousarray(np.asarray(inp["ssd_dt_bias"], f32).reshape(NS, 1, 32))
        sh["ssd_alog"] = np.ascontiguousarray(np.asarray(inp["ssd_a_log"], f32).reshape(NS, 1, 32))
        sh["ssd_dtbT"] = np.ascontiguousarray(np.asarray(inp["ssd_dt_bias"], f32).reshape(NS, 32, 1))
        sh["ssd_alogT"] = np.ascontiguousarray(np.asarray(inp["ssd_a_log"], f32).reshape(NS, 32, 1))
        sh["ssd_dT"] = np.stack([fm(np.repeat(np.asarray(inp["ssd_d"][l], f32), 64), 16) for l in range(NS)])
        sh["ssd_norm_wT"] = np.stack([fm(inp["ssd_norm_w"][l], 16) for l in range(NS)])
        sh["ssd_w_out"] = np.ascontiguousarray(inp["ssd_w_out"], f32)
    sh.update(consts)
    return sh


def prep_core(inp, cfg, c):
    f32 = np.float32
    NN, NS = cfg.NN, cfg.NS
    bs = slice(16 * c, 16 * c + 16)
    m = {}
    m["xT"] = np.ascontiguousarray(np.asarray(inp["x_prompt"][c], f32).T)
    m["xsT"] = np.ascontiguousarray(np.asarray(inp["x_sample"][bs, 0], f32).T)
    cc = np.concatenate([np.asarray(inp["c_prompt"][c:c + 1], f32), np.asarray(inp["c_sample"][bs], f32)], axis=0)
    m["cT"] = np.ascontiguousarray(cc.T)
    m["pt"] = np.ascontiguousarray(inp["page_table"][bs], np.int32)
    if NN:
        m["cwin"] = np.ascontiguousarray(np.asarray(inp["cache_kv_win"], f32)[:, bs]).reshape(NN, 16, 512, 512)
    if NS:
        m["sssm"] = np.ascontiguousarray(np.asarray(inp["state_ssm"], f32)[:, bs]).reshape(NS, 16, 2048, 128)
        sc = np.asarray(inp["state_conv"], f32)[:, bs]
        m["sconvT"] = np.ascontiguousarray(sc.transpose(0, 3, 1, 2))
    return m


def assemble(res, cfg, ncores):
    T, NN, NS = cfg.T, cfg.NN, cfg.NS
    f32 = np.float32
    B = ncores
    y_p = np.stack([res[c]["yT"][:, :T].T for c in range(B)]).astype(f32)
    y_s = np.concatenate([res[c]["yT"][:, T:].T for c in range(B)])[:, None, :].astype(f32)

    def kv_p(cn):
        return np.stack([np.stack([res[c][f"kvp_{cn}{j}"].T.reshape(T, 2, 4, 64) for c in range(B)]) for j in range(NN)])

    def kv_s(cn):
        return np.stack([np.concatenate([res[c][f"kvs_{cn}{j}"].T.reshape(16, 1, 2, 4, 64) for c in range(B)]) for j in range(NN)])

    wk = min(512, T)
    win_p = np.stack([np.stack([res[c][f"kvwin_p{j}"].T.reshape(wk, 2, 4, 64) for c in range(B)]) for j in range(NN)])
    win_s = np.stack([np.concatenate([np.concatenate([res[c][f"kvwin_old{j}"].reshape(16, 511, 2, 4, 64),
                                                      res[c][f"kvs_w{j}"].T.reshape(16, 1, 2, 4, 64)], axis=1)
                                      for c in range(B)]) for j in range(NN)])
    outs = [y_p, y_s, kv_p("c"), kv_s("c"), kv_p("s"), kv_s("s"), win_p, win_s]
    if NS:
        ssm_p = np.stack([np.stack([res[c][f"ssmp{j}"].reshape(16, 128, 2, 64).transpose(0, 2, 3, 1).reshape(32, 64, 128)
                                    for c in range(B)]) for j in range(NS)])
        ssm_s = np.stack([np.concatenate([res[c][f"ssms{j}"].reshape(16, 32, 64, 128) for c in range(B)]) for j in range(NS)])
        conv_p = np.stack([np.stack([res[c][f"convp{j}"].T for c in range(B)]) for j in range(NS)])
        conv_s = np.stack([np.concatenate([res[c][f"convs{j}"].transpose(1, 2, 0) for c in range(B)]) for j in range(NS)])
        outs += [ssm_p, ssm_s, conv_p, conv_s]
    return tuple(np.ascontiguousarray(o, dtype=f32) for o in outs)


def run(inp, cfg, ncores):
    nc, consts, _ = build(cfg)
    sh = prep_shared(inp, cfg, consts)
    in_maps = []
    for c in range(ncores):
        m = dict(sh)
        m.update(prep_core(inp, cfg, c))
        in_maps.append(m)
    res = run_bass_kernel_spmd(nc, in_maps, core_ids=list(range(ncores)))
    return assemble(res.results, cfg, ncores)


def kernel(**inputs):
    cfg = Cfg(T=2048, NL=4, NPOOL=int(inputs["cache_kv_cmp"].shape[1]))
    return run(inputs, cfg, 8)
```
